# Optimizing a Trainium2 kernel written in Bass

```python
import math
import jax
import jax.numpy as jnp
from jax import lax
import numpy as np

D_MODEL = 1024
BATCH = 4
SEQ = 8192
DEPTH = 1

GRID_W = 64
CTX_LEN = 256
N_MOD = 6
MLSTM_HEADS = 4
MLSTM_DQK = 128
MLSTM_DV = 256
MLSTM_CHUNK = 64
CONV_K = 5
DIFF_HEADS = 8
DIFF_HEAD_DIM = 64
Q_BLOCK = 128
ROPE_BASE = 10000.0
ROPE_AXIS_DIM = DIFF_HEAD_DIM // 2
ROPE_FREQS = ROPE_AXIS_DIM // 2
FFN_HIDDEN = -(-8 * D_MODEL // (3 * 256)) * 256

M_QK_W = MLSTM_HEADS * MLSTM_DQK
M_V_W = MLSTM_HEADS * MLSTM_DV
M_GATES = 4 * MLSTM_HEADS
DA_W = DIFF_HEADS * 2 * DIFF_HEAD_DIM
IN_SIZES = (M_QK_W, M_QK_W, M_V_W, M_V_W, M_GATES, DA_W, DA_W, DA_W, D_MODEL, D_MODEL)
IN_WIDTH = sum(IN_SIZES)
IN_SPLIT_POINTS = tuple(int(s) for s in np.cumsum(IN_SIZES)[:-1])

kernel_name = 'hybrid_mlstm_diffattn_dit_block'


def rms_norm(x, g, eps=1e-6):
    xf = x.astype(jnp.float32)
    y = xf * lax.rsqrt(jnp.mean(xf * xf, axis=-1, keepdims=True) + eps)
    return (y * g.astype(jnp.float32)).astype(x.dtype)


def modulated_norm(x, g, shift, scale):
    return rms_norm(x, g) * (1 + scale) + shift


def centred_dwconv(x, w, b):
    y = lax.conv_general_dilated(x, w[:, None, :], (1,), [(CONV_K // 2, CONV_K // 2)],
                                 dimension_numbers=('NWC', 'WIO', 'NWC'),
                                 feature_group_count=x.shape[-1])
    return y + b


def to_heads(a, n_heads):
    B, T, _ = a.shape
    return a.reshape(B, T, n_heads, -1).transpose(0, 2, 1, 3)


def flip_t(a):
    return jnp.flip(a, axis=2)


def axial_rope_tables(n_tokens):
    rows = n_tokens // GRID_W
    row = jnp.repeat(jnp.arange(rows, dtype=jnp.float32), GRID_W)
    col = jnp.tile(jnp.arange(GRID_W, dtype=jnp.float32), rows)
    inv = ROPE_BASE ** (-jnp.arange(ROPE_FREQS, dtype=jnp.float32) / ROPE_FREQS)
    ang = jnp.concatenate([row[:, None] * inv, col[:, None] * inv], axis=-1)
    return jnp.cos(ang), jnp.sin(ang)


def rotate_pairs(xh, cos, sin):
    x1, x2 = xh[..., :ROPE_FREQS], xh[..., ROPE_FREQS:]
    return jnp.concatenate([x1 * cos - x2 * sin, x2 * cos + x1 * sin], axis=-1)


def apply_axial_rope(x, cos, sin):
    xf = x.astype(jnp.float32)
    y = jnp.concatenate([
        rotate_pairs(xf[..., :ROPE_AXIS_DIM], cos[:, :ROPE_FREQS], sin[:, :ROPE_FREQS]),
        rotate_pairs(xf[..., ROPE_AXIS_DIM:], cos[:, ROPE_FREQS:], sin[:, ROPE_FREQS:])], axis=-1)
    return y.astype(x.dtype)


def mlstm_chunkwise(q, k, v, ig, lf, C0, n0, m0):
    B, H, T, DK = q.shape
    DV = v.shape[-1]
    L = MLSTM_CHUNK
    NC = T // L
    lower = jnp.tril(jnp.ones((L, L), dtype=bool))

    def chunks(a):
        return jnp.moveaxis(a.reshape(a.shape[:2] + (NC, L) + a.shape[3:]), 2, 0)

    def step(carry, inp):
        C, n, m = carry
        qc, kc, vc, igc, lfc = inp
        b = jnp.cumsum(lfc, axis=-1)
        logD = jnp.where(lower, b[..., :, None] - b[..., None, :] + igc[..., None, :], -jnp.inf)
        m_inter = b + m[..., None]
        m_j = jnp.maximum(m_inter, jnp.max(logD, axis=-1))
        dmat = jnp.exp(logD - m_j[..., None])
        inter = jnp.exp(m_inter - m_j)
        s = jnp.einsum('bhjd,bhsd->bhjs', qc, kc) * dmat
        num = jnp.einsum('bhjs,bhse->bhje', s, vc) + inter[..., None] * jnp.einsum('bhjd,bhde->bhje', qc, C)
        den = jnp.sum(s, axis=-1) + inter * jnp.einsum('bhjd,bhd->bhj', qc, n)
        h = num / jnp.maximum(jnp.abs(den), jnp.exp(-m_j))[..., None]
        bL = b[..., -1]
        logw = bL[..., None] - b + igc
        m_new = jnp.maximum(bL + m, jnp.max(logw, axis=-1))
        w = jnp.exp(logw - m_new[..., None])
        dec = jnp.exp(bL + m - m_new)
        C_new = dec[..., None, None] * C + jnp.einsum('bhs,bhsd,bhse->bhde', w, kc, vc)
        n_new = dec[..., None] * n + jnp.einsum('bhs,bhsd->bhd', w, kc)
        return (C_new, n_new, m_new), h

    state, h = lax.scan(step, (C0, n0, m0), (chunks(q), chunks(k), chunks(v), chunks(ig), chunks(lf)))
    h = jnp.moveaxis(h, 0, 2).reshape(B, H, T, DV)
    return h, state


def project_stream(xn, w_in, b_gate, conv_w, conv_b, q_norm, k_norm):
    f32 = jnp.float32
    p = xn @ w_in
    mq, mk, mv, mo, mg, dq, dk, dv, ga, gb = jnp.split(p, IN_SPLIT_POINTS, axis=-1)
    qk = jax.nn.silu(centred_dwconv(jnp.concatenate([mq, mk], axis=-1), conv_w, conv_b))
    mq = to_heads(qk[..., :M_QK_W], MLSTM_HEADS).astype(f32) * MLSTM_DQK ** -0.5
    mk = to_heads(qk[..., M_QK_W:], MLSTM_HEADS).astype(f32)
    mv = to_heads(mv, MLSTM_HEADS).astype(f32)
    g = (mg + b_gate).astype(f32)
    B, T, _ = g.shape
    g = g.reshape(B, T, 4, MLSTM_HEADS).transpose(2, 0, 3, 1)
    gates = (g[0], jax.nn.log_sigmoid(g[1]), g[2], jax.nn.log_sigmoid(g[3]))
    dq = to_heads(dq, DIFF_HEADS)
    dk = to_heads(dk, DIFF_HEADS)
    q1 = rms_norm(dq[..., :DIFF_HEAD_DIM], q_norm)
    q2 = rms_norm(dq[..., DIFF_HEAD_DIM:], q_norm)
    k1 = rms_norm(dk[..., :DIFF_HEAD_DIM], k_norm)
    k2 = rms_norm(dk[..., DIFF_HEAD_DIM:], k_norm)
    dv = to_heads(dv, DIFF_HEADS)
    return mq, mk, mv, mo, gates, q1, q2, k1, k2, dv, ga, gb


def mlstm_output(h, o, g):
    B, H, T, DV = h.shape
    hn = rms_norm(h.transpose(0, 2, 1, 3), g.reshape(H, DV)).reshape(B, T, H * DV)
    return jax.nn.sigmoid(o.astype(jnp.float32)) * hn


def diff_weights(q1, q2, k1, k2, lam):
    s1 = jnp.einsum('bhqd,bhkd->bhqk', q1, k1, preferred_element_type=jnp.float32)
    s2 = jnp.einsum('bhqd,bhkd->bhqk', q2, k2, preferred_element_type=jnp.float32)
    return jax.nn.softmax(s1, axis=-1) - lam * jax.nn.softmax(s2, axis=-1)


def diff_attention_blocks(q1, q2, k1, k2, v, lam):
    B, H, T, d = q1.shape
    nb = T // Q_BLOCK
    vf = v.astype(jnp.float32)

    def blocks(a):
        return jnp.moveaxis(a.reshape(B, H, nb, Q_BLOCK, d), 2, 0)

    def one_block(qs):
        w = diff_weights(qs[0], qs[1], k1, k2, lam)
        return jnp.einsum('bhqk,bhke->bhqe', w, vf)

    out = lax.map(one_block, (blocks(q1), blocks(q2)))
    return jnp.moveaxis(out, 0, 2).reshape(B, H, T, v.shape[-1])


def diff_output(o, g, lam_init):
    B, H, T, E = o.shape
    return (rms_norm(o, g) * (1 - lam_init)).transpose(0, 2, 1, 3).reshape(B, T, H * E)


def merge_branches(hA, hB, ga, gb, w_a, w_b, w_o):
    y = jax.nn.sigmoid(ga) * (hA @ w_a) + jax.nn.sigmoid(gb) * (hB @ w_b)
    return y @ w_o


def swiglu_ffn(xn, w_in, w_out):
    a, b = jnp.split(xn @ w_in, 2, axis=-1)
    return (jax.nn.silu(a) * b) @ w_out


def setup_inputs(seed: int = 0) -> dict:
    key = jax.random.key(seed)
    ks = jax.random.split(key, 32)

    def nrm(k, shape, s):
        return jax.random.normal(k, shape, jnp.float32) * s

    fg_base = jnp.linspace(3.0, 6.0, MLSTM_HEADS, dtype=jnp.float32)[None, :]
    b_gate = jnp.concatenate([
        nrm(ks[9], (DEPTH, MLSTM_HEADS), 0.1),
        fg_base + nrm(ks[10], (DEPTH, MLSTM_HEADS), 0.1),
        nrm(ks[11], (DEPTH, MLSTM_HEADS), 0.1),
        fg_base + nrm(ks[12], (DEPTH, MLSTM_HEADS), 0.1)], axis=-1)
    return {
        'x': nrm(ks[0], (BATCH, SEQ, D_MODEL), 1.0),
        'c': nrm(ks[1], (BATCH, D_MODEL), 1.0),
        'ctx': nrm(ks[2], (BATCH, CTX_LEN, D_MODEL), 1.0),
        'c_ctx': nrm(ks[3], (D_MODEL,), 1.0),
        'w_mod': nrm(ks[4], (DEPTH, D_MODEL, N_MOD * D_MODEL), D_MODEL ** -0.5),
        'b_mod': nrm(ks[5], (DEPTH, N_MOD * D_MODEL), 0.02),
        'norm1': 1.0 + nrm(ks[6], (DEPTH, D_MODEL), 0.02),
        'norm2': 1.0 + nrm(ks[7], (DEPTH, D_MODEL), 0.02),
        'w_in': nrm(ks[8], (DEPTH, D_MODEL, IN_WIDTH), D_MODEL ** -0.5),
        'b_gate': b_gate,
        'conv_w': nrm(ks[13], (DEPTH, CONV_K, 2 * M_QK_W), CONV_K ** -0.5),
        'conv_b': nrm(ks[14], (DEPTH, 2 * M_QK_W), 0.02),
        'mlstm_norm': 1.0 + nrm(ks[15], (DEPTH, M_V_W), 0.02),
        'q_norm': 1.0 + nrm(ks[16], (DEPTH, DIFF_HEAD_DIM), 0.02),
        'k_norm': 1.0 + nrm(ks[17], (DEPTH, DIFF_HEAD_DIM), 0.02),
        'lam_vecs': nrm(ks[18], (DEPTH, 4, DIFF_HEAD_DIM), 0.1),
        'diff_norm': 1.0 + nrm(ks[19], (DEPTH, 2 * DIFF_HEAD_DIM), 0.02),
        'w_branch_a': nrm(ks[20], (DEPTH, M_V_W, D_MODEL), M_V_W ** -0.5),
        'w_branch_b': nrm(ks[21], (DEPTH, DA_W, D_MODEL), DA_W ** -0.5),
        'w_out': nrm(ks[22], (DEPTH, D_MODEL, D_MODEL), D_MODEL ** -0.5),
        'w_ffn_in': nrm(ks[23], (DEPTH, D_MODEL, 2 * FFN_HIDDEN), D_MODEL ** -0.5),
        'w_ffn_out': nrm(ks[24], (DEPTH, FFN_HIDDEN, D_MODEL), FFN_HIDDEN ** -0.5),
    }


def reference(x, c, ctx, c_ctx, w_mod, b_mod, norm1, norm2, w_in, b_gate, conv_w, conv_b, mlstm_norm,
              q_norm, k_norm, lam_vecs, diff_norm, w_branch_a, w_branch_b, w_out, w_ffn_in, w_ffn_out):
    f32 = jnp.float32
    B, S, _ = x.shape
    cos, sin = axial_rope_tables(S)
    scale_q = DIFF_HEAD_DIM ** -0.5
    for l in range(DEPTH):
        last = l == DEPTH - 1
        mod = jax.nn.silu(c) @ w_mod[l] + b_mod[l]
        sh1, sc1, g1, sh2, sc2, g2 = [m[:, None, :] for m in jnp.split(mod, N_MOD, axis=-1)]
        mod_c = jax.nn.silu(c_ctx) @ w_mod[l] + b_mod[l]
        csh1, csc1, cg1, csh2, csc2, cg2 = jnp.split(mod_c, N_MOD)

        xn = modulated_norm(x, norm1[l], sh1, sc1)
        cn = modulated_norm(ctx, norm1[l], csh1, csc1)
        (mq, mk, mv, mo, (ig_f, lf_f, ig_b, lf_b), q1, q2, k1, k2, dv, ga, gb) = project_stream(
            xn, w_in[l], b_gate[l], conv_w[l], conv_b[l], q_norm[l], k_norm[l])
        (cmq, cmk, cmv, cmo, (cig_f, clf_f, cig_b, clf_b), cq1, cq2, ck1, ck2, cdv, cga, cgb) = project_stream(
            cn, w_in[l], b_gate[l], conv_w[l], conv_b[l], q_norm[l], k_norm[l])

        zero = (jnp.zeros((B, MLSTM_HEADS, MLSTM_DQK, MLSTM_DV), f32),
                jnp.zeros((B, MLSTM_HEADS, MLSTM_DQK), f32),
                jnp.zeros((B, MLSTM_HEADS), f32))
        hc_f, st_f = mlstm_chunkwise(cmq, cmk, cmv, cig_f, clf_f, *zero)
        hc_b, st_b = mlstm_chunkwise(flip_t(cmq), flip_t(cmk), flip_t(cmv), flip_t(cig_b), flip_t(clf_b), *zero)
        h_f, _ = mlstm_chunkwise(mq, mk, mv, ig_f, lf_f, *st_f)
        h_b, _ = mlstm_chunkwise(flip_t(mq), flip_t(mk), flip_t(mv), flip_t(ig_b), flip_t(lf_b), *st_b)
        hA = mlstm_output(h_f + flip_t(h_b), mo, mlstm_norm[l])

        lam_init = 0.8 - 0.6 * math.exp(-0.3 * l)
        lv = lam_vecs[l].astype(f32)
        lam = jnp.exp(jnp.sum(lv[0] * lv[1])) - jnp.exp(jnp.sum(lv[2] * lv[3])) + lam_init
        q1r = apply_axial_rope(q1, cos, sin) * scale_q
        q2r = apply_axial_rope(q2, cos, sin) * scale_q
        k1_all = jnp.concatenate([apply_axial_rope(k1, cos, sin), ck1], axis=2)
        k2_all = jnp.concatenate([apply_axial_rope(k2, cos, sin), ck2], axis=2)
        v_all = jnp.concatenate([dv, cdv], axis=2)
        o = diff_attention_blocks(q1r, q2r, k1_all, k2_all, v_all, lam)
        hB = diff_output(o, diff_norm[l], lam_init)

        y = merge_branches(hA, hB, ga, gb, w_branch_a[l], w_branch_b[l], w_out[l])
        x = x + (g1 * y).astype(x.dtype)
        x = x + (g2 * swiglu_ffn(modulated_norm(x, norm2[l], sh2, sc2), w_ffn_in[l], w_ffn_out[l])).astype(x.dtype)

        if not last:
            hcA = mlstm_output(hc_f + flip_t(hc_b), cmo, mlstm_norm[l])
            wc = diff_weights(cq1 * scale_q, cq2 * scale_q, ck1, ck2, lam)
            hcB = diff_output(jnp.einsum('bhqk,bhke->bhqe', wc, cdv.astype(f32)), diff_norm[l], lam_init)
            yc = merge_branches(hcA, hcB, cga, cgb, w_branch_a[l], w_branch_b[l], w_out[l])
            ctx = ctx + (cg1 * yc).astype(ctx.dtype)
            ctx = ctx + (cg2 * swiglu_ffn(modulated_norm(ctx, norm2[l], csh2, csc2), w_ffn_in[l], w_ffn_out[l])).astype(ctx.dtype)
    return x
```

```python
import math
import types
import numpy as np
from contextlib import ExitStack
import concourse.bass as bass
import concourse.mybir as mybir
from concourse.bass_utils import run_bass_kernel_spmd

F32 = mybir.dt.float32
BF16 = mybir.dt.bfloat16
AF = mybir.ActivationFunctionType
ALU = mybir.AluOpType
AX = mybir.AxisListType

NDS = 8


def _freeze(fn):
    if fn.__closure__ is None:
        return fn
    cells = []
    for c in fn.__closure__:
        try:
            cells.append(types.CellType(c.cell_contents))
        except ValueError:
            cells.append(c)
    g = types.FunctionType(fn.__code__, fn.__globals__, fn.__name__, fn.__defaults__, tuple(cells))
    g.__kwdefaults__ = fn.__kwdefaults__
    return g


class Buf:
    __slots__ = ("name", "lw", "rd", "rd_dma")

    def __init__(self, name=""):
        self.name = name
        self.lw = None
        self.rd = {}
        self.rd_dma = []


class Op:
    __slots__ = ("eng", "fn", "reads", "writes", "dma", "deps", "signal", "semkey", "semval", "waits", "idx")

    def __init__(self, eng, fn, reads, writes, dma):
        self.eng = eng
        self.fn = fn
        self.reads = reads
        self.writes = writes
        self.dma = dma
        self.deps = None
        self.signal = False
        self.semkey = None
        self.semval = 0
        self.waits = None


class Prog:
    ENGS = ("pe", "act", "dve", "pool", "sp")

    def __init__(self):
        self.ops = []
        self.G = Buf("G")

    def add(self, eng, fn, reads=(), writes=(), dma=False):
        op = Op(eng, _freeze(fn), tuple(reads) + (self.G,), tuple(writes), dma)
        self.ops.append(op)
        return op

    def pe(self, fn, reads, writes):
        return self.add("pe", fn, reads, writes)

    def act(self, fn, reads, writes):
        return self.add("act", fn, reads, writes)

    def dve(self, fn, reads, writes):
        return self.add("dve", fn, reads, writes)

    def pool(self, fn, reads, writes):
        return self.add("pool", fn, reads, writes)

    def dma(self, q, fn, reads, writes):
        return self.add(q, fn, reads, writes, dma=True)

    def barrier(self, fn):
        op = Op("pool", _freeze(fn), (), (self.G,), False)
        self.ops.append(op)

    def schedule(self):
        for i, op in enumerate(self.ops):
            op.idx = i
            deps = set()
            for b in op.reads:
                if b.lw is not None:
                    d = b.lw
                    if not (d.eng == "pe" and op.eng == "pe" and not d.dma):
                        deps.add(d)
            for b in op.writes:
                if b.lw is not None:
                    d = b.lw
                    if d.dma or op.dma or d.eng != op.eng:
                        deps.add(d)
                for e, d in b.rd.items():
                    if e != op.eng or op.dma:
                        deps.add(d)
                for d in b.rd_dma:
                    deps.add(d)
            for b in op.reads:
                if op.dma:
                    b.rd_dma.append(op)
                else:
                    b.rd[op.eng] = op
            for b in op.writes:
                b.lw = op
                b.rd = {}
                b.rd_dma = []
            deps.discard(op)
            op.deps = deps
            for d in deps:
                d.signal = True
        cnt = {e: 0 for e in self.ENGS}
        dcnt = {e: 0 for e in self.ENGS}
        for op in self.ops:
            if op.dma:
                j = dcnt[op.eng]
                dcnt[op.eng] += 1
                op.semkey = ("d", op.eng, j % NDS)
                op.semval = 16 * (j // NDS + 1)
                op.signal = True
            elif op.signal:
                cnt[op.eng] += 1
                op.semkey = ("c", op.eng)
                op.semval = cnt[op.eng]
        waited = {e: {} for e in self.ENGS}
        for op in self.ops:
            w = {}
            if op.dma and op.semval > 16:
                w[op.semkey] = op.semval - 16
            for d in op.deps:
                if w.get(d.semkey, 0) < d.semval:
                    w[d.semkey] = d.semval
            wt = waited[op.eng]
            out = []
            for k, v in w.items():
                if wt.get(k, 0) < v:
                    wt[k] = v
                    out.append((k, v))
            op.waits = out
        self.final_dma = {}
        for op in self.ops:
            if op.dma:
                self.final_dma[op.semkey] = max(self.final_dma.get(op.semkey, 0), op.semval)

    def emit(self, nc, es):
        self.schedule()
        sems = {}
        for e in self.ENGS:
            sems[("c", e)] = es.enter_context(nc.semaphore("c_" + e))
            for j in range(NDS):
                sems[("d", e, j)] = es.enter_context(nc.semaphore("d_%s_%d" % (e, j)))
        block = es.enter_context(nc.Block())
        by_eng = {e: [op for op in self.ops if op.eng == e] for e in self.ENGS}
        final_dma = self.final_dma

        def run(eng_name, e):
            for op in by_eng[eng_name]:
                for k, v in op.waits:
                    e.wait_ge(sems[k], v)
                ins = op.fn(e)
                if op.signal:
                    ins.then_inc(sems[op.semkey], 16 if op.dma else 1)
            for k, v in final_dma.items():
                if k[1] == eng_name:
                    e.wait_ge(sems[k], v)

        @block.tensor
        def _(e):
            run("pe", e)

        @block.scalar
        def _(e):
            run("act", e)

        @block.vector
        def _(e):
            run("dve", e)

        @block.gpsimd
        def _(e):
            run("pool", e)

        @block.sync
        def _(e):
            run("sp", e)


D = 1024
NT = 66
TOK = 8448
OWN = 4096
PTW = 8456
FFH = 2816
EPS = 1e-6
NEG = -30000.0
O_N1, O_N2, O_MN, O_DN, O_QN, O_KN, O_BG, O_LV, SMW = 0, 1024, 2048, 3072, 3200, 3264, 3328, 3344, 3600
CI, CLF, CUF, CLR, CUR, CNF, CNR, CON, CZE, CSTW = 0, 128, 256, 384, 512, 640, 768, 896, 1024, 1152

UNITS = [
    ("mq", 0, 512, "fm_qk", False), ("mk", 512, 512, "fm_qk", False),
    ("mv0", 1024, 512, "mv", False), ("mv1", 1536, 512, "mv", False),
    ("mo0", 2048, 512, "mo", True), ("mo1", 2560, 512, "mo", True),
    ("mg", 3072, 16, "mg", False),
    ("dq0", 3088, 512, "dq", True), ("dq1", 3600, 512, "dq", True),
    ("dk0", 4112, 512, "dk", False), ("dk1", 4624, 512, "dk", False),
    ("dv0", 5136, 512, "dv", False), ("dv1", 5648, 512, "dv", False),
    ("ga0", 6160, 512, "fm_sig", True), ("ga1", 6672, 512, "fm_sig", True),
    ("gb0", 7184, 512, "fm_sig", True), ("gb1", 7696, 512, "fm_sig", True),
]


def pt_col(tok):
    return 2 + tok if tok < 256 else 262 + (tok - 256)


def build(stop_after=None, dbg=False):
    nc = bass.Bass("TRN2", target_bir_lowering=False)
    P = Prog()
    skind = "ExternalOutput" if dbg else "Internal"

    def din(name, shape, dt=F32):
        return nc.dram_tensor(name, list(shape), dt, kind="ExternalInput").ap()

    def dscr(name, shape, dt):
        return nc.dram_tensor(name, list(shape), dt, kind=skind).ap()

    xa = din("xa", [TOK, D])
    cvec = din("cvec", [128, 16])
    w_mod = din("w_mod", [D, 6 * D])
    b_mod = din("b_mod", [1, 6 * D])
    small = din("small", [1, SMW])
    cst = din("cst", [128, CSTW])
    convp = din("convp", [128, 8, 6])
    rope = din("rope", [TOK, 128])
    w_in = din("w_in", [D, 8208])
    w_a = din("w_a", [D, D])
    w_b = din("w_b", [D, D])
    w_o = din("w_o", [D, D])
    w_f1 = din("w_f1", [22, 128, 8, 256])
    w_f2 = din("w_f2", [FFH, D])
    out = nc.dram_tensor("out", [OWN, D], F32, kind="ExternalOutput").ap()

    PT = dscr("PT", [D, PTW], F32)
    VM = dscr("VM", [TOK, D], BF16)
    GT = dscr("GT", [TOK, 16], F32)
    SMO = dscr("SMO", [OWN, D], F32)
    QTD = dscr("QTD", [8, 128, OWN], BF16)
    KTD = dscr("KTD", [8, 128, TOK], BF16)
    VD = dscr("VD", [TOK, D], BF16)
    SGA = dscr("SGA", [D, OWN], F32)
    SGB = dscr("SGB", [D, OWN], F32)
    QTM = dscr("QTM", [4, 128, OWN], BF16)
    KTM = dscr("KTM", [4, 128, TOK], BF16)
    KM = dscr("KM", [TOK, 512], BF16)
    HF = dscr("HF", [OWN, D], F32)
    HAT = dscr("HAT", [8, 128, OWN], BF16)
    HBT = dscr("HBT", [8, 128, OWN], BF16)
    X1 = dscr("X1", [OWN, D], F32)

    with ExitStack() as es:
        cnt = [0]

        def sbt(shape, dt):
            cnt[0] += 1
            return es.enter_context(nc.sbuf_tensor("t%d" % cnt[0], list(shape), dt))

        CST = sbt([128, CSTW], F32)
        bCST = Buf()
        IDB = sbt([128, 128], BF16)
        bIDB = Buf()
        ONEB = sbt([128, 128], BF16)
        B1p = sbt([128, D], F32)
        G1p = sbt([128, D], F32)
        B2p = sbt([128, D], F32)
        G2p = sbt([128, D], F32)
        CB1p = sbt([128, D], F32)
        bMOD = Buf()
        MN = sbt([128, D], F32)
        BG = sbt([128, 16], F32)
        bSM = Buf()
        A1 = sbt([128, D], F32)
        CA1 = sbt([128, D], F32)
        A2 = sbt([128, D], F32)
        QG = sbt([128, 512], F32)
        KG = sbt([128, 512], F32)
        G128 = sbt([128, 128], F32)
        NLAM = sbt([128, 1], F32)
        CONV = sbt([128, 8, 6], F32)
        NCB = sbt([128, 8], F32)
        bK = Buf()
        JUNK = sbt([128, 8], F32)
        ident = CST[:, CI:CI + 128]
        ones = CST[:, CON:CON + 128]
        B1 = B1p[:]
        G1 = G1p[:]
        B2 = B2p[:]
        G2 = G2p[:]
        CB1 = CB1p[:]

        AR_N = 38400
        ARENA = sbt([128, AR_N], F32)
        offs = [0]

        def reset_arena():
            offs[0] = 0

        def af(n):
            a = offs[0]
            offs[0] += n
            assert offs[0] <= AR_N, ("arena overflow", offs[0])
            return ARENA[:, a:a + n]

        def ab(n):
            w = (n + 1) // 2
            a = offs[0]
            offs[0] += w
            assert offs[0] <= AR_N, ("arena overflow", offs[0])
            return ARENA[:, a:a + w].bitcast(BF16)[:, 0:n]

        class T:
            __slots__ = ("ap", "b")

            def __init__(self, ap):
                self.ap = ap
                self.b = Buf()

        def rot(n, alloc, width, view=None):
            res = []
            for _ in range(n):
                a = alloc(width)
                if view is not None:
                    a = view(a)
                res.append(T(a))
            return res

        PS2 = [es.enter_context(nc.psum_tensor("ps2_%d" % i, [128, 1024], F32)) for i in range(2)]
        PSF = [T(PS2[0][:, 0:512]), T(PS2[0][:, 512:1024]), T(PS2[1][:, 0:512]), T(PS2[1][:, 512:1024])]
        PSF += [T(es.enter_context(nc.psum_tensor("ps%d" % i, [128, 512], F32))[:, :]) for i in range(4, 7)]
        PSB = T(es.enter_context(nc.psum_tensor("psb", [128, 1024], BF16)))

        def barrier():
            P.barrier(lambda e: e.memset(JUNK[:], 0.0))

        reset_arena()
        P.dma("sp", lambda e: e.dma_start(out=CST[:], in_=cst[:, :]), [], [bCST])
        P.dma("sp", lambda e: e.dma_start(out=CONV[:], in_=convp[:, :, :]), [], [bK])
        P.dve(lambda e: e.tensor_copy(IDB[:], ident), [bCST], [bIDB])
        P.dve(lambda e: e.tensor_copy(ONEB[:], ones), [bCST], [bIDB])
        cv = T(af(16))
        ce = T(af(16))
        cs = T(af(16))
        P.dma("sp", lambda e: e.dma_start(out=cv.ap, in_=cvec[:, :]), [], [cv.b])
        P.act(lambda e: e.activation(ce.ap, cv.ap, AF.Exp, scale=-1.0), [cv.b], [ce.b])
        P.dve(lambda e: e.tensor_scalar(ce.ap, ce.ap, 1.0, None, ALU.add), [ce.b], [ce.b])
        P.dve(lambda e: e.reciprocal(ce.ap, ce.ap), [ce.b], [ce.b])
        P.dve(lambda e: e.tensor_tensor(cs.ap, cv.ap, ce.ap, ALU.mult), [cv.b, ce.b], [cs.b])
        SC = T(af(16 * 128).rearrange("p (k m) -> p k m", k=16))
        for k in range(16):
            P.dve(lambda e, k=k: e.tensor_scalar(SC.ap[:, k, :], ones, cs.ap[:, k:k + 1], None, ALU.mult),
                  [cs.b, bCST], [SC.b])
        SC1t = T(af(D))
        SC2t = T(af(D))
        CSC1t = T(af(D))
        SMALLt = T(af(SMW))
        SMALL = SMALLt.ap
        wms = rot(2, af, 8 * 512, lambda a: a.rearrange("p (k n) -> p k n", k=8))
        bms = rot(2, af, 512)
        smr = T(af(SMW))
        P.dma("sp", lambda e: e.dma_start(out=smr.ap[0:1, :], in_=small[:, :]), [], [smr.b])
        for j in range(12):
            wm = wms[j % 2]
            bm = bms[j % 2]
            P.dma("sp", lambda e, wm=wm, j=j: e.dma_start(
                out=wm.ap, in_=w_mod[:, j * 512:(j + 1) * 512].rearrange("(k p) n -> p k n", p=128)), [], [wm.b])
            P.dma("sp", lambda e, bm=bm, j=j: e.dma_start(out=bm.ap[0:1, :], in_=b_mod[:, j * 512:(j + 1) * 512]),
                  [], [bm.b])
            for which in range(2 if j < 4 else 1):
                ps = PSF[(j * 2 + which) % 4]
                for k in range(8):
                    P.pe(lambda e, ps=ps, wm=wm, k=k, which=which: e.matmul(
                        ps.ap[:, :], SC.ap[:, which * 8 + k, :], wm.ap[:, k, :], start=(k == 0), stop=False),
                        [SC.b, wm.b], [ps.b])
                P.pe(lambda e, ps=ps, bm=bm: e.matmul(ps.ap[:, :], ones[0:1, :], bm.ap[0:1, :], start=False, stop=True),
                     [bCST, bm.b], [ps.b])
                hs = slice((j % 2) * 512, (j % 2) * 512 + 512)
                if which == 0:
                    dst = (B1p, SC1t.ap, G1p, B2p, SC2t.ap, G2p)[j // 2][:, hs]
                else:
                    dst = (CB1p, CSC1t.ap)[j // 2][:, hs]
                P.act(lambda e, ps=ps, dst=dst: e.activation(dst, ps.ap[:, :], AF.Copy), [ps.b], [bMOD])
        for j in range(8):
            n = min(512, SMW - j * 512)
            ps = PSF[4 + j % 2]
            P.pe(lambda e, ps=ps, j=j, n=n: e.matmul(ps.ap[:, 0:n], ones[0:1, :], smr.ap[0:1, j * 512:j * 512 + n],
                                                    start=True, stop=True), [bCST, smr.b], [ps.b])
            P.act(lambda e, ps=ps, j=j, n=n: e.activation(SMALL[:, j * 512:j * 512 + n], ps.ap[:, 0:n], AF.Copy),
                  [ps.b], [bSM])
        tmp = T(af(D))
        P.dve(lambda e: e.tensor_copy(MN[:], SMALL[:, O_MN:O_MN + D]), [bSM], [bK])
        P.dve(lambda e: e.tensor_copy(BG[:], SMALL[:, O_BG:O_BG + 16]), [bSM], [bK])
        for (dst, sc, gn) in ((A1, SC1t.ap, SMALL[:, O_N1:O_N1 + D]),
                              (CA1, CSC1t.ap, SMALL[:, O_N1:O_N1 + D]),
                              (A2, SC2t.ap, SMALL[:, O_N2:O_N2 + D])):
            P.dve(lambda e, sc=sc: e.tensor_scalar(tmp.ap, sc, 1.0, None, ALU.add), [bMOD], [tmp.b])
            P.dve(lambda e, dst=dst, gn=gn: e.tensor_tensor(dst[:], tmp.ap, gn, ALU.mult), [tmp.b, bSM], [bK])
        for g in range(8):
            P.dve(lambda e, g=g: e.tensor_scalar(QG[:, g * 64:(g + 1) * 64], SMALL[:, O_QN:O_QN + 64], 0.125, None,
                                                 ALU.mult), [bSM], [bK])
            P.dve(lambda e, g=g: e.tensor_copy(KG[:, g * 64:(g + 1) * 64], SMALL[:, O_KN:O_KN + 64]), [bSM], [bK])
        P.dve(lambda e: e.tensor_scalar(G128[:], SMALL[:, O_DN:O_DN + 128], 0.8, None, ALU.mult), [bSM], [bK])
        P.dve(lambda e: e.tensor_scalar(NCB[:], CONV[:, :, 5], -1.0, None, ALU.mult), [bK], [bK])
        lt = T(af(128))
        ls = T(af(2))
        P.dve(lambda e: e.tensor_tensor(lt.ap[:, 0:64], SMALL[:, O_LV:O_LV + 64], SMALL[:, O_LV + 64:O_LV + 128],
                                        ALU.mult), [bSM], [lt.b])
        P.dve(lambda e: e.tensor_tensor(lt.ap[:, 64:128], SMALL[:, O_LV + 128:O_LV + 192],
                                        SMALL[:, O_LV + 192:O_LV + 256], ALU.mult), [lt.b, bSM], [lt.b])
        P.dve(lambda e: e.tensor_reduce(ls.ap, lt.ap.rearrange("p (a f) -> p a f", a=2), AX.X, ALU.add), [lt.b], [ls.b])
        P.act(lambda e: e.activation(ls.ap, ls.ap, AF.Exp), [ls.b], [ls.b])
        P.dve(lambda e: e.tensor_tensor(NLAM[:], ls.ap[:, 1:2], ls.ap[:, 0:1], ALU.subtract), [ls.b], [bK])
        P.dve(lambda e: e.tensor_scalar(NLAM[:], NLAM[:], -0.2, None, ALU.add), [bK], [bK])
        barrier()

        def rstd_chain(ss, width, nfeat):
            P.dve(lambda e: e.tensor_scalar(ss.ap, ss.ap, 1.0 / nfeat, EPS, ALU.mult, ALU.add), [ss.b], [ss.b])
            P.act(lambda e: e.activation(ss.ap, ss.ap, AF.Ln), [ss.b], [ss.b])
            P.act(lambda e: e.activation(ss.ap, ss.ap, AF.Exp, scale=-0.5), [ss.b], [ss.b])

        def mod_norm_T(xt, A, Bv, consts_b, sq, ss, t1, xn, xnT_dst, xnT_b):
            P.act(lambda e: e.activation(sq.ap, xt.ap, AF.Square, accum_out=ss.ap), [xt.b], [sq.b, ss.b])
            yield
            P.dve(lambda e: e.tensor_scalar(ss.ap, ss.ap, 1.0 / D, EPS, ALU.mult, ALU.add), [ss.b], [ss.b])
            yield
            P.act(lambda e: e.activation(ss.ap, ss.ap, AF.Ln), [ss.b], [ss.b])
            P.act(lambda e: e.activation(ss.ap, ss.ap, AF.Exp, scale=-0.5), [ss.b], [ss.b])
            yield
            P.dve(lambda e: e.scalar_tensor_tensor(t1.ap, xt.ap, ss.ap[:, 0:1], A, ALU.mult, ALU.mult),
                  [xt.b, ss.b] + consts_b, [t1.b])
            yield
            P.pool(lambda e: e.tensor_tensor(xn.ap, t1.ap, Bv, ALU.add), [t1.b] + consts_b, [xn.b])
            yield
            for c in range(8):
                P.pe(lambda e, c=c: e.transpose(PSB.ap[:, c * 128:(c + 1) * 128], xn.ap[:, c * 128:(c + 1) * 128], IDB[:]),
                     [xn.b, bIDB], [PSB.b])
            P.act(lambda e: e.activation(xnT_dst, PSB.ap.rearrange("p (c t) -> p c t", c=8), AF.Copy),
                  [PSB.b], [xnT_b])

        def interleave(gens):
            gens = list(gens)
            while gens:
                nxt_ = []
                for g_ in gens:
                    try:
                        next(g_)
                        nxt_.append(g_)
                    except StopIteration:
                        pass
                gens = nxt_

        def sigmoid_act(dst, src, dst_b, src_b, scale=1.0, bias=None, bias_b=()):
            if bias is None:
                P.act(lambda e: e.activation(dst, src, AF.Exp, scale=-scale), [src_b], [dst_b])
            else:
                P.act(lambda e: e.activation(dst, src, AF.Exp, bias=bias, scale=-scale), [src_b] + list(bias_b), [dst_b])
            P.act(lambda e: e.activation(dst, dst, AF.Ln, bias=1.0), [dst_b], [dst_b])
            P.act(lambda e: e.activation(dst, dst, AF.Exp, scale=-1.0), [dst_b], [dst_b])

        def load_w_cast(src_ap, dst_ap, dst_b):
            P.dma("pool", lambda e: e.dma_start(out=dst_ap, in_=src_ap), [], [dst_b])

        if True:
            reset_arena()
            v8 = lambda a: a.rearrange("p (k n) -> p k n", k=8)
            xnT = T(ab(8 * 2048).rearrange("p (k t) -> p k t", k=8))
            wbfs = rot(2, ab, 8 * 512, v8)
            xts = rot(3, af, D)
            sqs = rot(3, af, D)
            t1s = rot(3, af, D)
            sss = rot(3, af, 1)
            xns = rot(3, ab, D)
            w512 = rot(12, af, 512)
            s8 = rot(6, af, 8)
            ropeblk = T(af(16 * 128).rearrange("p (t c) -> p t c", t=16))
            ob512 = rot(6, ab, 512)
            obT = rot(4, ab, 512, lambda a: a.rearrange("p (h t) -> p h t", h=4))
            fst = rot(2, af, 4 * 512, lambda a: a.rearrange("p (c t) -> p c t", c=4))
            g16 = rot(6, af, 16)
            zt = T(af(8))
            ctr = {"w": 0, "x": 0, "w512": 0, "ob": 0, "obT": 0, "fst": 0, "ps": 0, "s8": 0, "rope": 0, "g16": 0}

            def nxt(lst, key):
                i = ctr[key]
                ctr[key] += 1
                return lst[i % len(lst)]

            P.pool(lambda e: e.memset(zt.ap, 0.0), [], [zt.b])
            for (c0, w) in ((0, 2), (258, 4), (8454, 2)):
                for cc in range(8):
                    P.dma("pool", lambda e, c0=c0, w=w, cc=cc: e.dma_start(
                        out=PT[cc * 128:(cc + 1) * 128, c0:c0 + w], in_=zt.ap[:, 0:w]), [zt.b], [])

            blocks = [(0, 2, "ctx"), (2, 16, "own"), (18, 16, "own"), (34, 16, "oth"), (50, 16, "oth")]
            items = []
            for bi, (bt0, bnt, bkind) in enumerate(blocks):
                first = True
                for u in UNITS:
                    if u[4] and bkind != "own":
                        continue
                    items.append((bi, u, first))
                    first = False

            def issue_w(ii):
                (_, (un_, c0_, nc_, _k, _o), _f) = items[ii]
                wb_ = wbfs[ii % 2]
                load_w_cast(w_in[:, c0_:c0_ + nc_].rearrange("(k p) n -> p k n", p=128), wb_.ap[:, :, 0:nc_], wb_.b)

            issue_w(0)
            for ii, (bi, (uname, c0, ncols, kind, own_only), first) in enumerate(items):
                (bt0, bnt, bkind) = blocks[bi]
                own = bkind == "own"
                if first:
                    A, Bv = (CA1[:], CB1) if bkind == "ctx" else (A1[:], B1)
                    P.dma("sp", lambda e: e.dma_start(
                        out=ropeblk.ap[:, 0:bnt, :],
                        in_=rope[bt0 * 128:(bt0 + bnt) * 128, :].rearrange("(t p) c -> p t c", p=128)), [], [ropeblk.b])
                    for t0_ in range(0, bnt, 3):
                        gens = []
                        for ti in range(t0_, min(bnt, t0_ + 3)):
                            tt = bt0 + ti
                            xt = nxt(xts, "x")
                            P.dma("sp", lambda e, xt=xt, tt=tt: e.dma_start(out=xt.ap, in_=xa[tt * 128:(tt + 1) * 128, :]),
                                  [], [xt.b])
                            gens.append(mod_norm_T(xt, A, Bv, [bK, bMOD], sqs[ti % 3], sss[ti % 3], t1s[ti % 3], xns[ti % 3],
                                                   xnT.ap[:, :, ti * 128:(ti + 1) * 128], xnT.b))
                        interleave(gens)
                wbf = wbfs[ii % 2]
                if ii + 1 < len(items):
                    issue_w(ii + 1)
                if True:
                    if kind in ("fm_qk", "fm_sig"):
                        ngroups = max(1, bnt // 4)
                        gn = min(512, bnt * 128)
                        for g in range(ngroups):
                            st = nxt(fst, "fst")
                            for cc in range(4):
                                ps = PSF[ctr["ps"] % 6]
                                ctr["ps"] += 1
                                for k in range(8):
                                    P.pe(lambda e, ps=ps, wbf=wbf, k=k, cc=cc, g=g, gn=gn: e.matmul(
                                        ps.ap[:, 0:gn], wbf.ap[:, k, cc * 128:(cc + 1) * 128],
                                        xnT.ap[:, k, g * 512:g * 512 + gn], start=(k == 0), stop=(k == 7)),
                                        [wbf.b, xnT.b], [ps.b])
                                if kind == "fm_qk":
                                    P.act(lambda e, ps=ps, st=st, cc=cc, gn=gn: e.activation(
                                        st.ap[:, cc, 0:gn], ps.ap[:, 0:gn], AF.Copy), [ps.b], [st.b])
                                else:
                                    sigmoid_act(st.ap[:, cc, 0:gn], ps.ap[:, 0:gn], st.b, ps.b)
                            tok0 = bt0 * 128 + g * 512
                            if kind == "fm_qk":
                                pc = pt_col(tok0)
                                P.dma("sp", lambda e, st=st, c0=c0, pc=pc, gn=gn: e.dma_start(
                                    out=PT[c0:c0 + 512, pc:pc + gn].rearrange("(c p) t -> p c t", p=128),
                                    in_=st.ap[:, :, 0:gn]), [st.b], [])
                            else:
                                dstT = SGA if uname.startswith("ga") else SGB
                                r0 = 512 * int(uname[2])
                                oc = tok0 - 256
                                P.dma("sp", lambda e, st=st, dstT=dstT, r0=r0, oc=oc: e.dma_start(
                                    out=dstT[r0:r0 + 512, oc:oc + 512].rearrange("(c p) t -> p c t", p=128),
                                    in_=st.ap), [st.b], [])
                        continue
                    def post_tile(ps, tt, ti):
                        tok0 = tt * 128
                        half = int(uname[2]) if kind != "mg" else 0
                        if kind in ("mv", "dv"):
                            ob = nxt(ob512, "ob")
                            P.act(lambda e: e.activation(ob.ap, ps.ap[:, :], AF.Copy), [ps.b], [ob.b])
                            yield
                            dst = VM if kind == "mv" else VD
                            P.dma("sp", lambda e: e.dma_start(
                                out=dst[tok0:tok0 + 128, half * 512:(half + 1) * 512], in_=ob.ap), [ob.b], [])
                        elif kind == "mo":
                            w = nxt(w512, "w512")
                            sigmoid_act(w.ap, ps.ap[:, :], w.b, ps.b)
                            yield
                            oc = tok0 - 256
                            P.dma("sp", lambda e: e.dma_start(
                                out=SMO[oc:oc + 128, half * 512:(half + 1) * 512], in_=w.ap), [w.b], [])
                        elif kind == "mg":
                            gg = nxt(g16, "g16")
                            P.dve(lambda e: e.tensor_tensor(gg.ap, ps.ap[:, 0:16], BG[:], ALU.add), [ps.b, bK], [gg.b])
                            yield
                            P.dma("sp", lambda e: e.dma_start(out=GT[tok0:tok0 + 128, :], in_=gg.ap), [gg.b], [])
                        elif kind in ("dq", "dk"):
                            gains = QG if kind == "dq" else KG
                            sq = nxt(w512, "w512")
                            qn = nxt(w512, "w512")
                            t2 = nxt(w512, "w512")
                            t3 = nxt(w512, "w512")
                            st8 = nxt(s8, "s8")
                            rp = T(ropeblk.ap[:, ti, :])
                            rp.b = ropeblk.b
                            P.act(lambda e: e.activation(sq.ap, ps.ap[:, :], AF.Square), [ps.b], [sq.b])
                            yield
                            P.dve(lambda e: e.tensor_reduce(st8.ap, sq.ap.rearrange("p (g f) -> p g f", g=8), AX.X, ALU.add),
                                  [sq.b], [st8.b])
                            P.dve(lambda e: e.tensor_scalar(st8.ap, st8.ap, 1.0 / 64, EPS, ALU.mult, ALU.add), [st8.b], [st8.b])
                            yield
                            P.act(lambda e: e.activation(st8.ap, st8.ap, AF.Ln), [st8.b], [st8.b])
                            P.act(lambda e: e.activation(st8.ap, st8.ap, AF.Exp, scale=-0.5), [st8.b], [st8.b])
                            yield
                            P.dve(lambda e: e.tensor_tensor(
                                qn.ap.rearrange("p (g f) -> p g f", g=8), ps.ap[:, :].rearrange("p (g f) -> p g f", g=8),
                                st8.ap.unsqueeze(2).to_broadcast([128, 8, 64]), ALU.mult), [ps.b, st8.b], [qn.b])
                            yield
                            P.pool(lambda e: e.tensor_tensor(qn.ap, qn.ap, gains[:], ALU.mult), [qn.b, bK], [qn.b])
                            yield
                            qv = qn.ap.rearrange("p (g a s f) -> p g a s f", g=8, a=2, s=2)
                            tv = t2.ap.rearrange("p (g a s f) -> p g a s f", g=8, a=2, s=2)
                            sv = rp.ap[:, 64:128].rearrange("p (a s f) -> p a s f", a=2, s=2)
                            for s_ in range(2):
                                P.pool(lambda e, s_=s_: e.tensor_tensor(
                                    tv[:, :, :, s_, :], qv[:, :, :, 1 - s_, :],
                                    sv[:, :, s_, :].unsqueeze(1).to_broadcast([128, 8, 2, 16]), ALU.mult),
                                    [qn.b, rp.b], [t2.b])
                            P.dve(lambda e: e.tensor_tensor(
                                t3.ap.rearrange("p (g f) -> p g f", g=8), qn.ap.rearrange("p (g f) -> p g f", g=8),
                                rp.ap[:, 0:64].unsqueeze(1).to_broadcast([128, 8, 64]), ALU.mult), [qn.b, rp.b], [t3.b])
                            yield
                            ob = nxt(ob512, "ob")
                            P.dve(lambda e: e.tensor_tensor(ob.ap, t3.ap, t2.ap, ALU.add), [t3.b, t2.b], [ob.b])
                            yield
                            for hh in range(4):
                                P.pe(lambda e, hh=hh: e.transpose(PSB.ap[:, hh * 128:(hh + 1) * 128],
                                                                  ob.ap[:, hh * 128:(hh + 1) * 128], IDB[:]),
                                     [ob.b, bIDB], [PSB.b])
                            oT = nxt(obT, "obT")
                            P.act(lambda e: e.activation(oT.ap, PSB.ap[:, 0:512].rearrange("p (h t) -> p h t", h=4), AF.Copy),
                                  [PSB.b], [oT.b])
                            yield
                            if kind == "dq":
                                oc = tok0 - 256
                                P.dma("sp", lambda e: e.dma_start(
                                    out=QTD[half * 4:(half + 1) * 4, :, oc:oc + 128].rearrange("h p t -> p h t"),
                                    in_=oT.ap), [oT.b], [])
                            else:
                                P.dma("sp", lambda e: e.dma_start(
                                    out=KTD[half * 4:(half + 1) * 4, :, tok0:tok0 + 128].rearrange("h p t -> p h t"),
                                    in_=oT.ap), [oT.b], [])

                    GI = 3
                    for t0_ in range(0, bnt, GI):
                        gens = []
                        for ti in range(t0_, min(bnt, t0_ + GI)):
                            ps = PSF[ctr["ps"] % 6]
                            ctr["ps"] += 1
                            for k in range(8):
                                P.pe(lambda e, ps=ps, k=k, ti=ti: e.matmul(
                                    ps.ap[:, 0:ncols], xnT.ap[:, k, ti * 128:(ti + 1) * 128], wbf.ap[:, k, 0:ncols],
                                    start=(k == 0), stop=(k == 7)), [wbf.b, xnT.b], [ps.b])
                            gens.append(post_tile(ps, bt0 + ti, ti))
                        interleave(gens)
            barrier()
        if stop_after == 1:
            P.emit(nc, es)
            return nc

        if True:
            reset_arena()
            wins = rot(2, af, 4 * 516, lambda a: a.rearrange("p (c t) -> p c t", c=4))
            accs = rot(4, af, 512)
            es_ = rot(4, af, 512)
            okT = rot(4, ab, 512)
            LNS = T(af(1))
            P.pool(lambda e: e.memset(LNS.ap, math.log(128.0 ** -0.5)), [], [LNS.b])
            kms = rot(2, ab, 2048, lambda a: a.rearrange("p (t c) -> p t c", t=4))
            groups = [(0, 256)] + [(256 + g * 512, 512) for g in range(16)]
            ci = 0
            for gi, (tok0, gn) in enumerate(groups):
                own = 256 <= tok0 < 256 + OWN
                pc = pt_col(tok0)
                km = kms[gi % 2]
                for qk in ((0, 1) if own else (1,)):
                    win = wins[ci % 2]
                    ci += 1
                    P.dma("sp", lambda e, win=win, qk=qk, pc=pc, gn=gn: e.dma_start(
                        out=win.ap[:, :, 0:gn + 4],
                        in_=PT[qk * 512:(qk + 1) * 512, pc - 2:pc + gn + 2].rearrange("(c p) t -> p c t", p=128)),
                        [], [win.b])
                    def conv_chain(hc):
                        cc = qk * 4 + hc
                        acc = accs[hc]
                        ee = es_[hc]
                        ok = okT[hc]
                        P.dve(lambda e: e.tensor_scalar(acc.ap[:, 0:gn], win.ap[:, hc, 0:gn], CONV[:, cc, 0:1], None, ALU.mult),
                              [win.b, bK], [acc.b])
                        yield
                        for k in range(1, 5):
                            P.dve(lambda e, k=k: e.scalar_tensor_tensor(
                                acc.ap[:, 0:gn], win.ap[:, hc, k:k + gn], CONV[:, cc, k:k + 1], acc.ap[:, 0:gn],
                                ALU.mult, ALU.add), [win.b, bK, acc.b], [acc.b])
                            yield
                        P.act(lambda e: e.activation(ee.ap[:, 0:gn], acc.ap[:, 0:gn], AF.Exp, bias=NCB[:, cc:cc + 1], scale=-1.0),
                              [acc.b, bK], [ee.b])
                        P.act(lambda e: e.activation(ee.ap[:, 0:gn], ee.ap[:, 0:gn], AF.Ln, bias=1.0), [ee.b], [ee.b])
                        if qk == 0:
                            P.act(lambda e: e.activation(ee.ap[:, 0:gn], ee.ap[:, 0:gn], AF.Exp, bias=LNS.ap[:, 0:1], scale=-1.0),
                                  [ee.b, LNS.b], [ee.b])
                        else:
                            P.act(lambda e: e.activation(ee.ap[:, 0:gn], ee.ap[:, 0:gn], AF.Exp, scale=-1.0), [ee.b], [ee.b])
                        yield
                        P.dve(lambda e: e.scalar_tensor_tensor(
                            ok.ap[:, 0:gn], acc.ap[:, 0:gn], CONV[:, cc, 5:6], ee.ap[:, 0:gn], ALU.add, ALU.mult),
                            [acc.b, ee.b, bK], [ok.b])
                        yield
                        if qk == 0:
                            oc = tok0 - 256
                            P.dma("pool", lambda e: e.dma_start(out=QTM[hc, :, oc:oc + 512], in_=ok.ap), [ok.b], [])
                        else:
                            P.dma("pool", lambda e: e.dma_start(out=KTM[hc, :, tok0:tok0 + gn], in_=ok.ap[:, 0:gn]), [ok.b], [])
                            nt = gn // 128
                            for t in range(nt):
                                P.pe(lambda e, t=t: e.transpose(
                                    PSB.ap[:, t * 128:(t + 1) * 128], ok.ap[:, t * 128:(t + 1) * 128], IDB[:]),
                                    [ok.b, bIDB], [PSB.b])
                            P.act(lambda e: e.activation(
                                km.ap[:, 0:nt, hc * 128:(hc + 1) * 128],
                                PSB.ap[:, 0:nt * 128].rearrange("p (t c) -> p t c", t=nt), AF.Copy), [PSB.b], [km.b])

                    interleave([conv_chain(hc) for hc in range(4)])
                nt = gn // 128
                P.dma("pool", lambda e, km=km, tok0=tok0, nt=nt: e.dma_start(
                    out=KM[tok0:tok0 + nt * 128, :].rearrange("(t p) c -> p t c", p=128), in_=km.ap[:, 0:nt, :]),
                    [km.b], [])
            barrier()
        if stop_after == 2:
            P.emit(nc, es)
            return nc

        LFm = CST[:, CLF:CLF + 128]
        UFm = CST[:, CUF:CUF + 128]
        LRm = CST[:, CLR:CLR + 128]
        URm = CST[:, CUR:CUR + 128]
        NFm = CST[:, CNF:CNF + 128]
        NRm = CST[:, CNR:CNR + 128]

        def mlstm_dir(direction):
            reset_arena()
            isF = direction == "F"
            gofs = 0 if isF else 8
            Lm, Um, Nm = (LFm, UFm, NFm) if isF else (LRm, URm, NRm)
            Cst = T(af(4 * 257).rearrange("p (h e) -> p h e", h=4))
            Cbf = rot(2, ab, 4 * 257 + 4, lambda a: a[:, 0:4 * 257].rearrange("p (h e) -> p h e", h=4))
            gts = rot(2, af, 16)
            lfs = rot(2, af, 4)
            rhsE = rot(2, af, 512, lambda a: a.rearrange("p (h j) -> p h j", h=4))
            DTs = rot(2, af, 512, lambda a: a.rearrange("p (h j) -> p h j", h=4))
            AT = rot(4, ab, 128)
            numB = rot(4, af, 257)
            smalls = rot(2, af, 16)
            qTs = rot(2, ab, 512, lambda a: a.rearrange("p (h t) -> p h t", h=4))
            kTs = rot(2, ab, 512, lambda a: a.rearrange("p (h t) -> p h t", h=4))
            kms_ = rot(2, ab, 512)
            kws = rot(4, ab, 128)
            vxs = rot(2, ab, 4 * 258, lambda a: a[:, 0:4 * 257].rearrange("p (h e) -> p h e", h=4))
            numA = rot(4, af, 257)
            hts = rot(2, af, D)
            hfs = rot(2, af, D)
            smo = rot(2, af, D)
            dens = rot(4, af, 4)
            sq2 = rot(1, af, D)
            st4 = rot(2, af, 4)
            hab = rot(2, ab, D)
            haT = rot(2, ab, D, lambda a: a.rearrange("p (c t) -> p c t", c=8))
            for vx in vxs:
                P.pool(lambda e, vx=vx: e.memset(vx.ap[:, :, 256:257], 1.0), [], [vx.b])
            P.pool(lambda e: e.memset(Cst.ap, 0.0), [], [Cst.b])
            P.pool(lambda e: e.memset(Cbf[0].ap, 0.0), [], [Cbf[0].b])
            if isF:
                order = [(t, False) for t in (0, 1)] + [(t, True) for t in range(2, 34)]
            else:
                order = [(t, False) for t in (1, 0)] + [(t, False) for t in range(65, 33, -1)] + \
                        [(t, True) for t in range(33, 1, -1)]
            cx = {}

            def prologue(step):
                tt, outp = order[step]
                tok0 = tt * 128
                oc = tok0 - 256
                gt = gts[step % 2]
                lf = lfs[step % 2]
                sm = smalls[step % 2]
                kmt = kms_[step % 2]
                vx = vxs[step % 2]
                d = dict(tok0=tok0, oc=oc, gt=gt, lf=lf, sm=sm, kmt=kmt, vx=vx)
                P.dma("sp", lambda e: e.dma_start(out=gt.ap, in_=GT[tok0:tok0 + 128, :]), [], [gt.b])
                P.dma("sp", lambda e: e.dma_start(out=kmt.ap, in_=KM[tok0:tok0 + 128, :]), [], [kmt.b])
                P.dma("sp", lambda e: e.dma_start(
                    out=vx.ap[:, :, 0:256], in_=VM[tok0:tok0 + 128, :].rearrange("p (h e) -> p h e", h=4)), [], [vx.b])
                if outp:
                    qT = qTs[step % 2]
                    kT = kTs[step % 2]
                    d.update(qT=qT, kT=kT)
                    P.dma("sp", lambda e: e.dma_start(
                        out=qT.ap, in_=QTM[:, :, oc:oc + 128].rearrange("h p t -> p h t")), [], [qT.b])
                    P.dma("sp", lambda e: e.dma_start(
                        out=kT.ap, in_=KTM[:, :, tok0:tok0 + 128].rearrange("h p t -> p h t")), [], [kT.b])
                    d["ht"] = hts[step % 2]
                    if not isF:
                        hf = hfs[step % 2]
                        so = smo[step % 2]
                        d.update(hf=hf, so=so)
                        P.dma("sp", lambda e: e.dma_start(out=hf.ap, in_=HF[oc:oc + 128, :]), [], [hf.b])
                        P.dma("sp", lambda e: e.dma_start(out=so.ap, in_=SMO[oc:oc + 128, :]), [], [so.b])
                yield
                P.act(lambda e: e.activation(lf.ap, gt.ap[:, gofs + 4:gofs + 8], AF.Exp, scale=-1.0), [gt.b], [lf.b])
                P.act(lambda e: e.activation(lf.ap, lf.ap, AF.Ln, bias=1.0), [lf.b], [lf.b])
                yield
                P.dve(lambda e: e.tensor_scalar(lf.ap, lf.ap, -1.0, None, ALU.mult), [lf.b], [lf.b])
                yield
                pss = PSF[6]
                P.pe(lambda e: e.matmul(pss.ap[:, 0:4], Lm, lf.ap, start=True, stop=True), [bCST, lf.b], [pss.b])
                P.pe(lambda e: e.matmul(pss.ap[:, 4:8], Um, lf.ap, start=True, stop=True), [bCST, lf.b], [pss.b])
                P.pe(lambda e: e.matmul(pss.ap[:, 8:12], ones, lf.ap, start=True, stop=True), [bCST, lf.b], [pss.b])
                if outp:
                    rE = rhsE[step % 2]
                    DT = DTs[step % 2]
                    d["DT"] = DT
                    for h in range(4):
                        P.pool(lambda e, h=h: e.tensor_scalar(rE.ap[:, h, :], Lm, lf.ap[:, h:h + 1], None, ALU.mult),
                               [bCST, lf.b], [rE.b])
                yield
                P.dve(lambda e: e.tensor_copy(sm.ap[:, 0:12], pss.ap[:, 0:12]), [pss.b], [sm.b])
                P.dve(lambda e: e.tensor_tensor(sm.ap[:, 4:8], sm.ap[:, 4:8], gt.ap[:, gofs:gofs + 4], ALU.add),
                      [sm.b, gt.b], [sm.b])
                if outp:
                    pe_ = PSF[4]
                    P.pe(lambda e: e.matmul(pe_.ap[:, :], Um, rE.ap.rearrange("p h j -> p (h j)"), start=True, stop=False),
                         [bCST, rE.b], [pe_.b])
                    for h in range(4):
                        P.pe(lambda e, h=h: e.matmul(pe_.ap[:, h * 128:(h + 1) * 128], ident, Nm, start=False, stop=(h == 3)),
                             [bCST], [pe_.b])
                yield
                P.act(lambda e: e.activation(sm.ap[:, 0:12], sm.ap[:, 0:12], AF.Exp), [sm.b], [sm.b])
                if outp:
                    for h in range(4):
                        P.act(lambda e, h=h: e.activation(
                            DT.ap[:, h, :], pe_.ap[:, h * 128:(h + 1) * 128], AF.Exp, bias=gt.ap[:, gofs + h:gofs + h + 1]),
                            [pe_.b, gt.b], [DT.b])
                cx[step] = d

            for _ in prologue(0):
                pass
            for step, (tt, outp) in enumerate(order):
                d = cx[step]
                tok0, oc, gt, lf, sm, kmt, vx = d["tok0"], d["oc"], d["gt"], d["lf"], d["sm"], d["kmt"], d["vx"]
                qT, kT, DT, ht = d.get("qT"), d.get("kT"), d.get("DT"), d.get("ht")
                hf, so = d.get("hf"), d.get("so")
                cb_in = Cbf[step % 2]
                cb_out = Cbf[(step + 1) % 2]
                def head_chain(h):
                    X = PSF[(h % 2) * 2]
                    Y = PSF[(h % 2) * 2 + 1]
                    at = AT[h]
                    na = numA[h]
                    nb = numB[h]
                    dn = dens[h]
                    kw = kws[h]
                    if outp:
                        P.pe(lambda e: e.matmul(Y.ap[:, 0:257], qT.ap[:, h, :], cb_in.ap[:, h, :], start=True, stop=True),
                             [qT.b, cb_in.b], [Y.b])
                        P.pe(lambda e: e.matmul(X.ap[:, 260:388], kT.ap[:, h, :], qT.ap[:, h, :], start=True, stop=True),
                             [kT.b, qT.b], [X.b])
                        yield
                        P.act(lambda e: e.activation(nb.ap, Y.ap[:, 0:257], AF.Copy), [Y.b], [nb.b])
                        P.dve(lambda e: e.tensor_tensor(at.ap, X.ap[:, 260:388], DT.ap[:, h, :], ALU.mult), [X.b, DT.b], [at.b])
                        yield
                        P.pe(lambda e: e.matmul(X.ap[:, 0:257], at.ap, vx.ap[:, h, :], start=True, stop=True),
                             [at.b, vx.b], [X.b])
                        yield
                        P.dve(lambda e: e.scalar_tensor_tensor(
                            na.ap, nb.ap, sm.ap[:, h:h + 1], X.ap[:, 0:257], ALU.mult, ALU.add), [X.b, sm.b, nb.b], [na.b])
                        yield
                        P.dve(lambda e: e.tensor_scalar(dn.ap[:, 0:1], na.ap[:, 256:257], -1.0, None, ALU.mult), [na.b], [dn.b])
                        P.dve(lambda e: e.tensor_tensor(dn.ap[:, 1:2], dn.ap[:, 0:1], na.ap[:, 256:257], ALU.max),
                              [na.b, dn.b], [dn.b])
                        yield
                        P.dve(lambda e: e.tensor_scalar(dn.ap[:, 2:3], dn.ap[:, 1:2], 1.0, None, ALU.max), [dn.b], [dn.b])
                        P.dve(lambda e: e.reciprocal(dn.ap[:, 3:4], dn.ap[:, 2:3]), [dn.b], [dn.b])
                        yield
                        if isF:
                            P.dve(lambda e: e.tensor_scalar(
                                ht.ap[:, h * 256:(h + 1) * 256], na.ap[:, 0:256], dn.ap[:, 3:4], None, ALU.mult),
                                [na.b, dn.b], [ht.b])
                        else:
                            P.dve(lambda e: e.scalar_tensor_tensor(
                                ht.ap[:, h * 256:(h + 1) * 256], na.ap[:, 0:256], dn.ap[:, 3:4],
                                hf.ap[:, h * 256:(h + 1) * 256], ALU.mult, ALU.add), [na.b, dn.b, hf.b], [ht.b])
                    P.pool(lambda e: e.tensor_scalar(
                        kw.ap, kmt.ap[:, h * 128:(h + 1) * 128], sm.ap[:, 4 + h:5 + h], None, ALU.mult),
                        [kmt.b, sm.b], [kw.b])
                    yield
                    P.pe(lambda e: e.matmul(Y.ap[:, 0:257], kw.ap, vx.ap[:, h, :], start=True, stop=True),
                         [kw.b, vx.b], [Y.b])
                    yield
                    P.dve(lambda e: e.scalar_tensor_tensor(
                        Cst.ap[:, h, :], Cst.ap[:, h, :], sm.ap[:, 8 + h:9 + h], Y.ap[:, 0:257], ALU.mult, ALU.add),
                        [Cst.b, sm.b, Y.b], [Cst.b])

                pro = prologue(step + 1) if step + 1 < len(order) else iter(())
                interleave([head_chain(0), head_chain(1), pro])
                interleave([head_chain(2), head_chain(3)])
                P.act(lambda e, cb_out=cb_out: e.activation(cb_out.ap, Cst.ap, AF.Copy), [Cst.b], [cb_out.b])
                if outp and isF:
                    P.dma("pool", lambda e, ht=ht, oc=oc: e.dma_start(out=HF[oc:oc + 128, :], in_=ht.ap), [ht.b], [])
                if outp and not isF:
                    s4 = st4[step % 2]
                    P.act(lambda e, ht=ht: e.activation(sq2[0].ap, ht.ap, AF.Square), [ht.b], [sq2[0].b])
                    P.dve(lambda e, s4=s4: e.tensor_reduce(s4.ap, sq2[0].ap.rearrange("p (h f) -> p h f", h=4), AX.X, ALU.add),
                          [sq2[0].b], [s4.b])
                    rstd_chain(s4, 4, 256)
                    for h in range(4):
                        P.dve(lambda e, ht=ht, s4=s4, h=h: e.scalar_tensor_tensor(
                            ht.ap[:, h * 256:(h + 1) * 256], ht.ap[:, h * 256:(h + 1) * 256], s4.ap[:, h:h + 1],
                            MN[:, h * 256:(h + 1) * 256], ALU.mult, ALU.mult), [ht.b, s4.b, bK], [ht.b])
                    hb = hab[step % 2]
                    P.pool(lambda e, hb=hb, ht=ht, so=so: e.tensor_tensor(hb.ap, ht.ap, so.ap, ALU.mult), [ht.b, so.b], [hb.b])
                    for c in range(8):
                        P.pe(lambda e, hb=hb, c=c: e.transpose(PSB.ap[:, c * 128:(c + 1) * 128], hb.ap[:, c * 128:(c + 1) * 128],
                                                               IDB[:]), [hb.b, bIDB], [PSB.b])
                    hT = haT[step % 2]
                    P.act(lambda e, hT=hT: e.activation(hT.ap, PSB.ap.rearrange("p (c t) -> p c t", c=8), AF.Copy),
                          [PSB.b], [hT.b])
                    P.dma("pool", lambda e, hT=hT, oc=oc: e.dma_start(
                        out=HAT[:, :, oc:oc + 128].rearrange("c p t -> p c t"), in_=hT.ap), [hT.b], [])
            barrier()

        mlstm_dir("F")
        if stop_after == 3:
            P.emit(nc, es)
            return nc
        mlstm_dir("R")
        if stop_after == 4:
            P.emit(nc, es)
            return nc

        if True:
            reset_arena()
            NKC = 66
            KTs = rot(2, ab, TOK)
            Vs = rot(2, ab, NKC * 130, lambda a: a[:, 0:NKC * 129].rearrange("p (c e) -> p c e", c=NKC))
            QTs = rot(2, ab, OWN)
            PTs = rot(3, ab, 1024)
            obufs = rot(2, af, OWN, lambda a: a.rearrange("p (t e) -> p t e", t=32))
            sst = rot(2, af, 32)
            rr = rot(3, af, 4)
            tA = rot(2, af, 128)
            sqj = rot(1, af, 128)
            hbb = rot(2, ab, 128)
            hbT = rot(2, ab, 512)
            accS = rot(2, af, 8 * 129, lambda a: a.rearrange("p (i e) -> p i e", i=8))
            for v in Vs:
                P.pool(lambda e, v=v: e.memset(v.ap[:, :, 128:129], 1.0), [], [v.b])
            accs = []
            for i in range(8):
                bk = PSF[4 + i // 3]
                accs.append((bk, (i % 3) * 129))
            SB2 = [(PS2[0], (PSF[0].b, PSF[1].b)), (PS2[1], (PSF[2].b, PSF[3].b))]
            heads = []
            for h in range(8):
                Kt = KTs[h % 2]
                Vt = Vs[h % 2]
                Qt = QTs[h % 2]
                heads.append((Kt, Vt, Qt, obufs[h % 2], sst[h % 2]))
            steps = [(h, qb, kc) for h in range(8) for qb in range(8) for kc in range(NKC)]

            def emit_loads(h):
                Kt, Vt, Qt, _, _ = heads[h]
                P.dma("sp", lambda e: e.dma_start(out=Kt.ap, in_=KTD[h, :, :]), [], [Kt.b])
                P.dma("sp", lambda e: e.dma_start(out=Qt.ap, in_=QTD[h, :, :]), [], [Qt.b])
                P.dma("sp", lambda e: e.dma_start(
                    out=Vt.ap[:, :, 0:128], in_=VD[:, h * 128:(h + 1) * 128].rearrange("(c p) e -> p c e", p=128)),
                    [], [Vt.b])

            def emit_qk(i):
                h, qb, kc = steps[i]
                Kt, Vt, Qt, _, _ = heads[h]
                ps2, bb = SB2[i % 2]
                P.pe(lambda e: e.matmul(ps2[:, 0:512], Kt.ap[0:64, kc * 128:(kc + 1) * 128],
                                        Qt.ap[0:64, qb * 512:(qb + 1) * 512], start=True, stop=True),
                     [Kt.b, Qt.b], [bb[0]])
                P.pe(lambda e: e.matmul(ps2[:, 512:1024], Kt.ap[64:128, kc * 128:(kc + 1) * 128],
                                        Qt.ap[64:128, qb * 512:(qb + 1) * 512], start=True, stop=True),
                     [Kt.b, Qt.b], [bb[1]])

            emit_loads(0)
            emit_qk(0)
            for i, (h, qb, kc) in enumerate(steps):
                Kt, Vt, Qt, ob, ssq = heads[h]
                if qb == 0 and kc == 0 and h + 1 < 8:
                    emit_loads(h + 1)
                ps2, bb = SB2[i % 2]
                pt = PTs[i % 3]
                P.act(lambda e: e.activation(pt.ap, ps2[:, :], AF.Exp), [bb[0], bb[1]], [pt.b])
                if i + 1 < len(steps):
                    emit_qk(i + 1)
                for m in range(2):
                    for qs in range(4):
                        bk, o0 = accs[m * 4 + qs]
                        P.pe(lambda e, bk=bk, o0=o0, m=m, qs=qs: e.matmul(
                            bk.ap[:, o0:o0 + 129], pt.ap[:, m * 512 + qs * 128:m * 512 + (qs + 1) * 128],
                            Vt.ap[:, kc, :], start=(kc == 0), stop=(kc == NKC - 1)), [pt.b, Vt.b], [bk.b])
                if kc != NKC - 1:
                    continue
                aS = accS[(h * 8 + qb) % 2]
                for bi_ in range(3):
                    n_ = 3 if bi_ < 2 else 2
                    bk = PSF[4 + bi_]
                    P.dve(lambda e, bk=bk, bi_=bi_, n_=n_: e.tensor_copy(
                        aS.ap[:, bi_ * 3:bi_ * 3 + n_, :], bk.ap[:, 0:n_ * 129].rearrange("p (i e) -> p i e", i=n_)),
                        [bk.b], [aS.b])
                for qs in range(4):
                    qt = qb * 4 + qs
                    r = rr[qt % 3]
                    ta = tA[qt % 2]
                    P.dve(lambda e, r=r, qs=qs: e.reciprocal(r.ap[:, 0:1], aS.ap[:, qs, 128:129]), [aS.b], [r.b])
                    P.dve(lambda e, r=r, qs=qs: e.reciprocal(r.ap[:, 1:2], aS.ap[:, 4 + qs, 128:129]), [aS.b, r.b], [r.b])
                    P.dve(lambda e, r=r: e.tensor_tensor(r.ap[:, 2:3], r.ap[:, 1:2], NLAM[:], ALU.mult), [r.b, bK], [r.b])
                    P.pool(lambda e, r=r, ta=ta, qs=qs: e.tensor_scalar(ta.ap, aS.ap[:, qs, 0:128], r.ap[:, 0:1], None,
                                                                        ALU.mult), [aS.b, r.b], [ta.b])
                    P.dve(lambda e, r=r, ta=ta, qs=qs, qt=qt: e.scalar_tensor_tensor(
                        ob.ap[:, qt, :], aS.ap[:, 4 + qs, 0:128], r.ap[:, 2:3], ta.ap, ALU.mult, ALU.add),
                        [aS.b, r.b, ta.b], [ob.b])
                    P.pool(lambda e, qt=qt: e.tensor_tensor(sqj[0].ap, ob.ap[:, qt, :], ob.ap[:, qt, :], ALU.mult),
                           [ob.b], [sqj[0].b])
                    P.dve(lambda e, qt=qt: e.tensor_reduce(ssq.ap[:, qt:qt + 1], sqj[0].ap, AX.X, ALU.add),
                          [sqj[0].b], [ssq.b])
                if qb != 7:
                    continue
                rstd_chain(ssq, 32, 128)
                for g4 in range(8):
                    hT = hbT[g4 % 2]
                    for j in range(4):
                        qt = g4 * 4 + j
                        hb = hbb[qt % 2]
                        P.dve(lambda e, hb=hb, qt=qt: e.scalar_tensor_tensor(
                            hb.ap, ob.ap[:, qt, :], ssq.ap[:, qt:qt + 1], G128[:], ALU.mult, ALU.mult),
                            [ob.b, ssq.b, bK], [hb.b])
                        P.pe(lambda e, hb=hb, j=j: e.transpose(PSB.ap[:, j * 128:(j + 1) * 128], hb.ap, IDB[:]),
                             [hb.b, bIDB], [PSB.b])
                    P.dve(lambda e, hT=hT: e.tensor_copy(hT.ap, PSB.ap[:, 0:512]), [PSB.b], [hT.b])
                    P.dma("pool", lambda e, hT=hT, g4=g4: e.dma_start(out=HBT[h, :, g4 * 512:(g4 + 1) * 512], in_=hT.ap),
                          [hT.b], [])
            barrier()
        if stop_after == 5:
            P.emit(nc, es)
            return nc

        if True:
            reset_arena()
            v8 = lambda a: a.rearrange("p (k n) -> p k n", k=8)
            WA = T(v8(ab(8 * D)))
            WB = T(v8(ab(8 * D)))
            WO = T(v8(ab(8 * D)))
            for (W, src) in ((WA, w_a), (WB, w_b), (WO, w_o)):
                for hf_ in range(2):
                    load_w_cast(src[:, hf_ * 512:(hf_ + 1) * 512].rearrange("(k p) n -> p k n", p=128),
                                W.ap[:, :, hf_ * 512:(hf_ + 1) * 512], W.b)
            hAs = rot(1, ab, 8 * 512, v8)
            hBs = rot(1, ab, 8 * 512, v8)
            sgs = rot(4, af, 512)
            yTs = rot(1, ab, 8 * 512, v8)
            tms = rot(3, af, 512)
            xts = rot(2, af, D)
            x1s = rot(2, af, D)
            for g in range(8):
                hA = hAs[0]
                hB = hBs[0]
                yT = yTs[0]
                P.dma("sp", lambda e, hA=hA, g=g: e.dma_start(
                    out=hA.ap, in_=HAT[:, :, g * 512:(g + 1) * 512].rearrange("c p t -> p c t")), [], [hA.b])
                P.dma("sp", lambda e, hB=hB, g=g: e.dma_start(
                    out=hB.ap, in_=HBT[:, :, g * 512:(g + 1) * 512].rearrange("c p t -> p c t")), [], [hB.b])
                for fc in range(8):
                    sa = sgs[(fc * 2) % 4]
                    sb_ = sgs[(fc * 2 + 1) % 4]
                    P.dma("sp", lambda e, sa=sa, fc=fc, g=g: e.dma_start(
                        out=sa.ap, in_=SGA[fc * 128:(fc + 1) * 128, g * 512:(g + 1) * 512]), [], [sa.b])
                    P.dma("sp", lambda e, sb_=sb_, fc=fc, g=g: e.dma_start(
                        out=sb_.ap, in_=SGB[fc * 128:(fc + 1) * 128, g * 512:(g + 1) * 512]), [], [sb_.b])
                    pa = PSF[(fc % 2) * 2]
                    pb = PSF[(fc % 2) * 2 + 1]
                    for k in range(8):
                        P.pe(lambda e, pa=pa, hA=hA, k=k, fc=fc: e.matmul(
                            pa.ap[:, :], WA.ap[:, k, fc * 128:(fc + 1) * 128], hA.ap[:, k, :], start=(k == 0), stop=(k == 7)),
                            [WA.b, hA.b], [pa.b])
                    for k in range(8):
                        P.pe(lambda e, pb=pb, hB=hB, k=k, fc=fc: e.matmul(
                            pb.ap[:, :], WB.ap[:, k, fc * 128:(fc + 1) * 128], hB.ap[:, k, :], start=(k == 0), stop=(k == 7)),
                            [WB.b, hB.b], [pb.b])
                    tm = tms[fc % 3]
                    P.dve(lambda e, tm=tm, pa=pa, sa=sa: e.tensor_tensor(tm.ap, pa.ap[:, :], sa.ap, ALU.mult), [pa.b, sa.b], [tm.b])
                    P.dve(lambda e, sb_=sb_, pb=pb: e.tensor_tensor(sb_.ap, pb.ap[:, :], sb_.ap, ALU.mult), [pb.b, sb_.b], [sb_.b])
                    P.pool(lambda e, yT=yT, tm=tm, sb_=sb_, fc=fc: e.tensor_tensor(yT.ap[:, fc, :], tm.ap, sb_.ap, ALU.add),
                           [tm.b, sb_.b], [yT.b])
                for j in range(4):
                    tl = g * 4 + j
                    xt = xts[tl % 2]
                    x1 = x1s[tl % 2]
                    P.dma("sp", lambda e, xt=xt, tl=tl: e.dma_start(out=xt.ap, in_=xa[256 + tl * 128:256 + (tl + 1) * 128, :]),
                          [], [xt.b])
                    for nh in range(2):
                        pz = PSF[4 + nh]
                        for k in range(8):
                            P.pe(lambda e, pz=pz, yT=yT, k=k, j=j, nh=nh: e.matmul(
                                pz.ap[:, :], yT.ap[:, k, j * 128:(j + 1) * 128], WO.ap[:, k, nh * 512:(nh + 1) * 512],
                                start=(k == 0), stop=(k == 7)), [yT.b, WO.b], [pz.b])
                        P.dve(lambda e, pz=pz, x1=x1, nh=nh: e.tensor_tensor(
                            x1.ap[:, nh * 512:(nh + 1) * 512], pz.ap[:, :], G1[:, nh * 512:(nh + 1) * 512], ALU.mult),
                            [pz.b, bMOD], [x1.b])
                    P.pool(lambda e, x1=x1, xt=xt: e.tensor_tensor(x1.ap, x1.ap, xt.ap, ALU.add), [x1.b, xt.b], [x1.b])
                    P.dma("pool", lambda e, x1=x1, tl=tl: e.dma_start(out=X1[tl * 128:(tl + 1) * 128, :], in_=x1.ap), [x1.b], [])
            barrier()
        if stop_after == 6:
            P.emit(nc, es)
            return nc

        if True:
            reset_arena()
            W2 = T(ab(22 * D).rearrange("p (c n) -> p c n", c=22))
            for j in range(11):
                load_w_cast(w_f2[j * 256:(j + 1) * 256, :].rearrange("(c p) n -> p c n", p=128),
                            W2.ap[:, 2 * j:2 * j + 2, :], W2.b)
            xnT = T(ab(8 * 512).rearrange("p (k t) -> p k t", k=8))
            uT = T(ab(22 * 512).rearrange("p (c t) -> p c t", c=22))
            x1r = rot(4, af, D)
            sqs = rot(2, af, D)
            t1s = rot(2, af, D)
            sss = rot(4, af, 1)
            xns = rot(2, ab, D)
            wbf = rot(3, ab, 8 * 256, lambda a: a.rearrange("p (k n) -> p k n", k=8))
            ees = rot(3, af, 512)
            tts = rot(2, af, 512)
            outs = rot(2, af, D)
            for sg in range(8):
                gens = []
                for ti in range(4):
                    tl = sg * 4 + ti
                    xt = x1r[ti]
                    P.dma("sp", lambda e, xt=xt, tl=tl: e.dma_start(out=xt.ap, in_=X1[tl * 128:(tl + 1) * 128, :]), [], [xt.b])
                    gens.append(mod_norm_T(xt, A2[:], B2, [bK, bMOD], sqs[ti % 2], sss[ti], t1s[ti % 2], xns[ti % 2],
                                           xnT.ap[:, :, ti * 128:(ti + 1) * 128], xnT.b))
                    if len(gens) == 2:
                        interleave(gens)
                        gens = []
                if sg == 0:
                    load_w_cast(w_f1[0], wbf[0].ap, wbf[0].b)
                for j in range(22):
                    jj = sg * 22 + j
                    wb_ = wbf[jj % 3]
                    if jj + 1 < 8 * 22:
                        nb = wbf[(jj + 1) % 3]
                        load_w_cast(w_f1[(j + 1) % 22], nb.ap, nb.b)
                    pa = PSF[(j % 2) * 2]
                    pb = PSF[(j % 2) * 2 + 1]
                    for k in range(8):
                        P.pe(lambda e, pa=pa, wb_=wb_, k=k: e.matmul(
                            pa.ap[:, :], wb_.ap[:, k, 0:128], xnT.ap[:, k, :], start=(k == 0), stop=(k == 7)),
                            [wb_.b, xnT.b], [pa.b])
                    for k in range(8):
                        P.pe(lambda e, pb=pb, wb_=wb_, k=k: e.matmul(
                            pb.ap[:, :], wb_.ap[:, k, 128:256], xnT.ap[:, k, :], start=(k == 0), stop=(k == 7)),
                            [wb_.b, xnT.b], [pb.b])
                    ee = ees[j % 3]
                    tq = tts[j % 2]
                    sigmoid_act(ee.ap, pa.ap[:, :], ee.b, pa.b)
                    P.dve(lambda e, ee=ee, pa=pa, tq=tq: e.tensor_tensor(tq.ap, pa.ap[:, :], ee.ap, ALU.mult), [pa.b, ee.b], [tq.b])
                    P.dve(lambda e, tq=tq, pb=pb, j=j: e.tensor_tensor(uT.ap[:, j, :], pb.ap[:, :], tq.ap, ALU.mult),
                          [pb.b, tq.b], [uT.b])
                for ti in range(4):
                    tl = sg * 4 + ti
                    xt = x1r[ti]
                    ot = outs[ti % 2]
                    for nh in range(2):
                        pz = PSF[4 + nh]
                        for c in range(22):
                            P.pe(lambda e, pz=pz, c=c, ti=ti, nh=nh: e.matmul(
                                pz.ap[:, :], uT.ap[:, c, ti * 128:(ti + 1) * 128], W2.ap[:, c, nh * 512:(nh + 1) * 512],
                                start=(c == 0), stop=(c == 21)), [uT.b, W2.b], [pz.b])
                        P.dve(lambda e, pz=pz, ot=ot, nh=nh: e.tensor_tensor(
                            ot.ap[:, nh * 512:(nh + 1) * 512], pz.ap[:, :], G2[:, nh * 512:(nh + 1) * 512], ALU.mult),
                            [pz.b, bMOD], [ot.b])
                    P.pool(lambda e, ot=ot, xt=xt: e.tensor_tensor(ot.ap, ot.ap, xt.ap, ALU.add), [ot.b, xt.b], [ot.b])
                    P.dma("pool", lambda e, ot=ot, tl=tl: e.dma_start(out=out[tl * 128:(tl + 1) * 128, :], in_=ot.ap), [ot.b], [])
        P.emit(nc, es)
    return nc


def _consts():
    t = np.arange(128)
    c = np.zeros((128, CSTW), np.float32)
    c[:, CI:CI + 128] = np.eye(128)
    c[:, CLF:CLF + 128] = (t[:, None] <= t[None, :])
    c[:, CUF:CUF + 128] = (t[:, None] > t[None, :])
    c[:, CLR:CLR + 128] = (t[:, None] >= t[None, :])
    c[:, CUR:CUR + 128] = (t[:, None] < t[None, :])
    c[:, CNF:CNF + 128] = np.where(t[:, None] <= t[None, :], 0.0, NEG)
    c[:, CNR:CNR + 128] = np.where(t[:, None] >= t[None, :], 0.0, NEG)
    c[:, CON:CON + 128] = 1.0
    return c


def _rope_table(S):
    pos = np.arange(S)
    row = (pos // 64).astype(np.float32)
    col = (pos % 64).astype(np.float32)
    inv = (np.float32(10000.0) ** (-np.arange(16, dtype=np.float32) / np.float32(16))).astype(np.float32)
    ar = row[:, None] * inv[None, :]
    ac = col[:, None] * inv[None, :]
    cos = np.concatenate([np.cos(ar), np.cos(ar), np.cos(ac), np.cos(ac)], axis=1)
    sin = np.concatenate([-np.sin(ar), np.sin(ar), -np.sin(ac), np.sin(ac)], axis=1)
    return np.concatenate([cos, sin], axis=1).astype(np.float32)


def make_in_maps(inp):
    f = lambda a: np.ascontiguousarray(np.asarray(a, dtype=np.float32))
    x, c, ctx, c_ctx = f(inp["x"]), f(inp["c"]), f(inp["ctx"]), f(inp["c_ctx"])
    w_in = f(inp["w_in"][0])
    b_gate = f(inp["b_gate"][0])
    conv_w = f(inp["conv_w"][0])
    conv_b = f(inp["conv_b"][0])
    cst = _consts()
    rt = _rope_table(8192)
    ctx_rt = np.zeros((256, 128), np.float32)
    ctx_rt[:, 0:64] = 1.0
    perm = np.arange(8208)
    perm[3072:3080] = np.arange(3080, 3088)
    perm[3080:3088] = np.arange(3072, 3080)
    bperm = np.concatenate([np.arange(8, 16), np.arange(0, 8)])
    wf = f(inp["w_ffn_in"][0]).reshape(8, 128, 2, 22, 128)
    w_f1r = np.ascontiguousarray(wf.transpose(3, 1, 0, 2, 4).reshape(22, 128, 8, 256))
    maps = []
    for core in range(8):
        b, half = core // 2, core % 2
        if half == 0:
            xl, cl, rl = x[b], ctx[b], rt
            wl, bg, cw = w_in, b_gate, conv_w
        else:
            xl, cl, rl = x[b][::-1], ctx[b][::-1], rt[::-1]
            wl, bg, cw = w_in[:, perm], b_gate[bperm], conv_w[::-1]
        xa = np.ascontiguousarray(np.concatenate([cl, xl], axis=0))
        cvec = np.ascontiguousarray(np.concatenate([c[b].reshape(8, 128).T, c_ctx.reshape(8, 128).T], axis=1))
        small = np.concatenate([f(inp["norm1"][0]), f(inp["norm2"][0]), f(inp["mlstm_norm"][0]), f(inp["diff_norm"][0]),
                                f(inp["q_norm"][0]), f(inp["k_norm"][0]), bg, f(inp["lam_vecs"][0]).reshape(-1)])[None, :]
        convp = np.concatenate([cw.T.reshape(8, 128, 5), conv_b.reshape(8, 128, 1)], axis=2).transpose(1, 0, 2)
        maps.append({
            "xa": xa, "cvec": cvec, "w_mod": f(inp["w_mod"][0]), "b_mod": f(inp["b_mod"][0])[None, :],
            "small": np.ascontiguousarray(small), "cst": cst, "convp": np.ascontiguousarray(convp),
            "rope": np.ascontiguousarray(np.concatenate([ctx_rt, rl], axis=0)),
            "w_in": np.ascontiguousarray(wl), "w_a": f(inp["w_branch_a"][0]), "w_b": f(inp["w_branch_b"][0]),
            "w_o": f(inp["w_out"][0]), "w_f1": w_f1r, "w_f2": f(inp["w_ffn_out"][0]),
        })
    return maps


def kernel(**inputs):
    maps = make_in_maps(inputs)
    nc = build()
    res = run_bass_kernel_spmd(nc, maps, core_ids=list(range(8)))
    outp = np.zeros((4, 8192, D), np.float32)
    for core in range(8):
        b, half = core // 2, core % 2
        o = np.asarray(res.results[core]["out"], dtype=np.float32)
        if half == 0:
            outp[b, 0:OWN] = o
        else:
            outp[b, OWN:] = o[::-1]
    return outp
```

```python
import math
import types
import numpy as np
from contextlib import ExitStack
import concourse.bass as bass
import concourse.mybir as mybir
from concourse.bass_utils import run_bass_kernel_spmd

F32 = mybir.dt.float32
BF16 = mybir.dt.bfloat16
AF = mybir.ActivationFunctionType
ALU = mybir.AluOpType
AX = mybir.AxisListType

NDS = 8


def _freeze(fn):
    if fn.__closure__ is None:
        return fn
    cells = []
    for c in fn.__closure__:
        try:
            cells.append(types.CellType(c.cell_contents))
        except ValueError:
            cells.append(c)
    g = types.FunctionType(fn.__code__, fn.__globals__, fn.__name__, fn.__defaults__, tuple(cells))
    g.__kwdefaults__ = fn.__kwdefaults__
    return g


class Buf:
    __slots__ = ("name", "lw", "rd", "rd_dma")

    def __init__(self, name=""):
        self.name = name
        self.lw = None
        self.rd = {}
        self.rd_dma = []


class Op:
    __slots__ = ("eng", "fn", "reads", "writes", "dma", "deps", "signal", "semkey", "semval", "waits", "idx")

    def __init__(self, eng, fn, reads, writes, dma):
        self.eng = eng
        self.fn = fn
        self.reads = reads
        self.writes = writes
        self.dma = dma
        self.deps = None
        self.signal = False
        self.semkey = None
        self.semval = 0
        self.waits = None


class Prog:
    ENGS = ("pe", "act", "dve", "pool", "sp")

    def __init__(self):
        self.ops = []
        self.G = Buf("G")

    def add(self, eng, fn, reads=(), writes=(), dma=False):
        op = Op(eng, _freeze(fn), tuple(reads) + (self.G,), tuple(writes), dma)
        self.ops.append(op)
        return op

    def pe(self, fn, reads, writes):
        return self.add("pe", fn, reads, writes)

    def act(self, fn, reads, writes):
        return self.add("act", fn, reads, writes)

    def dve(self, fn, reads, writes):
        return self.add("dve", fn, reads, writes)

    def pool(self, fn, reads, writes):
        return self.add("pool", fn, reads, writes)

    def dma(self, q, fn, reads, writes):
        return self.add(q, fn, reads, writes, dma=True)

    def barrier(self, fn):
        op = Op("pool", _freeze(fn), (), (self.G,), False)
        self.ops.append(op)

    def schedule(self):
        for i, op in enumerate(self.ops):
            op.idx = i
            deps = set()
            for b in op.reads:
                if b.lw is not None:
                    d = b.lw
                    if not (d.eng == "pe" and op.eng == "pe" and not d.dma):
                        deps.add(d)
            for b in op.writes:
                if b.lw is not None:
                    d = b.lw
                    if d.dma or op.dma or d.eng != op.eng:
                        deps.add(d)
                for e, d in b.rd.items():
                    if e != op.eng or op.dma:
                        deps.add(d)
                for d in b.rd_dma:
                    deps.add(d)
            for b in op.reads:
                if op.dma:
                    b.rd_dma.append(op)
                else:
                    b.rd[op.eng] = op
            for b in op.writes:
                b.lw = op
                b.rd = {}
                b.rd_dma = []
            deps.discard(op)
            op.deps = deps
            for d in deps:
                d.signal = True
        cnt = {e: 0 for e in self.ENGS}
        dcnt = {e: 0 for e in self.ENGS}
        for op in self.ops:
            if op.dma:
                j = dcnt[op.eng]
                dcnt[op.eng] += 1
                op.semkey = ("d", op.eng, j % NDS)
                op.semval = 16 * (j // NDS + 1)
                op.signal = True
            elif op.signal:
                cnt[op.eng] += 1
                op.semkey = ("c", op.eng)
                op.semval = cnt[op.eng]
        waited = {e: {} for e in self.ENGS}
        for op in self.ops:
            w = {}
            if op.dma and op.semval > 16:
                w[op.semkey] = op.semval - 16
            for d in op.deps:
                if w.get(d.semkey, 0) < d.semval:
                    w[d.semkey] = d.semval
            wt = waited[op.eng]
            out = []
            for k, v in w.items():
                if wt.get(k, 0) < v:
                    wt[k] = v
                    out.append((k, v))
            op.waits = out
        self.final_dma = {}
        for op in self.ops:
            if op.dma:
                self.final_dma[op.semkey] = max(self.final_dma.get(op.semkey, 0), op.semval)

    def emit(self, nc, es):
        self.schedule()
        sems = {}
        for e in self.ENGS:
            sems[("c", e)] = es.enter_context(nc.semaphore("c_" + e))
            for j in range(NDS):
                sems[("d", e, j)] = es.enter_context(nc.semaphore("d_%s_%d" % (e, j)))
        block = es.enter_context(nc.Block())
        by_eng = {e: [op for op in self.ops if op.eng == e] for e in self.ENGS}
        final_dma = self.final_dma

        def run(eng_name, e):
            for op in by_eng[eng_name]:
                for k, v in op.waits:
                    e.wait_ge(sems[k], v)
                ins = op.fn(e)
                if op.signal:
                    ins.then_inc(sems[op.semkey], 16 if op.dma else 1)
            for k, v in final_dma.items():
                if k[1] == eng_name:
                    e.wait_ge(sems[k], v)

        @block.tensor
        def _(e):
            run("pe", e)

        @block.scalar
        def _(e):
            run("act", e)

        @block.vector
        def _(e):
            run("dve", e)

        @block.gpsimd
        def _(e):
            run("pool", e)

        @block.sync
        def _(e):
            run("sp", e)


D = 1024
NT = 66
TOK = 8448
OWN = 4096
PTW = 8456
FFH = 2816
EPS = 1e-6
NEG = -30000.0
O_N1, O_N2, O_MN, O_DN, O_QN, O_KN, O_BG, O_LV, SMW = 0, 1024, 2048, 3072, 3200, 3264, 3328, 3344, 3600
CI, CLF, CUF, CLR, CUR, CNF, CNR, CON, CZE, CSTW = 0, 128, 256, 384, 512, 640, 768, 896, 1024, 1152

UNITS = [
    ("mq", 0, 512, "fm_qk", False), ("mk", 512, 512, "fm_qk", False),
    ("mv0", 1024, 512, "mv", False), ("mv1", 1536, 512, "mv", False),
    ("mo0", 2048, 512, "mo", True), ("mo1", 2560, 512, "mo", True),
    ("mg", 3072, 16, "mg", False),
    ("dq0", 3088, 512, "dq", True), ("dq1", 3600, 512, "dq", True),
    ("dk0", 4112, 512, "dk", False), ("dk1", 4624, 512, "dk", False),
    ("dv0", 5136, 512, "dv", False), ("dv1", 5648, 512, "dv", False),
    ("ga0", 6160, 512, "fm_sig", True), ("ga1", 6672, 512, "fm_sig", True),
    ("gb0", 7184, 512, "fm_sig", True), ("gb1", 7696, 512, "fm_sig", True),
]


def pt_col(tok):
    return 2 + tok if tok < 256 else 262 + (tok - 256)


def build(stop_after=None, dbg=False):
    nc = bass.Bass("TRN2", target_bir_lowering=False)
    P = Prog()
    skind = "ExternalOutput" if dbg else "Internal"

    def din(name, shape, dt=F32):
        return nc.dram_tensor(name, list(shape), dt, kind="ExternalInput").ap()

    def dscr(name, shape, dt):
        return nc.dram_tensor(name, list(shape), dt, kind=skind).ap()

    xa = din("xa", [TOK, D])
    cvec = din("cvec", [128, 16])
    w_mod = din("w_mod", [D, 6 * D])
    b_mod = din("b_mod", [1, 6 * D])
    small = din("small", [1, SMW])
    cst = din("cst", [128, CSTW])
    convp = din("convp", [128, 8, 6])
    rope = din("rope", [TOK, 128])
    w_in = din("w_in", [D, 8208])
    w_a = din("w_a", [D, D])
    w_b = din("w_b", [D, D])
    w_o = din("w_o", [D, D])
    w_f1 = din("w_f1", [22, 128, 8, 256])
    w_f2 = din("w_f2", [FFH, D])
    out = nc.dram_tensor("out", [OWN, D], F32, kind="ExternalOutput").ap()

    PT = dscr("PT", [D, PTW], F32)
    VM = dscr("VM", [TOK, D], BF16)
    GT = dscr("GT", [TOK, 16], F32)
    SMO = dscr("SMO", [OWN, D], F32)
    QTD = dscr("QTD", [8, 128, OWN], BF16)
    KTD = dscr("KTD", [8, 128, TOK], BF16)
    VD = dscr("VD", [TOK, D], BF16)
    SGA = dscr("SGA", [D, OWN], F32)
    SGB = dscr("SGB", [D, OWN], F32)
    QTM = dscr("QTM", [4, 128, OWN], BF16)
    KTM = dscr("KTM", [4, 128, TOK], BF16)
    KM = dscr("KM", [TOK, 512], BF16)
    HF = dscr("HF", [OWN, D], F32)
    HAT = dscr("HAT", [8, 128, OWN], BF16)
    HBT = dscr("HBT", [8, 128, OWN], BF16)
    X1 = dscr("X1", [OWN, D], F32)

    with ExitStack() as es:
        cnt = [0]

        def sbt(shape, dt):
            cnt[0] += 1
            return es.enter_context(nc.sbuf_tensor("t%d" % cnt[0], list(shape), dt))

        CST = sbt([128, CSTW], F32)
        bCST = Buf()
        IDB = sbt([128, 128], BF16)
        bIDB = Buf()
        ONEB = sbt([128, 128], BF16)
        B1p = sbt([128, D], F32)
        G1p = sbt([128, D], F32)
        B2p = sbt([128, D], F32)
        G2p = sbt([128, D], F32)
        CB1p = sbt([128, D], F32)
        bMOD = Buf()
        MN = sbt([128, D], F32)
        BG = sbt([128, 16], F32)
        bSM = Buf()
        A1 = sbt([128, D], F32)
        CA1 = sbt([128, D], F32)
        A2 = sbt([128, D], F32)
        QG = sbt([128, 512], F32)
        KG = sbt([128, 512], F32)
        G128 = sbt([128, 128], F32)
        NLAM = sbt([128, 1], F32)
        CONV = sbt([128, 8, 6], F32)
        NCB = sbt([128, 8], F32)
        bK = Buf()
        JUNK = sbt([128, 8], F32)
        ident = CST[:, CI:CI + 128]
        ones = CST[:, CON:CON + 128]
        B1 = B1p[:]
        G1 = G1p[:]
        B2 = B2p[:]
        G2 = G2p[:]
        CB1 = CB1p[:]

        AR_N = 38400
        ARENA = sbt([128, AR_N], F32)
        offs = [0]

        def reset_arena():
            offs[0] = 0

        def af(n):
            a = offs[0]
            offs[0] += n
            assert offs[0] <= AR_N, ("arena overflow", offs[0])
            return ARENA[:, a:a + n]

        def ab(n):
            w = (n + 1) // 2
            a = offs[0]
            offs[0] += w
            assert offs[0] <= AR_N, ("arena overflow", offs[0])
            return ARENA[:, a:a + w].bitcast(BF16)[:, 0:n]

        class T:
            __slots__ = ("ap", "b")

            def __init__(self, ap):
                self.ap = ap
                self.b = Buf()

        def rot(n, alloc, width, view=None):
            res = []
            for _ in range(n):
                a = alloc(width)
                if view is not None:
                    a = view(a)
                res.append(T(a))
            return res

        PS2 = [es.enter_context(nc.psum_tensor("ps2_%d" % i, [128, 1024], F32)) for i in range(2)]
        PSF = [T(PS2[0][:, 0:512]), T(PS2[0][:, 512:1024]), T(PS2[1][:, 0:512]), T(PS2[1][:, 512:1024])]
        PSF += [T(es.enter_context(nc.psum_tensor("ps%d" % i, [128, 512], F32))[:, :]) for i in range(4, 7)]
        PSB = T(es.enter_context(nc.psum_tensor("psb", [128, 1024], BF16)))

        def barrier():
            P.barrier(lambda e: e.memset(JUNK[:], 0.0))

        reset_arena()
        P.dma("sp", lambda e: e.dma_start(out=CST[:], in_=cst[:, :]), [], [bCST])
        P.dma("sp", lambda e: e.dma_start(out=CONV[:], in_=convp[:, :, :]), [], [bK])
        P.dve(lambda e: e.tensor_copy(IDB[:], ident), [bCST], [bIDB])
        P.dve(lambda e: e.tensor_copy(ONEB[:], ones), [bCST], [bIDB])
        cv = T(af(16))
        ce = T(af(16))
        cs = T(af(16))
        P.dma("sp", lambda e: e.dma_start(out=cv.ap, in_=cvec[:, :]), [], [cv.b])
        P.act(lambda e: e.activation(ce.ap, cv.ap, AF.Exp, scale=-1.0), [cv.b], [ce.b])
        P.dve(lambda e: e.tensor_scalar(ce.ap, ce.ap, 1.0, None, ALU.add), [ce.b], [ce.b])
        P.dve(lambda e: e.reciprocal(ce.ap, ce.ap), [ce.b], [ce.b])
        P.dve(lambda e: e.tensor_tensor(cs.ap, cv.ap, ce.ap, ALU.mult), [cv.b, ce.b], [cs.b])
        SC = T(af(16 * 128).rearrange("p (k m) -> p k m", k=16))
        for k in range(16):
            P.dve(lambda e, k=k: e.tensor_scalar(SC.ap[:, k, :], ones, cs.ap[:, k:k + 1], None, ALU.mult),
                  [cs.b, bCST], [SC.b])
        SC1t = T(af(D))
        SC2t = T(af(D))
        CSC1t = T(af(D))
        SMALLt = T(af(SMW))
        SMALL = SMALLt.ap
        wms = rot(2, af, 8 * 512, lambda a: a.rearrange("p (k n) -> p k n", k=8))
        bms = rot(2, af, 512)
        smr = T(af(SMW))
        P.dma("sp", lambda e: e.dma_start(out=smr.ap[0:1, :], in_=small[:, :]), [], [smr.b])
        for j in range(12):
            wm = wms[j % 2]
            bm = bms[j % 2]
            P.dma("sp", lambda e, wm=wm, j=j: e.dma_start(
                out=wm.ap, in_=w_mod[:, j * 512:(j + 1) * 512].rearrange("(k p) n -> p k n", p=128)), [], [wm.b])
            P.dma("sp", lambda e, bm=bm, j=j: e.dma_start(out=bm.ap[0:1, :], in_=b_mod[:, j * 512:(j + 1) * 512]),
                  [], [bm.b])
            for which in range(2 if j < 4 else 1):
                ps = PSF[(j * 2 + which) % 4]
                for k in range(8):
                    P.pe(lambda e, ps=ps, wm=wm, k=k, which=which: e.matmul(
                        ps.ap[:, :], SC.ap[:, which * 8 + k, :], wm.ap[:, k, :], start=(k == 0), stop=False),
                        [SC.b, wm.b], [ps.b])
                P.pe(lambda e, ps=ps, bm=bm: e.matmul(ps.ap[:, :], ones[0:1, :], bm.ap[0:1, :], start=False, stop=True),
                     [bCST, bm.b], [ps.b])
                hs = slice((j % 2) * 512, (j % 2) * 512 + 512)
                if which == 0:
                    dst = (B1p, SC1t.ap, G1p, B2p, SC2t.ap, G2p)[j // 2][:, hs]
                else:
                    dst = (CB1p, CSC1t.ap)[j // 2][:, hs]
                P.act(lambda e, ps=ps, dst=dst: e.activation(dst, ps.ap[:, :], AF.Copy), [ps.b], [bMOD])
        for j in range(8):
            n = min(512, SMW - j * 512)
            ps = PSF[4 + j % 2]
            P.pe(lambda e, ps=ps, j=j, n=n: e.matmul(ps.ap[:, 0:n], ones[0:1, :], smr.ap[0:1, j * 512:j * 512 + n],
                                                    start=True, stop=True), [bCST, smr.b], [ps.b])
            P.act(lambda e, ps=ps, j=j, n=n: e.activation(SMALL[:, j * 512:j * 512 + n], ps.ap[:, 0:n], AF.Copy),
                  [ps.b], [bSM])
        tmp = T(af(D))
        P.dve(lambda e: e.tensor_copy(MN[:], SMALL[:, O_MN:O_MN + D]), [bSM], [bK])
        P.dve(lambda e: e.tensor_copy(BG[:], SMALL[:, O_BG:O_BG + 16]), [bSM], [bK])
        for (dst, sc, gn) in ((A1, SC1t.ap, SMALL[:, O_N1:O_N1 + D]),
                              (CA1, CSC1t.ap, SMALL[:, O_N1:O_N1 + D]),
                              (A2, SC2t.ap, SMALL[:, O_N2:O_N2 + D])):
            P.dve(lambda e, sc=sc: e.tensor_scalar(tmp.ap, sc, 1.0, None, ALU.add), [bMOD], [tmp.b])
            P.dve(lambda e, dst=dst, gn=gn: e.tensor_tensor(dst[:], tmp.ap, gn, ALU.mult), [tmp.b, bSM], [bK])
        for g in range(8):
            P.dve(lambda e, g=g: e.tensor_scalar(QG[:, g * 64:(g + 1) * 64], SMALL[:, O_QN:O_QN + 64], 0.125, None,
                                                 ALU.mult), [bSM], [bK])
            P.dve(lambda e, g=g: e.tensor_copy(KG[:, g * 64:(g + 1) * 64], SMALL[:, O_KN:O_KN + 64]), [bSM], [bK])
        P.dve(lambda e: e.tensor_scalar(G128[:], SMALL[:, O_DN:O_DN + 128], 0.8, None, ALU.mult), [bSM], [bK])
        P.dve(lambda e: e.tensor_scalar(NCB[:], CONV[:, :, 5], -1.0, None, ALU.mult), [bK], [bK])
        lt = T(af(128))
        ls = T(af(2))
        P.dve(lambda e: e.tensor_tensor(lt.ap[:, 0:64], SMALL[:, O_LV:O_LV + 64], SMALL[:, O_LV + 64:O_LV + 128],
                                        ALU.mult), [bSM], [lt.b])
        P.dve(lambda e: e.tensor_tensor(lt.ap[:, 64:128], SMALL[:, O_LV + 128:O_LV + 192],
                                        SMALL[:, O_LV + 192:O_LV + 256], ALU.mult), [lt.b, bSM], [lt.b])
        P.dve(lambda e: e.tensor_reduce(ls.ap, lt.ap.rearrange("p (a f) -> p a f", a=2), AX.X, ALU.add), [lt.b], [ls.b])
        P.act(lambda e: e.activation(ls.ap, ls.ap, AF.Exp), [ls.b], [ls.b])
        P.dve(lambda e: e.tensor_tensor(NLAM[:], ls.ap[:, 1:2], ls.ap[:, 0:1], ALU.subtract), [ls.b], [bK])
        P.dve(lambda e: e.tensor_scalar(NLAM[:], NLAM[:], -0.2, None, ALU.add), [bK], [bK])
        barrier()

        def rstd_chain(ss, width, nfeat):
            P.dve(lambda e: e.tensor_scalar(ss.ap, ss.ap, 1.0 / nfeat, EPS, ALU.mult, ALU.add), [ss.b], [ss.b])
            P.act(lambda e: e.activation(ss.ap, ss.ap, AF.Ln), [ss.b], [ss.b])
            P.act(lambda e: e.activation(ss.ap, ss.ap, AF.Exp, scale=-0.5), [ss.b], [ss.b])

        def mod_norm_T(xt, A, Bv, consts_b, sq, ss, t1, xn, xnT_dst, xnT_b):
            P.act(lambda e: e.activation(sq.ap, xt.ap, AF.Square, accum_out=ss.ap), [xt.b], [sq.b, ss.b])
            yield
            P.dve(lambda e: e.tensor_scalar(ss.ap, ss.ap, 1.0 / D, EPS, ALU.mult, ALU.add), [ss.b], [ss.b])
            yield
            P.act(lambda e: e.activation(ss.ap, ss.ap, AF.Ln), [ss.b], [ss.b])
            P.act(lambda e: e.activation(ss.ap, ss.ap, AF.Exp, scale=-0.5), [ss.b], [ss.b])
            yield
            P.dve(lambda e: e.scalar_tensor_tensor(t1.ap, xt.ap, ss.ap[:, 0:1], A, ALU.mult, ALU.mult),
                  [xt.b, ss.b] + consts_b, [t1.b])
            yield
            P.pool(lambda e: e.tensor_tensor(xn.ap, t1.ap, Bv, ALU.add), [t1.b] + consts_b, [xn.b])
            yield
            for c in range(8):
                P.pe(lambda e, c=c: e.transpose(PSB.ap[:, c * 128:(c + 1) * 128], xn.ap[:, c * 128:(c + 1) * 128], IDB[:]),
                     [xn.b, bIDB], [PSB.b])
            P.act(lambda e: e.activation(xnT_dst, PSB.ap.rearrange("p (c t) -> p c t", c=8), AF.Copy),
                  [PSB.b], [xnT_b])

        def interleave(gens):
            gens = list(gens)
            while gens:
                nxt_ = []
                for g_ in gens:
                    try:
                        next(g_)
                        nxt_.append(g_)
                    except StopIteration:
                        pass
                gens = nxt_

        def sigmoid_act(dst, src, dst_b, src_b, scale=1.0, bias=None, bias_b=()):
            if bias is None:
                P.act(lambda e: e.activation(dst, src, AF.Exp, scale=-scale), [src_b], [dst_b])
            else:
                P.act(lambda e: e.activation(dst, src, AF.Exp, bias=bias, scale=-scale), [src_b] + list(bias_b), [dst_b])
            P.act(lambda e: e.activation(dst, dst, AF.Ln, bias=1.0), [dst_b], [dst_b])
            P.act(lambda e: e.activation(dst, dst, AF.Exp, scale=-1.0), [dst_b], [dst_b])

        def load_w_cast(src_ap, dst_ap, dst_b):
            P.dma("pool", lambda e: e.dma_start(out=dst_ap, in_=src_ap), [], [dst_b])

        if True:
            reset_arena()
            v8 = lambda a: a.rearrange("p (k n) -> p k n", k=8)
            xnT = T(ab(8 * 2048).rearrange("p (k t) -> p k t", k=8))
            wbfs = rot(2, ab, 8 * 512, v8)
            xts = rot(4, af, D)
            sqs = rot(3, af, D)
            t1s = rot(3, af, D)
            sss = rot(3, af, 1)
            xns = rot(3, ab, D)
            w512 = rot(12, af, 512)
            s8 = rot(6, af, 8)
            ropes = rot(6, af, 128)
            ob512 = rot(6, ab, 512)
            obT = rot(4, ab, 512, lambda a: a.rearrange("p (h t) -> p h t", h=4))
            fst = rot(2, af, 4 * 512, lambda a: a.rearrange("p (c t) -> p c t", c=4))
            g16 = rot(6, af, 16)
            zt = T(af(8))
            ctr = {"w": 0, "x": 0, "w512": 0, "ob": 0, "obT": 0, "fst": 0, "ps": 0, "s8": 0, "rope": 0, "g16": 0}

            def nxt(lst, key):
                i = ctr[key]
                ctr[key] += 1
                return lst[i % len(lst)]

            P.pool(lambda e: e.memset(zt.ap, 0.0), [], [zt.b])
            for (c0, w) in ((0, 2), (258, 4), (8454, 2)):
                for cc in range(8):
                    P.dma("pool", lambda e, c0=c0, w=w, cc=cc: e.dma_start(
                        out=PT[cc * 128:(cc + 1) * 128, c0:c0 + w], in_=zt.ap[:, 0:w]), [zt.b], [])

            blocks = [(0, 2, "ctx"), (2, 16, "own"), (18, 16, "own"), (34, 16, "oth"), (50, 16, "oth")]
            items = []
            for bi, (bt0, bnt, bkind) in enumerate(blocks):
                first = True
                for u in UNITS:
                    if u[4] and bkind != "own":
                        continue
                    items.append((bi, u, first))
                    first = False

            def issue_w(ii):
                (_, (un_, c0_, nc_, _k, _o), _f) = items[ii]
                wb_ = wbfs[ii % 2]
                load_w_cast(w_in[:, c0_:c0_ + nc_].rearrange("(k p) n -> p k n", p=128), wb_.ap[:, :, 0:nc_], wb_.b)

            issue_w(0)
            for ii, (bi, (uname, c0, ncols, kind, own_only), first) in enumerate(items):
                (bt0, bnt, bkind) = blocks[bi]
                own = bkind == "own"
                if first:
                    A, Bv = (CA1[:], CB1) if bkind == "ctx" else (A1[:], B1)
                    for t0_ in range(0, bnt, 3):
                        gens = []
                        for ti in range(t0_, min(bnt, t0_ + 3)):
                            tt = bt0 + ti
                            xt = nxt(xts, "x")
                            P.dma("sp", lambda e, xt=xt, tt=tt: e.dma_start(out=xt.ap, in_=xa[tt * 128:(tt + 1) * 128, :]),
                                  [], [xt.b])
                            gens.append(mod_norm_T(xt, A, Bv, [bK, bMOD], sqs[ti % 3], sss[ti % 3], t1s[ti % 3], xns[ti % 3],
                                                   xnT.ap[:, :, ti * 128:(ti + 1) * 128], xnT.b))
                        interleave(gens)
                wbf = wbfs[ii % 2]
                if ii + 1 < len(items):
                    issue_w(ii + 1)
                if True:
                    if kind in ("fm_qk", "fm_sig"):
                        ngroups = max(1, bnt // 4)
                        gn = min(512, bnt * 128)
                        for g in range(ngroups):
                            st = nxt(fst, "fst")
                            for cc in range(4):
                                ps = PSF[ctr["ps"] % 6]
                                ctr["ps"] += 1
                                for k in range(8):
                                    P.pe(lambda e, ps=ps, wbf=wbf, k=k, cc=cc, g=g, gn=gn: e.matmul(
                                        ps.ap[:, 0:gn], wbf.ap[:, k, cc * 128:(cc + 1) * 128],
                                        xnT.ap[:, k, g * 512:g * 512 + gn], start=(k == 0), stop=(k == 7)),
                                        [wbf.b, xnT.b], [ps.b])
                                if kind == "fm_qk":
                                    P.act(lambda e, ps=ps, st=st, cc=cc, gn=gn: e.activation(
                                        st.ap[:, cc, 0:gn], ps.ap[:, 0:gn], AF.Copy), [ps.b], [st.b])
                                else:
                                    sigmoid_act(st.ap[:, cc, 0:gn], ps.ap[:, 0:gn], st.b, ps.b)
                            tok0 = bt0 * 128 + g * 512
                            if kind == "fm_qk":
                                pc = pt_col(tok0)
                                P.dma("pool", lambda e, st=st, c0=c0, pc=pc, gn=gn: e.dma_start(
                                    out=PT[c0:c0 + 512, pc:pc + gn].rearrange("(c p) t -> p c t", p=128),
                                    in_=st.ap[:, :, 0:gn]), [st.b], [])
                            else:
                                dstT = SGA if uname.startswith("ga") else SGB
                                r0 = 512 * int(uname[2])
                                oc = tok0 - 256
                                P.dma("pool", lambda e, st=st, dstT=dstT, r0=r0, oc=oc: e.dma_start(
                                    out=dstT[r0:r0 + 512, oc:oc + 512].rearrange("(c p) t -> p c t", p=128),
                                    in_=st.ap), [st.b], [])
                        continue
                    def post_tile(ps, tt, ti):
                        tok0 = tt * 128
                        half = int(uname[2]) if kind != "mg" else 0
                        if kind in ("mv", "dv"):
                            ob = nxt(ob512, "ob")
                            P.act(lambda e: e.activation(ob.ap, ps.ap[:, :], AF.Copy), [ps.b], [ob.b])
                            yield
                            dst = VM if kind == "mv" else VD
                            P.dma("pool", lambda e: e.dma_start(
                                out=dst[tok0:tok0 + 128, half * 512:(half + 1) * 512], in_=ob.ap), [ob.b], [])
                        elif kind == "mo":
                            w = nxt(w512, "w512")
                            sigmoid_act(w.ap, ps.ap[:, :], w.b, ps.b)
                            yield
                            oc = tok0 - 256
                            P.dma("pool", lambda e: e.dma_start(
                                out=SMO[oc:oc + 128, half * 512:(half + 1) * 512], in_=w.ap), [w.b], [])
                        elif kind == "mg":
                            gg = nxt(g16, "g16")
                            P.dve(lambda e: e.tensor_tensor(gg.ap, ps.ap[:, 0:16], BG[:], ALU.add), [ps.b, bK], [gg.b])
                            yield
                            P.dma("pool", lambda e: e.dma_start(out=GT[tok0:tok0 + 128, :], in_=gg.ap), [gg.b], [])
                        elif kind in ("dq", "dk"):
                            gains = QG if kind == "dq" else KG
                            sq = nxt(w512, "w512")
                            qn = nxt(w512, "w512")
                            t2 = nxt(w512, "w512")
                            t3 = nxt(w512, "w512")
                            st8 = nxt(s8, "s8")
                            rp = nxt(ropes, "rope")
                            P.dma("sp", lambda e: e.dma_start(out=rp.ap, in_=rope[tok0:tok0 + 128, :]), [], [rp.b])
                            P.act(lambda e: e.activation(sq.ap, ps.ap[:, :], AF.Square), [ps.b], [sq.b])
                            yield
                            P.dve(lambda e: e.tensor_reduce(st8.ap, sq.ap.rearrange("p (g f) -> p g f", g=8), AX.X, ALU.add),
                                  [sq.b], [st8.b])
                            P.dve(lambda e: e.tensor_scalar(st8.ap, st8.ap, 1.0 / 64, EPS, ALU.mult, ALU.add), [st8.b], [st8.b])
                            yield
                            P.act(lambda e: e.activation(st8.ap, st8.ap, AF.Ln), [st8.b], [st8.b])
                            P.act(lambda e: e.activation(st8.ap, st8.ap, AF.Exp, scale=-0.5), [st8.b], [st8.b])
                            yield
                            P.dve(lambda e: e.tensor_tensor(
                                qn.ap.rearrange("p (g f) -> p g f", g=8), ps.ap[:, :].rearrange("p (g f) -> p g f", g=8),
                                st8.ap.unsqueeze(2).to_broadcast([128, 8, 64]), ALU.mult), [ps.b, st8.b], [qn.b])
                            yield
                            P.pool(lambda e: e.tensor_tensor(qn.ap, qn.ap, gains[:], ALU.mult), [qn.b, bK], [qn.b])
                            yield
                            qv = qn.ap.rearrange("p (g a s f) -> p g a s f", g=8, a=2, s=2)
                            tv = t2.ap.rearrange("p (g a s f) -> p g a s f", g=8, a=2, s=2)
                            sv = rp.ap[:, 64:128].rearrange("p (a s f) -> p a s f", a=2, s=2)
                            for s_ in range(2):
                                P.pool(lambda e, s_=s_: e.tensor_tensor(
                                    tv[:, :, :, s_, :], qv[:, :, :, 1 - s_, :],
                                    sv[:, :, s_, :].unsqueeze(1).to_broadcast([128, 8, 2, 16]), ALU.mult),
                                    [qn.b, rp.b], [t2.b])
                            P.dve(lambda e: e.tensor_tensor(
                                t3.ap.rearrange("p (g f) -> p g f", g=8), qn.ap.rearrange("p (g f) -> p g f", g=8),
                                rp.ap[:, 0:64].unsqueeze(1).to_broadcast([128, 8, 64]), ALU.mult), [qn.b, rp.b], [t3.b])
                            yield
                            ob = nxt(ob512, "ob")
                            P.dve(lambda e: e.tensor_tensor(ob.ap, t3.ap, t2.ap, ALU.add), [t3.b, t2.b], [ob.b])
                            yield
                            for hh in range(4):
                                P.pe(lambda e, hh=hh: e.transpose(PSB.ap[:, hh * 128:(hh + 1) * 128],
                                                                  ob.ap[:, hh * 128:(hh + 1) * 128], IDB[:]),
                                     [ob.b, bIDB], [PSB.b])
                            oT = nxt(obT, "obT")
                            P.act(lambda e: e.activation(oT.ap, PSB.ap[:, 0:512].rearrange("p (h t) -> p h t", h=4), AF.Copy),
                                  [PSB.b], [oT.b])
                            yield
                            if kind == "dq":
                                oc = tok0 - 256
                                P.dma("pool", lambda e: e.dma_start(
                                    out=QTD[half * 4:(half + 1) * 4, :, oc:oc + 128].rearrange("h p t -> p h t"),
                                    in_=oT.ap), [oT.b], [])
                            else:
                                P.dma("pool", lambda e: e.dma_start(
                                    out=KTD[half * 4:(half + 1) * 4, :, tok0:tok0 + 128].rearrange("h p t -> p h t"),
                                    in_=oT.ap), [oT.b], [])

                    GI = 3
                    for t0_ in range(0, bnt, GI):
                        gens = []
                        for ti in range(t0_, min(bnt, t0_ + GI)):
                            ps = PSF[ctr["ps"] % 6]
                            ctr["ps"] += 1
                            for k in range(8):
                                P.pe(lambda e, ps=ps, k=k, ti=ti: e.matmul(
                                    ps.ap[:, 0:ncols], xnT.ap[:, k, ti * 128:(ti + 1) * 128], wbf.ap[:, k, 0:ncols],
                                    start=(k == 0), stop=(k == 7)), [wbf.b, xnT.b], [ps.b])
                            gens.append(post_tile(ps, bt0 + ti, ti))
                        interleave(gens)
            barrier()
        if stop_after == 1:
            P.emit(nc, es)
            return nc

        if True:
            reset_arena()
            wins = rot(2, af, 4 * 516, lambda a: a.rearrange("p (c t) -> p c t", c=4))
            accs = rot(4, af, 512)
            es_ = rot(4, af, 512)
            okT = rot(4, ab, 512)
            LNS = T(af(1))
            P.pool(lambda e: e.memset(LNS.ap, math.log(128.0 ** -0.5)), [], [LNS.b])
            kms = rot(2, ab, 2048, lambda a: a.rearrange("p (t c) -> p t c", t=4))
            groups = [(0, 256)] + [(256 + g * 512, 512) for g in range(16)]
            ci = 0
            for gi, (tok0, gn) in enumerate(groups):
                own = 256 <= tok0 < 256 + OWN
                pc = pt_col(tok0)
                km = kms[gi % 2]
                for qk in ((0, 1) if own else (1,)):
                    win = wins[ci % 2]
                    ci += 1
                    P.dma("sp", lambda e, win=win, qk=qk, pc=pc, gn=gn: e.dma_start(
                        out=win.ap[:, :, 0:gn + 4],
                        in_=PT[qk * 512:(qk + 1) * 512, pc - 2:pc + gn + 2].rearrange("(c p) t -> p c t", p=128)),
                        [], [win.b])
                    def conv_chain(hc):
                        cc = qk * 4 + hc
                        acc = accs[hc]
                        ee = es_[hc]
                        ok = okT[hc]
                        P.dve(lambda e: e.tensor_scalar(acc.ap[:, 0:gn], win.ap[:, hc, 0:gn], CONV[:, cc, 0:1], None, ALU.mult),
                              [win.b, bK], [acc.b])
                        yield
                        for k in range(1, 5):
                            P.dve(lambda e, k=k: e.scalar_tensor_tensor(
                                acc.ap[:, 0:gn], win.ap[:, hc, k:k + gn], CONV[:, cc, k:k + 1], acc.ap[:, 0:gn],
                                ALU.mult, ALU.add), [win.b, bK, acc.b], [acc.b])
                            yield
                        P.act(lambda e: e.activation(ee.ap[:, 0:gn], acc.ap[:, 0:gn], AF.Exp, bias=NCB[:, cc:cc + 1], scale=-1.0),
                              [acc.b, bK], [ee.b])
                        P.act(lambda e: e.activation(ee.ap[:, 0:gn], ee.ap[:, 0:gn], AF.Ln, bias=1.0), [ee.b], [ee.b])
                        if qk == 0:
                            P.act(lambda e: e.activation(ee.ap[:, 0:gn], ee.ap[:, 0:gn], AF.Exp, bias=LNS.ap[:, 0:1], scale=-1.0),
                                  [ee.b, LNS.b], [ee.b])
                        else:
                            P.act(lambda e: e.activation(ee.ap[:, 0:gn], ee.ap[:, 0:gn], AF.Exp, scale=-1.0), [ee.b], [ee.b])
                        yield
                        P.dve(lambda e: e.scalar_tensor_tensor(
                            ok.ap[:, 0:gn], acc.ap[:, 0:gn], CONV[:, cc, 5:6], ee.ap[:, 0:gn], ALU.add, ALU.mult),
                            [acc.b, ee.b, bK], [ok.b])
                        yield
                        if qk == 0:
                            oc = tok0 - 256
                            P.dma("pool", lambda e: e.dma_start(out=QTM[hc, :, oc:oc + 512], in_=ok.ap), [ok.b], [])
                        else:
                            P.dma("pool", lambda e: e.dma_start(out=KTM[hc, :, tok0:tok0 + gn], in_=ok.ap[:, 0:gn]), [ok.b], [])
                            nt = gn // 128
                            for t in range(nt):
                                P.pe(lambda e, t=t: e.transpose(
                                    PSB.ap[:, t * 128:(t + 1) * 128], ok.ap[:, t * 128:(t + 1) * 128], IDB[:]),
                                    [ok.b, bIDB], [PSB.b])
                            P.act(lambda e: e.activation(
                                km.ap[:, 0:nt, hc * 128:(hc + 1) * 128],
                                PSB.ap[:, 0:nt * 128].rearrange("p (t c) -> p t c", t=nt), AF.Copy), [PSB.b], [km.b])

                    interleave([conv_chain(hc) for hc in range(4)])
                nt = gn // 128
                P.dma("pool", lambda e, km=km, tok0=tok0, nt=nt: e.dma_start(
                    out=KM[tok0:tok0 + nt * 128, :].rearrange("(t p) c -> p t c", p=128), in_=km.ap[:, 0:nt, :]),
                    [km.b], [])
            barrier()
        if stop_after == 2:
            P.emit(nc, es)
            return nc

        LFm = CST[:, CLF:CLF + 128]
        UFm = CST[:, CUF:CUF + 128]
        LRm = CST[:, CLR:CLR + 128]
        URm = CST[:, CUR:CUR + 128]
        NFm = CST[:, CNF:CNF + 128]
        NRm = CST[:, CNR:CNR + 128]

        def mlstm_dir(direction):
            reset_arena()
            isF = direction == "F"
            gofs = 0 if isF else 8
            Lm, Um, Nm = (LFm, UFm, NFm) if isF else (LRm, URm, NRm)
            Cst = T(af(4 * 257).rearrange("p (h e) -> p h e", h=4))
            Cbf = rot(2, ab, 4 * 257 + 4, lambda a: a[:, 0:4 * 257].rearrange("p (h e) -> p h e", h=4))
            gts = rot(2, af, 16)
            lfs = rot(2, af, 4)
            rhsE = rot(2, af, 512, lambda a: a.rearrange("p (h j) -> p h j", h=4))
            DTs = rot(2, af, 512, lambda a: a.rearrange("p (h j) -> p h j", h=4))
            AT = rot(4, ab, 128)
            numB = rot(4, af, 257)
            smalls = rot(2, af, 16)
            qTs = rot(2, ab, 512, lambda a: a.rearrange("p (h t) -> p h t", h=4))
            kTs = rot(2, ab, 512, lambda a: a.rearrange("p (h t) -> p h t", h=4))
            kms_ = rot(2, ab, 512)
            kws = rot(4, ab, 128)
            vxs = rot(2, ab, 4 * 258, lambda a: a[:, 0:4 * 257].rearrange("p (h e) -> p h e", h=4))
            numA = rot(4, af, 257)
            hts = rot(2, af, D)
            hfs = rot(2, af, D)
            smo = rot(2, af, D)
            dens = rot(4, af, 4)
            sq2 = rot(1, af, D)
            st4 = rot(2, af, 4)
            hab = rot(2, ab, D)
            haT = rot(2, ab, D, lambda a: a.rearrange("p (c t) -> p c t", c=8))
            for vx in vxs:
                P.pool(lambda e, vx=vx: e.memset(vx.ap[:, :, 256:257], 1.0), [], [vx.b])
            P.pool(lambda e: e.memset(Cst.ap, 0.0), [], [Cst.b])
            P.pool(lambda e: e.memset(Cbf[0].ap, 0.0), [], [Cbf[0].b])
            if isF:
                order = [(t, False) for t in (0, 1)] + [(t, True) for t in range(2, 34)]
            else:
                order = [(t, False) for t in (1, 0)] + [(t, False) for t in range(65, 33, -1)] + \
                        [(t, True) for t in range(33, 1, -1)]
            cx = {}

            def prologue(step):
                tt, outp = order[step]
                tok0 = tt * 128
                oc = tok0 - 256
                gt = gts[step % 2]
                lf = lfs[step % 2]
                sm = smalls[step % 2]
                kmt = kms_[step % 2]
                vx = vxs[step % 2]
                d = dict(tok0=tok0, oc=oc, gt=gt, lf=lf, sm=sm, kmt=kmt, vx=vx)
                P.dma("sp", lambda e: e.dma_start(out=gt.ap, in_=GT[tok0:tok0 + 128, :]), [], [gt.b])
                P.dma("sp", lambda e: e.dma_start(out=kmt.ap, in_=KM[tok0:tok0 + 128, :]), [], [kmt.b])
                P.dma("sp", lambda e: e.dma_start(
                    out=vx.ap[:, :, 0:256], in_=VM[tok0:tok0 + 128, :].rearrange("p (h e) -> p h e", h=4)), [], [vx.b])
                if outp:
                    qT = qTs[step % 2]
                    kT = kTs[step % 2]
                    d.update(qT=qT, kT=kT)
                    P.dma("sp", lambda e: e.dma_start(
                        out=qT.ap, in_=QTM[:, :, oc:oc + 128].rearrange("h p t -> p h t")), [], [qT.b])
                    P.dma("sp", lambda e: e.dma_start(
                        out=kT.ap, in_=KTM[:, :, tok0:tok0 + 128].rearrange("h p t -> p h t")), [], [kT.b])
                    d["ht"] = hts[step % 2]
                    if not isF:
                        hf = hfs[step % 2]
                        so = smo[step % 2]
                        d.update(hf=hf, so=so)
                        P.dma("sp", lambda e: e.dma_start(out=hf.ap, in_=HF[oc:oc + 128, :]), [], [hf.b])
                        P.dma("sp", lambda e: e.dma_start(out=so.ap, in_=SMO[oc:oc + 128, :]), [], [so.b])
                yield
                P.act(lambda e: e.activation(lf.ap, gt.ap[:, gofs + 4:gofs + 8], AF.Exp, scale=-1.0), [gt.b], [lf.b])
                P.act(lambda e: e.activation(lf.ap, lf.ap, AF.Ln, bias=1.0), [lf.b], [lf.b])
                yield
                P.dve(lambda e: e.tensor_scalar(lf.ap, lf.ap, -1.0, None, ALU.mult), [lf.b], [lf.b])
                yield
                pss = PSF[6]
                P.pe(lambda e: e.matmul(pss.ap[:, 0:4], Lm, lf.ap, start=True, stop=True), [bCST, lf.b], [pss.b])
                P.pe(lambda e: e.matmul(pss.ap[:, 4:8], Um, lf.ap, start=True, stop=True), [bCST, lf.b], [pss.b])
                P.pe(lambda e: e.matmul(pss.ap[:, 8:12], ones, lf.ap, start=True, stop=True), [bCST, lf.b], [pss.b])
                if outp:
                    rE = rhsE[step % 2]
                    DT = DTs[step % 2]
                    d["DT"] = DT
                    for h in range(4):
                        P.pool(lambda e, h=h: e.tensor_scalar(rE.ap[:, h, :], Lm, lf.ap[:, h:h + 1], None, ALU.mult),
                               [bCST, lf.b], [rE.b])
                yield
                P.dve(lambda e: e.tensor_copy(sm.ap[:, 0:12], pss.ap[:, 0:12]), [pss.b], [sm.b])
                P.dve(lambda e: e.tensor_tensor(sm.ap[:, 4:8], sm.ap[:, 4:8], gt.ap[:, gofs:gofs + 4], ALU.add),
                      [sm.b, gt.b], [sm.b])
                if outp:
                    pe_ = PSF[4]
                    P.pe(lambda e: e.matmul(pe_.ap[:, :], Um, rE.ap.rearrange("p h j -> p (h j)"), start=True, stop=False),
                         [bCST, rE.b], [pe_.b])
                    for h in range(4):
                        P.pe(lambda e, h=h: e.matmul(pe_.ap[:, h * 128:(h + 1) * 128], ident, Nm, start=False, stop=(h == 3)),
                             [bCST], [pe_.b])
                yield
                P.act(lambda e: e.activation(sm.ap[:, 0:12], sm.ap[:, 0:12], AF.Exp), [sm.b], [sm.b])
                if outp:
                    for h in range(4):
                        P.act(lambda e, h=h: e.activation(
                            DT.ap[:, h, :], pe_.ap[:, h * 128:(h + 1) * 128], AF.Exp, bias=gt.ap[:, gofs + h:gofs + h + 1]),
                            [pe_.b, gt.b], [DT.b])
                cx[step] = d

            for _ in prologue(0):
                pass
            for step, (tt, outp) in enumerate(order):
                d = cx[step]
                tok0, oc, gt, lf, sm, kmt, vx = d["tok0"], d["oc"], d["gt"], d["lf"], d["sm"], d["kmt"], d["vx"]
                qT, kT, DT, ht = d.get("qT"), d.get("kT"), d.get("DT"), d.get("ht")
                hf, so = d.get("hf"), d.get("so")
                cb_in = Cbf[step % 2]
                cb_out = Cbf[(step + 1) % 2]
                def head_chain(h):
                    X = PSF[(h % 2) * 2]
                    Y = PSF[(h % 2) * 2 + 1]
                    at = AT[h]
                    na = numA[h]
                    nb = numB[h]
                    dn = dens[h]
                    kw = kws[h]
                    if outp:
                        P.pe(lambda e: e.matmul(Y.ap[:, 0:257], qT.ap[:, h, :], cb_in.ap[:, h, :], start=True, stop=True),
                             [qT.b, cb_in.b], [Y.b])
                        P.pe(lambda e: e.matmul(X.ap[:, 260:388], kT.ap[:, h, :], qT.ap[:, h, :], start=True, stop=True),
                             [kT.b, qT.b], [X.b])
                        yield
                        P.act(lambda e: e.activation(nb.ap, Y.ap[:, 0:257], AF.Copy), [Y.b], [nb.b])
                        P.dve(lambda e: e.tensor_tensor(at.ap, X.ap[:, 260:388], DT.ap[:, h, :], ALU.mult), [X.b, DT.b], [at.b])
                        yield
                        P.pe(lambda e: e.matmul(X.ap[:, 0:257], at.ap, vx.ap[:, h, :], start=True, stop=True),
                             [at.b, vx.b], [X.b])
                        yield
                        P.dve(lambda e: e.scalar_tensor_tensor(
                            na.ap, nb.ap, sm.ap[:, h:h + 1], X.ap[:, 0:257], ALU.mult, ALU.add), [X.b, sm.b, nb.b], [na.b])
                        yield
                        P.dve(lambda e: e.tensor_scalar(dn.ap[:, 0:1], na.ap[:, 256:257], -1.0, None, ALU.mult), [na.b], [dn.b])
                        P.dve(lambda e: e.tensor_tensor(dn.ap[:, 1:2], dn.ap[:, 0:1], na.ap[:, 256:257], ALU.max),
                              [na.b, dn.b], [dn.b])
                        yield
                        P.dve(lambda e: e.tensor_scalar(dn.ap[:, 2:3], dn.ap[:, 1:2], 1.0, None, ALU.max), [dn.b], [dn.b])
                        P.dve(lambda e: e.reciprocal(dn.ap[:, 3:4], dn.ap[:, 2:3]), [dn.b], [dn.b])
                        yield
                        if isF:
                            P.dve(lambda e: e.tensor_scalar(
                                ht.ap[:, h * 256:(h + 1) * 256], na.ap[:, 0:256], dn.ap[:, 3:4], None, ALU.mult),
                                [na.b, dn.b], [ht.b])
                        else:
                            P.dve(lambda e: e.scalar_tensor_tensor(
                                ht.ap[:, h * 256:(h + 1) * 256], na.ap[:, 0:256], dn.ap[:, 3:4],
                                hf.ap[:, h * 256:(h + 1) * 256], ALU.mult, ALU.add), [na.b, dn.b, hf.b], [ht.b])
                    P.pool(lambda e: e.tensor_scalar(
                        kw.ap, kmt.ap[:, h * 128:(h + 1) * 128], sm.ap[:, 4 + h:5 + h], None, ALU.mult),
                        [kmt.b, sm.b], [kw.b])
                    yield
                    P.pe(lambda e: e.matmul(Y.ap[:, 0:257], kw.ap, vx.ap[:, h, :], start=True, stop=True),
                         [kw.b, vx.b], [Y.b])
                    yield
                    P.dve(lambda e: e.scalar_tensor_tensor(
                        Cst.ap[:, h, :], Cst.ap[:, h, :], sm.ap[:, 8 + h:9 + h], Y.ap[:, 0:257], ALU.mult, ALU.add),
                        [Cst.b, sm.b, Y.b], [Cst.b])

                pro = prologue(step + 1) if step + 1 < len(order) else iter(())
                interleave([head_chain(0), head_chain(1), pro])
                interleave([head_chain(2), head_chain(3)])
                P.act(lambda e, cb_out=cb_out: e.activation(cb_out.ap, Cst.ap, AF.Copy), [Cst.b], [cb_out.b])
                if outp and isF:
                    P.dma("pool", lambda e, ht=ht, oc=oc: e.dma_start(out=HF[oc:oc + 128, :], in_=ht.ap), [ht.b], [])
                if outp and not isF:
                    s4 = st4[step % 2]
                    P.act(lambda e, ht=ht: e.activation(sq2[0].ap, ht.ap, AF.Square), [ht.b], [sq2[0].b])
                    P.dve(lambda e, s4=s4: e.tensor_reduce(s4.ap, sq2[0].ap.rearrange("p (h f) -> p h f", h=4), AX.X, ALU.add),
                          [sq2[0].b], [s4.b])
                    rstd_chain(s4, 4, 256)
                    for h in range(4):
                        P.dve(lambda e, ht=ht, s4=s4, h=h: e.scalar_tensor_tensor(
                            ht.ap[:, h * 256:(h + 1) * 256], ht.ap[:, h * 256:(h + 1) * 256], s4.ap[:, h:h + 1],
                            MN[:, h * 256:(h + 1) * 256], ALU.mult, ALU.mult), [ht.b, s4.b, bK], [ht.b])
                    hb = hab[step % 2]
                    P.pool(lambda e, hb=hb, ht=ht, so=so: e.tensor_tensor(hb.ap, ht.ap, so.ap, ALU.mult), [ht.b, so.b], [hb.b])
                    for c in range(8):
                        P.pe(lambda e, hb=hb, c=c: e.transpose(PSB.ap[:, c * 128:(c + 1) * 128], hb.ap[:, c * 128:(c + 1) * 128],
                                                               IDB[:]), [hb.b, bIDB], [PSB.b])
                    hT = haT[step % 2]
                    P.act(lambda e, hT=hT: e.activation(hT.ap, PSB.ap.rearrange("p (c t) -> p c t", c=8), AF.Copy),
                          [PSB.b], [hT.b])
                    P.dma("pool", lambda e, hT=hT, oc=oc: e.dma_start(
                        out=HAT[:, :, oc:oc + 128].rearrange("c p t -> p c t"), in_=hT.ap), [hT.b], [])
            barrier()

        mlstm_dir("F")
        if stop_after == 3:
            P.emit(nc, es)
            return nc
        mlstm_dir("R")
        if stop_after == 4:
            P.emit(nc, es)
            return nc

        if True:
            reset_arena()
            NKC = 66
            KTs = rot(2, ab, TOK)
            Vs = rot(2, ab, NKC * 130, lambda a: a[:, 0:NKC * 129].rearrange("p (c e) -> p c e", c=NKC))
            QTs = rot(2, ab, OWN)
            PTs = rot(4, ab, 1024)
            obufs = rot(2, af, OWN, lambda a: a.rearrange("p (t e) -> p t e", t=32))
            sst = rot(2, af, 32)
            rr = rot(3, af, 4)
            tA = rot(2, af, 128)
            sqj = rot(1, af, 128)
            hbb = rot(2, ab, 128)
            hbT = rot(2, ab, 512)
            accS = rot(2, af, 8 * 129, lambda a: a.rearrange("p (i e) -> p i e", i=8))
            for v in Vs:
                P.pool(lambda e, v=v: e.memset(v.ap[:, :, 128:129], 1.0), [], [v.b])
            accs = []
            for i in range(8):
                bk = PSF[4 + i // 3]
                accs.append((bk, (i % 3) * 129))
            SB2 = [(PS2[0], (PSF[0].b, PSF[1].b)), (PS2[1], (PSF[2].b, PSF[3].b))]
            heads = []
            for h in range(8):
                Kt = KTs[h % 2]
                Vt = Vs[h % 2]
                Qt = QTs[h % 2]
                heads.append((Kt, Vt, Qt, obufs[h % 2], sst[h % 2]))
            steps = [(h, qb, kc) for h in range(8) for qb in range(8) for kc in range(NKC)]

            def emit_loads(h):
                Kt, Vt, Qt, _, _ = heads[h]
                P.dma("sp", lambda e: e.dma_start(out=Kt.ap, in_=KTD[h, :, :]), [], [Kt.b])
                P.dma("sp", lambda e: e.dma_start(out=Qt.ap, in_=QTD[h, :, :]), [], [Qt.b])
                P.dma("sp", lambda e: e.dma_start(
                    out=Vt.ap[:, :, 0:128], in_=VD[:, h * 128:(h + 1) * 128].rearrange("(c p) e -> p c e", p=128)),
                    [], [Vt.b])

            def emit_qk(i):
                h, qb, kc = steps[i]
                Kt, Vt, Qt, _, _ = heads[h]
                ps2, bb = SB2[i % 2]
                P.pe(lambda e: e.matmul(ps2[:, 0:512], Kt.ap[0:64, kc * 128:(kc + 1) * 128],
                                        Qt.ap[0:64, qb * 512:(qb + 1) * 512], start=True, stop=True),
                     [Kt.b, Qt.b], [bb[0]])
                P.pe(lambda e: e.matmul(ps2[:, 512:1024], Kt.ap[64:128, kc * 128:(kc + 1) * 128],
                                        Qt.ap[64:128, qb * 512:(qb + 1) * 512], start=True, stop=True),
                     [Kt.b, Qt.b], [bb[1]])

            def emit_exp(i):
                h, qb, kc = steps[i]
                ps2, bb = SB2[i % 2]
                pt = PTs[i % 4]
                P.act(lambda e: e.activation(pt.ap, ps2[:, :], AF.Exp), [bb[0], bb[1]], [pt.b])

            def emit_pv(i):
                h, qb, kc = steps[i]
                Kt, Vt, Qt, ob, ssq = heads[h]
                pt = PTs[i % 4]
                for m in range(2):
                    for qs in range(4):
                        bk, o0 = accs[m * 4 + qs]
                        P.pe(lambda e, bk=bk, o0=o0, m=m, qs=qs: e.matmul(
                            bk.ap[:, o0:o0 + 129], pt.ap[:, m * 512 + qs * 128:m * 512 + (qs + 1) * 128],
                            Vt.ap[:, kc, :], start=(kc == 0), stop=(kc == NKC - 1)), [pt.b, Vt.b], [bk.b])
                if kc != NKC - 1:
                    return
                aS = accS[(h * 8 + qb) % 2]
                for bi_ in range(3):
                    n_ = 3 if bi_ < 2 else 2
                    bk = PSF[4 + bi_]
                    P.dve(lambda e, bk=bk, bi_=bi_, n_=n_: e.tensor_copy(
                        aS.ap[:, bi_ * 3:bi_ * 3 + n_, :], bk.ap[:, 0:n_ * 129].rearrange("p (i e) -> p i e", i=n_)),
                        [bk.b], [aS.b])
                for qs in range(4):
                    qt = qb * 4 + qs
                    r = rr[qt % 3]
                    ta = tA[qt % 2]
                    P.dve(lambda e, r=r, qs=qs: e.reciprocal(r.ap[:, 0:1], aS.ap[:, qs, 128:129]), [aS.b], [r.b])
                    P.dve(lambda e, r=r, qs=qs: e.reciprocal(r.ap[:, 1:2], aS.ap[:, 4 + qs, 128:129]), [aS.b, r.b], [r.b])
                    P.dve(lambda e, r=r: e.tensor_tensor(r.ap[:, 2:3], r.ap[:, 1:2], NLAM[:], ALU.mult), [r.b, bK], [r.b])
                    P.pool(lambda e, r=r, ta=ta, qs=qs: e.tensor_scalar(ta.ap, aS.ap[:, qs, 0:128], r.ap[:, 0:1], None,
                                                                        ALU.mult), [aS.b, r.b], [ta.b])
                    P.dve(lambda e, r=r, ta=ta, qs=qs, qt=qt: e.scalar_tensor_tensor(
                        ob.ap[:, qt, :], aS.ap[:, 4 + qs, 0:128], r.ap[:, 2:3], ta.ap, ALU.mult, ALU.add),
                        [aS.b, r.b, ta.b], [ob.b])
                    P.pool(lambda e, qt=qt: e.tensor_tensor(sqj[0].ap, ob.ap[:, qt, :], ob.ap[:, qt, :], ALU.mult),
                           [ob.b], [sqj[0].b])
                    P.dve(lambda e, qt=qt: e.tensor_reduce(ssq.ap[:, qt:qt + 1], sqj[0].ap, AX.X, ALU.add),
                          [sqj[0].b], [ssq.b])
                if qb != 7:
                    return
                rstd_chain(ssq, 32, 128)
                for g4 in range(8):
                    hT = hbT[g4 % 2]
                    for j in range(4):
                        qt = g4 * 4 + j
                        hb = hbb[qt % 2]
                        P.dve(lambda e, hb=hb, qt=qt: e.scalar_tensor_tensor(
                            hb.ap, ob.ap[:, qt, :], ssq.ap[:, qt:qt + 1], G128[:], ALU.mult, ALU.mult),
                            [ob.b, ssq.b, bK], [hb.b])
                        P.pe(lambda e, hb=hb, j=j: e.transpose(PSB.ap[:, j * 128:(j + 1) * 128], hb.ap, IDB[:]),
                             [hb.b, bIDB], [PSB.b])
                    P.dve(lambda e, hT=hT: e.tensor_copy(hT.ap, PSB.ap[:, 0:512]), [PSB.b], [hT.b])
                    P.dma("pool", lambda e, hT=hT, g4=g4: e.dma_start(out=HBT[h, :, g4 * 512:(g4 + 1) * 512], in_=hT.ap),
                          [hT.b], [])

            emit_loads(0)
            emit_qk(0)
            NS = len(steps)
            for i in range(NS):
                emit_exp(i)
                if i + 1 < NS:
                    emit_qk(i + 1)
                if i >= 1:
                    emit_pv(i - 1)
                    h_, qb_, kc_ = steps[i - 1]
                    if qb_ == 0 and kc_ == 0 and h_ + 1 < 8:
                        emit_loads(h_ + 1)
            emit_pv(NS - 1)
            barrier()
        if stop_after == 5:
            P.emit(nc, es)
            return nc

        if True:
            reset_arena()
            v8 = lambda a: a.rearrange("p (k n) -> p k n", k=8)
            WA = T(v8(ab(8 * D)))
            WB = T(v8(ab(8 * D)))
            WO = T(v8(ab(8 * D)))
            for (W, src) in ((WA, w_a), (WB, w_b), (WO, w_o)):
                for hf_ in range(2):
                    load_w_cast(src[:, hf_ * 512:(hf_ + 1) * 512].rearrange("(k p) n -> p k n", p=128),
                                W.ap[:, :, hf_ * 512:(hf_ + 1) * 512], W.b)
            hAs = rot(1, ab, 8 * 512, v8)
            hBs = rot(1, ab, 8 * 512, v8)
            sgs = rot(4, af, 512)
            yTs = rot(1, ab, 8 * 512, v8)
            tms = rot(3, af, 512)
            xts = rot(2, af, D)
            x1s = rot(2, af, D)
            for g in range(8):
                hA = hAs[0]
                hB = hBs[0]
                yT = yTs[0]
                P.dma("sp", lambda e, hA=hA, g=g: e.dma_start(
                    out=hA.ap, in_=HAT[:, :, g * 512:(g + 1) * 512].rearrange("c p t -> p c t")), [], [hA.b])
                P.dma("sp", lambda e, hB=hB, g=g: e.dma_start(
                    out=hB.ap, in_=HBT[:, :, g * 512:(g + 1) * 512].rearrange("c p t -> p c t")), [], [hB.b])
                for fc in range(8):
                    sa = sgs[(fc * 2) % 4]
                    sb_ = sgs[(fc * 2 + 1) % 4]
                    P.dma("sp", lambda e, sa=sa, fc=fc, g=g: e.dma_start(
                        out=sa.ap, in_=SGA[fc * 128:(fc + 1) * 128, g * 512:(g + 1) * 512]), [], [sa.b])
                    P.dma("sp", lambda e, sb_=sb_, fc=fc, g=g: e.dma_start(
                        out=sb_.ap, in_=SGB[fc * 128:(fc + 1) * 128, g * 512:(g + 1) * 512]), [], [sb_.b])
                    pa = PSF[(fc % 2) * 2]
                    pb = PSF[(fc % 2) * 2 + 1]
                    for k in range(8):
                        P.pe(lambda e, pa=pa, hA=hA, k=k, fc=fc: e.matmul(
                            pa.ap[:, :], WA.ap[:, k, fc * 128:(fc + 1) * 128], hA.ap[:, k, :], start=(k == 0), stop=(k == 7)),
                            [WA.b, hA.b], [pa.b])
                    for k in range(8):
                        P.pe(lambda e, pb=pb, hB=hB, k=k, fc=fc: e.matmul(
                            pb.ap[:, :], WB.ap[:, k, fc * 128:(fc + 1) * 128], hB.ap[:, k, :], start=(k == 0), stop=(k == 7)),
                            [WB.b, hB.b], [pb.b])
                    tm = tms[fc % 3]
                    P.dve(lambda e, tm=tm, pa=pa, sa=sa: e.tensor_tensor(tm.ap, pa.ap[:, :], sa.ap, ALU.mult), [pa.b, sa.b], [tm.b])
                    P.dve(lambda e, sb_=sb_, pb=pb: e.tensor_tensor(sb_.ap, pb.ap[:, :], sb_.ap, ALU.mult), [pb.b, sb_.b], [sb_.b])
                    P.pool(lambda e, yT=yT, tm=tm, sb_=sb_, fc=fc: e.tensor_tensor(yT.ap[:, fc, :], tm.ap, sb_.ap, ALU.add),
                           [tm.b, sb_.b], [yT.b])
                for j in range(4):
                    tl = g * 4 + j
                    xt = xts[tl % 2]
                    x1 = x1s[tl % 2]
                    P.dma("sp", lambda e, xt=xt, tl=tl: e.dma_start(out=xt.ap, in_=xa[256 + tl * 128:256 + (tl + 1) * 128, :]),
                          [], [xt.b])
                    for nh in range(2):
                        pz = PSF[4 + nh]
                        for k in range(8):
                            P.pe(lambda e, pz=pz, yT=yT, k=k, j=j, nh=nh: e.matmul(
                                pz.ap[:, :], yT.ap[:, k, j * 128:(j + 1) * 128], WO.ap[:, k, nh * 512:(nh + 1) * 512],
                                start=(k == 0), stop=(k == 7)), [yT.b, WO.b], [pz.b])
                        P.dve(lambda e, pz=pz, x1=x1, nh=nh: e.tensor_tensor(
                            x1.ap[:, nh * 512:(nh + 1) * 512], pz.ap[:, :], G1[:, nh * 512:(nh + 1) * 512], ALU.mult),
                            [pz.b, bMOD], [x1.b])
                    P.pool(lambda e, x1=x1, xt=xt: e.tensor_tensor(x1.ap, x1.ap, xt.ap, ALU.add), [x1.b, xt.b], [x1.b])
                    P.dma("pool", lambda e, x1=x1, tl=tl: e.dma_start(out=X1[tl * 128:(tl + 1) * 128, :], in_=x1.ap), [x1.b], [])
            barrier()
        if stop_after == 6:
            P.emit(nc, es)
            return nc

        if True:
            reset_arena()
            W2 = T(ab(22 * D).rearrange("p (c n) -> p c n", c=22))
            for j in range(11):
                load_w_cast(w_f2[j * 256:(j + 1) * 256, :].rearrange("(c p) n -> p c n", p=128),
                            W2.ap[:, 2 * j:2 * j + 2, :], W2.b)
            xnT = T(ab(8 * 512).rearrange("p (k t) -> p k t", k=8))
            uT = T(ab(22 * 512).rearrange("p (c t) -> p c t", c=22))
            x1r = rot(4, af, D)
            sqs = rot(2, af, D)
            t1s = rot(2, af, D)
            sss = rot(4, af, 1)
            xns = rot(2, ab, D)
            wbf = rot(3, ab, 8 * 256, lambda a: a.rearrange("p (k n) -> p k n", k=8))
            ees = rot(3, af, 512)
            tts = rot(2, af, 512)
            outs = rot(2, af, D)
            for sg in range(8):
                gens = []
                for ti in range(4):
                    tl = sg * 4 + ti
                    xt = x1r[ti]
                    P.dma("sp", lambda e, xt=xt, tl=tl: e.dma_start(out=xt.ap, in_=X1[tl * 128:(tl + 1) * 128, :]), [], [xt.b])
                    gens.append(mod_norm_T(xt, A2[:], B2, [bK, bMOD], sqs[ti % 2], sss[ti], t1s[ti % 2], xns[ti % 2],
                                           xnT.ap[:, :, ti * 128:(ti + 1) * 128], xnT.b))
                    if len(gens) == 2:
                        interleave(gens)
                        gens = []
                if sg == 0:
                    load_w_cast(w_f1[0], wbf[0].ap, wbf[0].b)
                for j in range(22):
                    jj = sg * 22 + j
                    wb_ = wbf[jj % 3]
                    if jj + 1 < 8 * 22:
                        nb = wbf[(jj + 1) % 3]
                        load_w_cast(w_f1[(j + 1) % 22], nb.ap, nb.b)
                    pa = PSF[(j % 2) * 2]
                    pb = PSF[(j % 2) * 2 + 1]
                    for k in range(8):
                        P.pe(lambda e, pa=pa, wb_=wb_, k=k: e.matmul(
                            pa.ap[:, :], wb_.ap[:, k, 0:128], xnT.ap[:, k, :], start=(k == 0), stop=(k == 7)),
                            [wb_.b, xnT.b], [pa.b])
                    for k in range(8):
                        P.pe(lambda e, pb=pb, wb_=wb_, k=k: e.matmul(
                            pb.ap[:, :], wb_.ap[:, k, 128:256], xnT.ap[:, k, :], start=(k == 0), stop=(k == 7)),
                            [wb_.b, xnT.b], [pb.b])
                    ee = ees[j % 3]
                    tq = tts[j % 2]
                    sigmoid_act(ee.ap, pa.ap[:, :], ee.b, pa.b)
                    P.dve(lambda e, ee=ee, pa=pa, tq=tq: e.tensor_tensor(tq.ap, pa.ap[:, :], ee.ap, ALU.mult), [pa.b, ee.b], [tq.b])
                    P.dve(lambda e, tq=tq, pb=pb, j=j: e.tensor_tensor(uT.ap[:, j, :], pb.ap[:, :], tq.ap, ALU.mult),
                          [pb.b, tq.b], [uT.b])
                for ti in range(4):
                    tl = sg * 4 + ti
                    xt = x1r[ti]
                    ot = outs[ti % 2]
                    for nh in range(2):
                        pz = PSF[4 + nh]
                        for c in range(22):
                            P.pe(lambda e, pz=pz, c=c, ti=ti, nh=nh: e.matmul(
                                pz.ap[:, :], uT.ap[:, c, ti * 128:(ti + 1) * 128], W2.ap[:, c, nh * 512:(nh + 1) * 512],
                                start=(c == 0), stop=(c == 21)), [uT.b, W2.b], [pz.b])
                        P.dve(lambda e, pz=pz, ot=ot, nh=nh: e.tensor_tensor(
                            ot.ap[:, nh * 512:(nh + 1) * 512], pz.ap[:, :], G2[:, nh * 512:(nh + 1) * 512], ALU.mult),
                            [pz.b, bMOD], [ot.b])
                    P.pool(lambda e, ot=ot, xt=xt: e.tensor_tensor(ot.ap, ot.ap, xt.ap, ALU.add), [ot.b, xt.b], [ot.b])
                    P.dma("pool", lambda e, ot=ot, tl=tl: e.dma_start(out=out[tl * 128:(tl + 1) * 128, :], in_=ot.ap), [ot.b], [])
        P.emit(nc, es)
    return nc


def _consts():
    t = np.arange(128)
    c = np.zeros((128, CSTW), np.float32)
    c[:, CI:CI + 128] = np.eye(128)
    c[:, CLF:CLF + 128] = (t[:, None] <= t[None, :])
    c[:, CUF:CUF + 128] = (t[:, None] > t[None, :])
    c[:, CLR:CLR + 128] = (t[:, None] >= t[None, :])
    c[:, CUR:CUR + 128] = (t[:, None] < t[None, :])
    c[:, CNF:CNF + 128] = np.where(t[:, None] <= t[None, :], 0.0, NEG)
    c[:, CNR:CNR + 128] = np.where(t[:, None] >= t[None, :], 0.0, NEG)
    c[:, CON:CON + 128] = 1.0
    return c


def _rope_table(S):
    pos = np.arange(S)
    row = (pos // 64).astype(np.float32)
    col = (pos % 64).astype(np.float32)
    inv = (np.float32(10000.0) ** (-np.arange(16, dtype=np.float32) / np.float32(16))).astype(np.float32)
    ar = row[:, None] * inv[None, :]
    ac = col[:, None] * inv[None, :]
    cos = np.concatenate([np.cos(ar), np.cos(ar), np.cos(ac), np.cos(ac)], axis=1)
    sin = np.concatenate([-np.sin(ar), np.sin(ar), -np.sin(ac), np.sin(ac)], axis=1)
    return np.concatenate([cos, sin], axis=1).astype(np.float32)


def make_in_maps(inp):
    f = lambda a: np.ascontiguousarray(np.asarray(a, dtype=np.float32))
    x, c, ctx, c_ctx = f(inp["x"]), f(inp["c"]), f(inp["ctx"]), f(inp["c_ctx"])
    w_in = f(inp["w_in"][0])
    b_gate = f(inp["b_gate"][0])
    conv_w = f(inp["conv_w"][0])
    conv_b = f(inp["conv_b"][0])
    cst = _consts()
    rt = _rope_table(8192)
    ctx_rt = np.zeros((256, 128), np.float32)
    ctx_rt[:, 0:64] = 1.0
    perm = np.arange(8208)
    perm[3072:3080] = np.arange(3080, 3088)
    perm[3080:3088] = np.arange(3072, 3080)
    bperm = np.concatenate([np.arange(8, 16), np.arange(0, 8)])
    wf = f(inp["w_ffn_in"][0]).reshape(8, 128, 2, 22, 128)
    w_f1r = np.ascontiguousarray(wf.transpose(3, 1, 0, 2, 4).reshape(22, 128, 8, 256))
    maps = []
    for core in range(8):
        b, half = core // 2, core % 2
        if half == 0:
            xl, cl, rl = x[b], ctx[b], rt
            wl, bg, cw = w_in, b_gate, conv_w
        else:
            xl, cl, rl = x[b][::-1], ctx[b][::-1], rt[::-1]
            wl, bg, cw = w_in[:, perm], b_gate[bperm], conv_w[::-1]
        xa = np.ascontiguousarray(np.concatenate([cl, xl], axis=0))
        cvec = np.ascontiguousarray(np.concatenate([c[b].reshape(8, 128).T, c_ctx.reshape(8, 128).T], axis=1))
        small = np.concatenate([f(inp["norm1"][0]), f(inp["norm2"][0]), f(inp["mlstm_norm"][0]), f(inp["diff_norm"][0]),
                                f(inp["q_norm"][0]), f(inp["k_norm"][0]), bg, f(inp["lam_vecs"][0]).reshape(-1)])[None, :]
        convp = np.concatenate([cw.T.reshape(8, 128, 5), conv_b.reshape(8, 128, 1)], axis=2).transpose(1, 0, 2)
        maps.append({
            "xa": xa, "cvec": cvec, "w_mod": f(inp["w_mod"][0]), "b_mod": f(inp["b_mod"][0])[None, :],
            "small": np.ascontiguousarray(small), "cst": cst, "convp": np.ascontiguousarray(convp),
            "rope": np.ascontiguousarray(np.concatenate([ctx_rt, rl], axis=0)),
            "w_in": np.ascontiguousarray(wl), "w_a": f(inp["w_branch_a"][0]), "w_b": f(inp["w_branch_b"][0]),
            "w_o": f(inp["w_out"][0]), "w_f1": w_f1r, "w_f2": f(inp["w_ffn_out"][0]),
        })
    return maps


def kernel(**inputs):
    maps = make_in_maps(inputs)
    nc = build()
    res = run_bass_kernel_spmd(nc, maps, core_ids=list(range(8)))
    outp = np.zeros((4, 8192, D), np.float32)
    for core in range(8):
        b, half = core // 2, core % 2
        o = np.asarray(res.results[core]["out"], dtype=np.float32)
        if half == 0:
            outp[b, 0:OWN] = o
        else:
            outp[b, OWN:] = o[::-1]
    return outp
```

```python
import math
import types
import numpy as np
from contextlib import ExitStack
import concourse.bass as bass
import concourse.mybir as mybir
from concourse.bass_utils import run_bass_kernel_spmd

F32 = mybir.dt.float32
BF16 = mybir.dt.bfloat16
AF = mybir.ActivationFunctionType
ALU = mybir.AluOpType
AX = mybir.AxisListType

NDS = 8


def _freeze(fn):
    if fn.__closure__ is None:
        return fn
    cells = []
    for c in fn.__closure__:
        try:
            cells.append(types.CellType(c.cell_contents))
        except ValueError:
            cells.append(c)
    g = types.FunctionType(fn.__code__, fn.__globals__, fn.__name__, fn.__defaults__, tuple(cells))
    g.__kwdefaults__ = fn.__kwdefaults__
    return g


class Buf:
    __slots__ = ("name", "lw", "rd", "rd_dma")

    def __init__(self, name=""):
        self.name = name
        self.lw = None
        self.rd = {}
        self.rd_dma = []


class Op:
    __slots__ = ("eng", "fn", "reads", "writes", "dma", "deps", "signal", "semkey", "semval", "waits", "idx")

    def __init__(self, eng, fn, reads, writes, dma):
        self.eng = eng
        self.fn = fn
        self.reads = reads
        self.writes = writes
        self.dma = dma
        self.deps = None
        self.signal = False
        self.semkey = None
        self.semval = 0
        self.waits = None


class Prog:
    ENGS = ("pe", "act", "dve", "pool", "sp")

    def __init__(self):
        self.ops = []
        self.G = Buf("G")

    def add(self, eng, fn, reads=(), writes=(), dma=False):
        op = Op(eng, _freeze(fn), tuple(reads) + (self.G,), tuple(writes), dma)
        self.ops.append(op)
        return op

    def pe(self, fn, reads, writes):
        return self.add("pe", fn, reads, writes)

    def act(self, fn, reads, writes):
        return self.add("act", fn, reads, writes)

    def dve(self, fn, reads, writes):
        return self.add("dve", fn, reads, writes)

    def pool(self, fn, reads, writes):
        return self.add("pool", fn, reads, writes)

    def dma(self, q, fn, reads, writes):
        return self.add(q, fn, reads, writes, dma=True)

    def barrier(self, fn):
        op = Op("pool", _freeze(fn), (), (self.G,), False)
        self.ops.append(op)

    def schedule(self):
        for i, op in enumerate(self.ops):
            op.idx = i
            deps = set()
            for b in op.reads:
                if b.lw is not None:
                    d = b.lw
                    if not (d.eng == "pe" and op.eng == "pe" and not d.dma):
                        deps.add(d)
            for b in op.writes:
                if b.lw is not None:
                    d = b.lw
                    if d.dma or op.dma or d.eng != op.eng:
                        deps.add(d)
                for e, d in b.rd.items():
                    if e != op.eng or op.dma:
                        deps.add(d)
                for d in b.rd_dma:
                    deps.add(d)
            for b in op.reads:
                if op.dma:
                    b.rd_dma.append(op)
                else:
                    b.rd[op.eng] = op
            for b in op.writes:
                b.lw = op
                b.rd = {}
                b.rd_dma = []
            deps.discard(op)
            op.deps = deps
            for d in deps:
                d.signal = True
        cnt = {e: 0 for e in self.ENGS}
        dcnt = {e: 0 for e in self.ENGS}
        for op in self.ops:
            if op.dma:
                j = dcnt[op.eng]
                dcnt[op.eng] += 1
                op.semkey = ("d", op.eng, j % NDS)
                op.semval = 16 * (j // NDS + 1)
                op.signal = True
            elif op.signal:
                cnt[op.eng] += 1
                op.semkey = ("c", op.eng)
                op.semval = cnt[op.eng]
        waited = {e: {} for e in self.ENGS}
        for op in self.ops:
            w = {}
            if op.dma and op.semval > 16:
                w[op.semkey] = op.semval - 16
            for d in op.deps:
                if w.get(d.semkey, 0) < d.semval:
                    w[d.semkey] = d.semval
            wt = waited[op.eng]
            out = []
            for k, v in w.items():
                if wt.get(k, 0) < v:
                    wt[k] = v
                    out.append((k, v))
            op.waits = out
        self.final_dma = {}
        for op in self.ops:
            if op.dma:
                self.final_dma[op.semkey] = max(self.final_dma.get(op.semkey, 0), op.semval)

    def emit(self, nc, es):
        self.schedule()
        sems = {}
        for e in self.ENGS:
            sems[("c", e)] = es.enter_context(nc.semaphore("c_" + e))
            for j in range(NDS):
                sems[("d", e, j)] = es.enter_context(nc.semaphore("d_%s_%d" % (e, j)))
        block = es.enter_context(nc.Block())
        by_eng = {e: [op for op in self.ops if op.eng == e] for e in self.ENGS}
        final_dma = self.final_dma

        def run(eng_name, e):
            for op in by_eng[eng_name]:
                for k, v in op.waits:
                    e.wait_ge(sems[k], v)
                ins = op.fn(e)
                if op.signal:
                    ins.then_inc(sems[op.semkey], 16 if op.dma else 1)
            for k, v in final_dma.items():
                if k[1] == eng_name:
                    e.wait_ge(sems[k], v)

        @block.tensor
        def _(e):
            run("pe", e)

        @block.scalar
        def _(e):
            run("act", e)

        @block.vector
        def _(e):
            run("dve", e)

        @block.gpsimd
        def _(e):
            run("pool", e)

        @block.sync
        def _(e):
            run("sp", e)


D = 1024
NT = 66
TOK = 8448
OWN = 4096
PTW = 8456
FFH = 2816
EPS = 1e-6
NEG = -30000.0
O_N1, O_N2, O_MN, O_DN, O_QN, O_KN, O_BG, O_LV, SMW = 0, 1024, 2048, 3072, 3200, 3264, 3328, 3344, 3600
CI, CLF, CUF, CLR, CUR, CNF, CNR, CON, CZE, CSTW = 0, 128, 256, 384, 512, 640, 768, 896, 1024, 1152

UNITS = [
    ("mq", 0, 512, "fm_qk", False), ("mk", 512, 512, "fm_qk", False),
    ("mv0", 1024, 512, "mv", False), ("mv1", 1536, 512, "mv", False),
    ("mo0", 2048, 512, "mo", True), ("mo1", 2560, 512, "mo", True),
    ("mg", 3072, 16, "mg", False),
    ("dq0", 3088, 512, "dq", True), ("dq1", 3600, 512, "dq", True),
    ("dk0", 4112, 512, "dk", False), ("dk1", 4624, 512, "dk", False),
    ("dv0", 5136, 512, "dv", False), ("dv1", 5648, 512, "dv", False),
    ("ga0", 6160, 512, "fm_sig", True), ("ga1", 6672, 512, "fm_sig", True),
    ("gb0", 7184, 512, "fm_sig", True), ("gb1", 7696, 512, "fm_sig", True),
]


def pt_col(tok):
    return 2 + tok if tok < 256 else 262 + (tok - 256)


def build(stop_after=None, dbg=False):
    nc = bass.Bass("TRN2", target_bir_lowering=False)
    P = Prog()
    skind = "ExternalOutput" if dbg else "Internal"

    def din(name, shape, dt=F32):
        return nc.dram_tensor(name, list(shape), dt, kind="ExternalInput").ap()

    def dscr(name, shape, dt):
        return nc.dram_tensor(name, list(shape), dt, kind=skind).ap()

    xa = din("xa", [TOK, D])
    cvec = din("cvec", [128, 16])
    w_mod = din("w_mod", [D, 6 * D])
    b_mod = din("b_mod", [1, 6 * D])
    small = din("small", [1, SMW])
    cst = din("cst", [128, CSTW])
    convp = din("convp", [128, 8, 6])
    rope = din("rope", [TOK, 128])
    w_in = din("w_in", [D, 8208])
    w_a = din("w_a", [D, D])
    w_b = din("w_b", [D, D])
    w_o = din("w_o", [D, D])
    w_f1 = din("w_f1", [22, 128, 8, 256])
    w_f2 = din("w_f2", [FFH, D])
    out = nc.dram_tensor("out", [OWN, D], F32, kind="ExternalOutput").ap()

    PT = dscr("PT", [D, PTW], F32)
    VM = dscr("VM", [TOK, D], BF16)
    GT = dscr("GT", [TOK, 16], F32)
    SMO = dscr("SMO", [OWN, D], F32)
    QTD = dscr("QTD", [8, 128, OWN], BF16)
    KTD = dscr("KTD", [8, 128, TOK], BF16)
    VD = dscr("VD", [TOK, D], BF16)
    SGA = dscr("SGA", [D, OWN], F32)
    SGB = dscr("SGB", [D, OWN], F32)
    QTM = dscr("QTM", [4, 128, OWN], BF16)
    KTM = dscr("KTM", [4, 128, TOK], BF16)
    KM = dscr("KM", [TOK, 512], BF16)
    HF = dscr("HF", [OWN, D], F32)
    HAT = dscr("HAT", [8, 128, OWN], BF16)
    HBT = dscr("HBT", [8, 128, OWN], BF16)
    X1 = dscr("X1", [OWN, D], F32)

    with ExitStack() as es:
        cnt = [0]

        def sbt(shape, dt):
            cnt[0] += 1
            return es.enter_context(nc.sbuf_tensor("t%d" % cnt[0], list(shape), dt))

        CST = sbt([128, CSTW], F32)
        bCST = Buf()
        IDB = sbt([128, 128], BF16)
        bIDB = Buf()
        ONEB = sbt([128, 128], BF16)
        B1p = sbt([128, D], F32)
        G1p = sbt([128, D], F32)
        B2p = sbt([128, D], F32)
        G2p = sbt([128, D], F32)
        CB1p = sbt([128, D], F32)
        bMOD = Buf()
        MN = sbt([128, D], F32)
        BG = sbt([128, 16], F32)
        bSM = Buf()
        A1 = sbt([128, D], F32)
        CA1 = sbt([128, D], F32)
        A2 = sbt([128, D], F32)
        QG = sbt([128, 512], F32)
        KG = sbt([128, 512], F32)
        G128 = sbt([128, 128], F32)
        NLAM = sbt([128, 1], F32)
        CONV = sbt([128, 8, 6], F32)
        NCB = sbt([128, 8], F32)
        bK = Buf()
        JUNK = sbt([128, 8], F32)
        ident = CST[:, CI:CI + 128]
        ones = CST[:, CON:CON + 128]
        B1 = B1p[:]
        G1 = G1p[:]
        B2 = B2p[:]
        G2 = G2p[:]
        CB1 = CB1p[:]

        AR_N = 38400
        ARENA = sbt([128, AR_N], F32)
        offs = [0]

        def reset_arena():
            offs[0] = 0

        def af(n):
            a = offs[0]
            offs[0] += n
            assert offs[0] <= AR_N, ("arena overflow", offs[0])
            return ARENA[:, a:a + n]

        def ab(n):
            w = (n + 1) // 2
            a = offs[0]
            offs[0] += w
            assert offs[0] <= AR_N, ("arena overflow", offs[0])
            return ARENA[:, a:a + w].bitcast(BF16)[:, 0:n]

        class T:
            __slots__ = ("ap", "b")

            def __init__(self, ap):
                self.ap = ap
                self.b = Buf()

        def rot(n, alloc, width, view=None):
            res = []
            for _ in range(n):
                a = alloc(width)
                if view is not None:
                    a = view(a)
                res.append(T(a))
            return res

        PS2 = [es.enter_context(nc.psum_tensor("ps2_%d" % i, [128, 1024], F32)) for i in range(2)]
        PSF = [T(PS2[0][:, 0:512]), T(PS2[0][:, 512:1024]), T(PS2[1][:, 0:512]), T(PS2[1][:, 512:1024])]
        PSF += [T(es.enter_context(nc.psum_tensor("ps%d" % i, [128, 512], F32))[:, :]) for i in range(4, 7)]
        PSB = T(es.enter_context(nc.psum_tensor("psb", [128, 1024], BF16)))

        def barrier():
            P.barrier(lambda e: e.memset(JUNK[:], 0.0))

        reset_arena()
        P.dma("sp", lambda e: e.dma_start(out=CST[:], in_=cst[:, :]), [], [bCST])
        P.dma("sp", lambda e: e.dma_start(out=CONV[:], in_=convp[:, :, :]), [], [bK])
        P.dve(lambda e: e.tensor_copy(IDB[:], ident), [bCST], [bIDB])
        P.dve(lambda e: e.tensor_copy(ONEB[:], ones), [bCST], [bIDB])
        cv = T(af(16))
        ce = T(af(16))
        cs = T(af(16))
        P.dma("sp", lambda e: e.dma_start(out=cv.ap, in_=cvec[:, :]), [], [cv.b])
        P.act(lambda e: e.activation(ce.ap, cv.ap, AF.Exp, scale=-1.0), [cv.b], [ce.b])
        P.dve(lambda e: e.tensor_scalar(ce.ap, ce.ap, 1.0, None, ALU.add), [ce.b], [ce.b])
        P.dve(lambda e: e.reciprocal(ce.ap, ce.ap), [ce.b], [ce.b])
        P.dve(lambda e: e.tensor_tensor(cs.ap, cv.ap, ce.ap, ALU.mult), [cv.b, ce.b], [cs.b])
        SC = T(af(16 * 128).rearrange("p (k m) -> p k m", k=16))
        for k in range(16):
            P.dve(lambda e, k=k: e.tensor_scalar(SC.ap[:, k, :], ones, cs.ap[:, k:k + 1], None, ALU.mult),
                  [cs.b, bCST], [SC.b])
        SC1t = T(af(D))
        SC2t = T(af(D))
        CSC1t = T(af(D))
        SMALLt = T(af(SMW))
        SMALL = SMALLt.ap
        wms = rot(2, af, 8 * 512, lambda a: a.rearrange("p (k n) -> p k n", k=8))
        bms = rot(2, af, 512)
        smr = T(af(SMW))
        P.dma("sp", lambda e: e.dma_start(out=smr.ap[0:1, :], in_=small[:, :]), [], [smr.b])
        for j in range(12):
            wm = wms[j % 2]
            bm = bms[j % 2]
            P.dma("sp", lambda e, wm=wm, j=j: e.dma_start(
                out=wm.ap, in_=w_mod[:, j * 512:(j + 1) * 512].rearrange("(k p) n -> p k n", p=128)), [], [wm.b])
            P.dma("sp", lambda e, bm=bm, j=j: e.dma_start(out=bm.ap[0:1, :], in_=b_mod[:, j * 512:(j + 1) * 512]),
                  [], [bm.b])
            for which in range(2 if j < 4 else 1):
                ps = PSF[(j * 2 + which) % 4]
                for k in range(8):
                    P.pe(lambda e, ps=ps, wm=wm, k=k, which=which: e.matmul(
                        ps.ap[:, :], SC.ap[:, which * 8 + k, :], wm.ap[:, k, :], start=(k == 0), stop=False),
                        [SC.b, wm.b], [ps.b])
                P.pe(lambda e, ps=ps, bm=bm: e.matmul(ps.ap[:, :], ones[0:1, :], bm.ap[0:1, :], start=False, stop=True),
                     [bCST, bm.b], [ps.b])
                hs = slice((j % 2) * 512, (j % 2) * 512 + 512)
                if which == 0:
                    dst = (B1p, SC1t.ap, G1p, B2p, SC2t.ap, G2p)[j // 2][:, hs]
                else:
                    dst = (CB1p, CSC1t.ap)[j // 2][:, hs]
                P.act(lambda e, ps=ps, dst=dst: e.activation(dst, ps.ap[:, :], AF.Copy), [ps.b], [bMOD])
        for j in range(8):
            n = min(512, SMW - j * 512)
            ps = PSF[4 + j % 2]
            P.pe(lambda e, ps=ps, j=j, n=n: e.matmul(ps.ap[:, 0:n], ones[0:1, :], smr.ap[0:1, j * 512:j * 512 + n],
                                                    start=True, stop=True), [bCST, smr.b], [ps.b])
            P.act(lambda e, ps=ps, j=j, n=n: e.activation(SMALL[:, j * 512:j * 512 + n], ps.ap[:, 0:n], AF.Copy),
                  [ps.b], [bSM])
        tmp = T(af(D))
        P.dve(lambda e: e.tensor_copy(MN[:], SMALL[:, O_MN:O_MN + D]), [bSM], [bK])
        P.dve(lambda e: e.tensor_copy(BG[:], SMALL[:, O_BG:O_BG + 16]), [bSM], [bK])
        for (dst, sc, gn) in ((A1, SC1t.ap, SMALL[:, O_N1:O_N1 + D]),
                              (CA1, CSC1t.ap, SMALL[:, O_N1:O_N1 + D]),
                              (A2, SC2t.ap, SMALL[:, O_N2:O_N2 + D])):
            P.dve(lambda e, sc=sc: e.tensor_scalar(tmp.ap, sc, 1.0, None, ALU.add), [bMOD], [tmp.b])
            P.dve(lambda e, dst=dst, gn=gn: e.tensor_tensor(dst[:], tmp.ap, gn, ALU.mult), [tmp.b, bSM], [bK])
        for g in range(8):
            P.dve(lambda e, g=g: e.tensor_scalar(QG[:, g * 64:(g + 1) * 64], SMALL[:, O_QN:O_QN + 64], 0.125, None,
                                                 ALU.mult), [bSM], [bK])
            P.dve(lambda e, g=g: e.tensor_copy(KG[:, g * 64:(g + 1) * 64], SMALL[:, O_KN:O_KN + 64]), [bSM], [bK])
        P.dve(lambda e: e.tensor_scalar(G128[:], SMALL[:, O_DN:O_DN + 128], 0.8, None, ALU.mult), [bSM], [bK])
        P.dve(lambda e: e.tensor_scalar(NCB[:], CONV[:, :, 5], -1.0, None, ALU.mult), [bK], [bK])
        lt = T(af(128))
        ls = T(af(2))
        P.dve(lambda e: e.tensor_tensor(lt.ap[:, 0:64], SMALL[:, O_LV:O_LV + 64], SMALL[:, O_LV + 64:O_LV + 128],
                                        ALU.mult), [bSM], [lt.b])
        P.dve(lambda e: e.tensor_tensor(lt.ap[:, 64:128], SMALL[:, O_LV + 128:O_LV + 192],
                                        SMALL[:, O_LV + 192:O_LV + 256], ALU.mult), [lt.b, bSM], [lt.b])
        P.dve(lambda e: e.tensor_reduce(ls.ap, lt.ap.rearrange("p (a f) -> p a f", a=2), AX.X, ALU.add), [lt.b], [ls.b])
        P.act(lambda e: e.activation(ls.ap, ls.ap, AF.Exp), [ls.b], [ls.b])
        P.dve(lambda e: e.tensor_tensor(NLAM[:], ls.ap[:, 1:2], ls.ap[:, 0:1], ALU.subtract), [ls.b], [bK])
        P.dve(lambda e: e.tensor_scalar(NLAM[:], NLAM[:], -0.2, None, ALU.add), [bK], [bK])
        barrier()

        def rstd_chain(ss, width, nfeat):
            P.dve(lambda e: e.tensor_scalar(ss.ap, ss.ap, 1.0 / nfeat, EPS, ALU.mult, ALU.add), [ss.b], [ss.b])
            P.act(lambda e: e.activation(ss.ap, ss.ap, AF.Ln), [ss.b], [ss.b])
            P.act(lambda e: e.activation(ss.ap, ss.ap, AF.Exp, scale=-0.5), [ss.b], [ss.b])

        def mod_norm_T(xt, A, Bv, consts_b, sq, ss, t1, xn, xnT_dst, xnT_b):
            P.act(lambda e: e.activation(sq.ap, xt.ap, AF.Square, accum_out=ss.ap), [xt.b], [sq.b, ss.b])
            yield
            P.dve(lambda e: e.tensor_scalar(ss.ap, ss.ap, 1.0 / D, EPS, ALU.mult, ALU.add), [ss.b], [ss.b])
            yield
            P.act(lambda e: e.activation(ss.ap, ss.ap, AF.Ln), [ss.b], [ss.b])
            P.act(lambda e: e.activation(ss.ap, ss.ap, AF.Exp, scale=-0.5), [ss.b], [ss.b])
            yield
            P.dve(lambda e: e.scalar_tensor_tensor(t1.ap, xt.ap, ss.ap[:, 0:1], A, ALU.mult, ALU.mult),
                  [xt.b, ss.b] + consts_b, [t1.b])
            yield
            P.pool(lambda e: e.tensor_tensor(xn.ap, t1.ap, Bv, ALU.add), [t1.b] + consts_b, [xn.b])
            yield
            for c in range(8):
                P.pe(lambda e, c=c: e.transpose(PSB.ap[:, c * 128:(c + 1) * 128], xn.ap[:, c * 128:(c + 1) * 128], IDB[:]),
                     [xn.b, bIDB], [PSB.b])
            P.act(lambda e: e.activation(xnT_dst, PSB.ap.rearrange("p (c t) -> p c t", c=8), AF.Copy),
                  [PSB.b], [xnT_b])

        def interleave(gens):
            gens = list(gens)
            while gens:
                nxt_ = []
                for g_ in gens:
                    try:
                        next(g_)
                        nxt_.append(g_)
                    except StopIteration:
                        pass
                gens = nxt_

        def sigmoid_act(dst, src, dst_b, src_b, scale=1.0, bias=None, bias_b=()):
            if bias is None:
                P.act(lambda e: e.activation(dst, src, AF.Exp, scale=-scale), [src_b], [dst_b])
            else:
                P.act(lambda e: e.activation(dst, src, AF.Exp, bias=bias, scale=-scale), [src_b] + list(bias_b), [dst_b])
            P.act(lambda e: e.activation(dst, dst, AF.Ln, bias=1.0), [dst_b], [dst_b])
            P.act(lambda e: e.activation(dst, dst, AF.Exp, scale=-1.0), [dst_b], [dst_b])

        def load_w_cast(src_ap, dst_ap, dst_b):
            P.dma("pool", lambda e: e.dma_start(out=dst_ap, in_=src_ap), [], [dst_b])

        if True:
            reset_arena()
            v8 = lambda a: a.rearrange("p (k n) -> p k n", k=8)
            xnT = T(ab(8 * 2048).rearrange("p (k t) -> p k t", k=8))
            wbfs = rot(2, ab, 8 * 512, v8)
            xts = rot(4, af, D)
            sqs = rot(3, af, D)
            t1s = rot(3, af, D)
            sss = rot(3, af, 1)
            xns = rot(3, ab, D)
            w512 = rot(12, af, 512)
            s8 = rot(6, af, 8)
            ropes = rot(6, af, 128)
            ob512 = rot(6, ab, 512)
            obT = rot(4, ab, 512, lambda a: a.rearrange("p (h t) -> p h t", h=4))
            fst = rot(2, af, 4 * 512, lambda a: a.rearrange("p (c t) -> p c t", c=4))
            g16 = rot(6, af, 16)
            zt = T(af(8))
            ctr = {"w": 0, "x": 0, "w512": 0, "ob": 0, "obT": 0, "fst": 0, "ps": 0, "s8": 0, "rope": 0, "g16": 0}

            def nxt(lst, key):
                i = ctr[key]
                ctr[key] += 1
                return lst[i % len(lst)]

            P.pool(lambda e: e.memset(zt.ap, 0.0), [], [zt.b])
            for (c0, w) in ((0, 2), (258, 4), (8454, 2)):
                for cc in range(8):
                    P.dma("pool", lambda e, c0=c0, w=w, cc=cc: e.dma_start(
                        out=PT[cc * 128:(cc + 1) * 128, c0:c0 + w], in_=zt.ap[:, 0:w]), [zt.b], [])

            blocks = [(0, 2, "ctx"), (2, 16, "own"), (18, 16, "own"), (34, 16, "oth"), (50, 16, "oth")]
            items = []
            for bi, (bt0, bnt, bkind) in enumerate(blocks):
                first = True
                for u in UNITS:
                    if u[4] and bkind != "own":
                        continue
                    items.append((bi, u, first))
                    first = False

            def issue_w(ii):
                (_, (un_, c0_, nc_, _k, _o), _f) = items[ii]
                wb_ = wbfs[ii % 2]
                load_w_cast(w_in[:, c0_:c0_ + nc_].rearrange("(k p) n -> p k n", p=128), wb_.ap[:, :, 0:nc_], wb_.b)

            issue_w(0)
            for ii, (bi, (uname, c0, ncols, kind, own_only), first) in enumerate(items):
                (bt0, bnt, bkind) = blocks[bi]
                own = bkind == "own"
                if first:
                    A, Bv = (CA1[:], CB1) if bkind == "ctx" else (A1[:], B1)
                    for t0_ in range(0, bnt, 3):
                        gens = []
                        for ti in range(t0_, min(bnt, t0_ + 3)):
                            tt = bt0 + ti
                            xt = nxt(xts, "x")
                            P.dma("sp", lambda e, xt=xt, tt=tt: e.dma_start(out=xt.ap, in_=xa[tt * 128:(tt + 1) * 128, :]),
                                  [], [xt.b])
                            gens.append(mod_norm_T(xt, A, Bv, [bK, bMOD], sqs[ti % 3], sss[ti % 3], t1s[ti % 3], xns[ti % 3],
                                                   xnT.ap[:, :, ti * 128:(ti + 1) * 128], xnT.b))
                        interleave(gens)
                wbf = wbfs[ii % 2]
                if ii + 1 < len(items):
                    issue_w(ii + 1)
                if True:
                    if kind in ("fm_qk", "fm_sig"):
                        ngroups = max(1, bnt // 4)
                        gn = min(512, bnt * 128)
                        for g in range(ngroups):
                            st = nxt(fst, "fst")
                            for cc in range(4):
                                ps = PSF[ctr["ps"] % 6]
                                ctr["ps"] += 1
                                for k in range(8):
                                    P.pe(lambda e, ps=ps, wbf=wbf, k=k, cc=cc, g=g, gn=gn: e.matmul(
                                        ps.ap[:, 0:gn], wbf.ap[:, k, cc * 128:(cc + 1) * 128],
                                        xnT.ap[:, k, g * 512:g * 512 + gn], start=(k == 0), stop=(k == 7)),
                                        [wbf.b, xnT.b], [ps.b])
                                if kind == "fm_qk":
                                    P.act(lambda e, ps=ps, st=st, cc=cc, gn=gn: e.activation(
                                        st.ap[:, cc, 0:gn], ps.ap[:, 0:gn], AF.Copy), [ps.b], [st.b])
                                else:
                                    sigmoid_act(st.ap[:, cc, 0:gn], ps.ap[:, 0:gn], st.b, ps.b)
                            tok0 = bt0 * 128 + g * 512
                            if kind == "fm_qk":
                                pc = pt_col(tok0)
                                P.dma("pool", lambda e, st=st, c0=c0, pc=pc, gn=gn: e.dma_start(
                                    out=PT[c0:c0 + 512, pc:pc + gn].rearrange("(c p) t -> p c t", p=128),
                                    in_=st.ap[:, :, 0:gn]), [st.b], [])
                            else:
                                dstT = SGA if uname.startswith("ga") else SGB
                                r0 = 512 * int(uname[2])
                                oc = tok0 - 256
                                P.dma("pool", lambda e, st=st, dstT=dstT, r0=r0, oc=oc: e.dma_start(
                                    out=dstT[r0:r0 + 512, oc:oc + 512].rearrange("(c p) t -> p c t", p=128),
                                    in_=st.ap), [st.b], [])
                        continue
                    def post_tile(ps, tt, ti):
                        tok0 = tt * 128
                        half = int(uname[2]) if kind != "mg" else 0
                        if kind in ("mv", "dv"):
                            ob = nxt(ob512, "ob")
                            P.act(lambda e: e.activation(ob.ap, ps.ap[:, :], AF.Copy), [ps.b], [ob.b])
                            yield
                            dst = VM if kind == "mv" else VD
                            P.dma("pool", lambda e: e.dma_start(
                                out=dst[tok0:tok0 + 128, half * 512:(half + 1) * 512], in_=ob.ap), [ob.b], [])
                        elif kind == "mo":
                            w = nxt(w512, "w512")
                            sigmoid_act(w.ap, ps.ap[:, :], w.b, ps.b)
                            yield
                            oc = tok0 - 256
                            P.dma("pool", lambda e: e.dma_start(
                                out=SMO[oc:oc + 128, half * 512:(half + 1) * 512], in_=w.ap), [w.b], [])
                        elif kind == "mg":
                            gg = nxt(g16, "g16")
                            P.dve(lambda e: e.tensor_tensor(gg.ap, ps.ap[:, 0:16], BG[:], ALU.add), [ps.b, bK], [gg.b])
                            yield
                            P.dma("pool", lambda e: e.dma_start(out=GT[tok0:tok0 + 128, :], in_=gg.ap), [gg.b], [])
                        elif kind in ("dq", "dk"):
                            gains = QG if kind == "dq" else KG
                            sq = nxt(w512, "w512")
                            qn = nxt(w512, "w512")
                            t2 = nxt(w512, "w512")
                            t3 = nxt(w512, "w512")
                            st8 = nxt(s8, "s8")
                            rp = nxt(ropes, "rope")
                            P.dma("sp", lambda e: e.dma_start(out=rp.ap, in_=rope[tok0:tok0 + 128, :]), [], [rp.b])
                            P.act(lambda e: e.activation(sq.ap, ps.ap[:, :], AF.Square), [ps.b], [sq.b])
                            yield
                            P.dve(lambda e: e.tensor_reduce(st8.ap, sq.ap.rearrange("p (g f) -> p g f", g=8), AX.X, ALU.add),
                                  [sq.b], [st8.b])
                            P.dve(lambda e: e.tensor_scalar(st8.ap, st8.ap, 1.0 / 64, EPS, ALU.mult, ALU.add), [st8.b], [st8.b])
                            yield
                            P.act(lambda e: e.activation(st8.ap, st8.ap, AF.Ln), [st8.b], [st8.b])
                            P.act(lambda e: e.activation(st8.ap, st8.ap, AF.Exp, scale=-0.5), [st8.b], [st8.b])
                            yield
                            P.dve(lambda e: e.tensor_tensor(
                                qn.ap.rearrange("p (g f) -> p g f", g=8), ps.ap[:, :].rearrange("p (g f) -> p g f", g=8),
                                st8.ap.unsqueeze(2).to_broadcast([128, 8, 64]), ALU.mult), [ps.b, st8.b], [qn.b])
                            yield
                            P.pool(lambda e: e.tensor_tensor(qn.ap, qn.ap, gains[:], ALU.mult), [qn.b, bK], [qn.b])
                            yield
                            qv = qn.ap.rearrange("p (g a s f) -> p g a s f", g=8, a=2, s=2)
                            tv = t2.ap.rearrange("p (g a s f) -> p g a s f", g=8, a=2, s=2)
                            sv = rp.ap[:, 64:128].rearrange("p (a s f) -> p a s f", a=2, s=2)
                            for s_ in range(2):
                                P.pool(lambda e, s_=s_: e.tensor_tensor(
                                    tv[:, :, :, s_, :], qv[:, :, :, 1 - s_, :],
                                    sv[:, :, s_, :].unsqueeze(1).to_broadcast([128, 8, 2, 16]), ALU.mult),
                                    [qn.b, rp.b], [t2.b])
                            P.dve(lambda e: e.tensor_tensor(
                                t3.ap.rearrange("p (g f) -> p g f", g=8), qn.ap.rearrange("p (g f) -> p g f", g=8),
                                rp.ap[:, 0:64].unsqueeze(1).to_broadcast([128, 8, 64]), ALU.mult), [qn.b, rp.b], [t3.b])
                            yield
                            ob = nxt(ob512, "ob")
                            P.dve(lambda e: e.tensor_tensor(ob.ap, t3.ap, t2.ap, ALU.add), [t3.b, t2.b], [ob.b])
                            yield
                            for hh in range(4):
                                P.pe(lambda e, hh=hh: e.transpose(PSB.ap[:, hh * 128:(hh + 1) * 128],
                                                                  ob.ap[:, hh * 128:(hh + 1) * 128], IDB[:]),
                                     [ob.b, bIDB], [PSB.b])
                            oT = nxt(obT, "obT")
                            P.act(lambda e: e.activation(oT.ap, PSB.ap[:, 0:512].rearrange("p (h t) -> p h t", h=4), AF.Copy),
                                  [PSB.b], [oT.b])
                            yield
                            if kind == "dq":
                                oc = tok0 - 256
                                P.dma("pool", lambda e: e.dma_start(
                                    out=QTD[half * 4:(half + 1) * 4, :, oc:oc + 128].rearrange("h p t -> p h t"),
                                    in_=oT.ap), [oT.b], [])
                            else:
                                P.dma("pool", lambda e: e.dma_start(
                                    out=KTD[half * 4:(half + 1) * 4, :, tok0:tok0 + 128].rearrange("h p t -> p h t"),
                                    in_=oT.ap), [oT.b], [])

                    GI = 3
                    for t0_ in range(0, bnt, GI):
                        gens = []
                        for ti in range(t0_, min(bnt, t0_ + GI)):
                            ps = PSF[ctr["ps"] % 6]
                            ctr["ps"] += 1
                            for k in range(8):
                                P.pe(lambda e, ps=ps, k=k, ti=ti: e.matmul(
                                    ps.ap[:, 0:ncols], xnT.ap[:, k, ti * 128:(ti + 1) * 128], wbf.ap[:, k, 0:ncols],
                                    start=(k == 0), stop=(k == 7)), [wbf.b, xnT.b], [ps.b])
                            gens.append(post_tile(ps, bt0 + ti, ti))
                        interleave(gens)
            barrier()
        if stop_after == 1:
            P.emit(nc, es)
            return nc

        if True:
            reset_arena()
            wins = rot(2, af, 4 * 516, lambda a: a.rearrange("p (c t) -> p c t", c=4))
            accs = rot(4, af, 512)
            es_ = rot(4, af, 512)
            okT = rot(4, ab, 512)
            LNS = T(af(1))
            P.pool(lambda e: e.memset(LNS.ap, math.log(128.0 ** -0.5)), [], [LNS.b])
            kms = rot(2, ab, 2048, lambda a: a.rearrange("p (t c) -> p t c", t=4))
            groups = [(0, 256)] + [(256 + g * 512, 512) for g in range(16)]
            ci = 0
            for gi, (tok0, gn) in enumerate(groups):
                own = 256 <= tok0 < 256 + OWN
                pc = pt_col(tok0)
                km = kms[gi % 2]
                for qk in ((0, 1) if own else (1,)):
                    win = wins[ci % 2]
                    ci += 1
                    P.dma("sp", lambda e, win=win, qk=qk, pc=pc, gn=gn: e.dma_start(
                        out=win.ap[:, :, 0:gn + 4],
                        in_=PT[qk * 512:(qk + 1) * 512, pc - 2:pc + gn + 2].rearrange("(c p) t -> p c t", p=128)),
                        [], [win.b])
                    def conv_chain(hc):
                        cc = qk * 4 + hc
                        acc = accs[hc]
                        ee = es_[hc]
                        ok = okT[hc]
                        P.dve(lambda e: e.tensor_scalar(acc.ap[:, 0:gn], win.ap[:, hc, 0:gn], CONV[:, cc, 0:1], None, ALU.mult),
                              [win.b, bK], [acc.b])
                        yield
                        for k in range(1, 5):
                            P.dve(lambda e, k=k: e.scalar_tensor_tensor(
                                acc.ap[:, 0:gn], win.ap[:, hc, k:k + gn], CONV[:, cc, k:k + 1], acc.ap[:, 0:gn],
                                ALU.mult, ALU.add), [win.b, bK, acc.b], [acc.b])
                            yield
                        P.act(lambda e: e.activation(ee.ap[:, 0:gn], acc.ap[:, 0:gn], AF.Exp, bias=NCB[:, cc:cc + 1], scale=-1.0),
                              [acc.b, bK], [ee.b])
                        P.act(lambda e: e.activation(ee.ap[:, 0:gn], ee.ap[:, 0:gn], AF.Ln, bias=1.0), [ee.b], [ee.b])
                        if qk == 0:
                            P.act(lambda e: e.activation(ee.ap[:, 0:gn], ee.ap[:, 0:gn], AF.Exp, bias=LNS.ap[:, 0:1], scale=-1.0),
                                  [ee.b, LNS.b], [ee.b])
                        else:
                            P.act(lambda e: e.activation(ee.ap[:, 0:gn], ee.ap[:, 0:gn], AF.Exp, scale=-1.0), [ee.b], [ee.b])
                        yield
                        P.dve(lambda e: e.scalar_tensor_tensor(
                            ok.ap[:, 0:gn], acc.ap[:, 0:gn], CONV[:, cc, 5:6], ee.ap[:, 0:gn], ALU.add, ALU.mult),
                            [acc.b, ee.b, bK], [ok.b])
                        yield
                        if qk == 0:
                            oc = tok0 - 256
                            P.dma("pool", lambda e: e.dma_start(out=QTM[hc, :, oc:oc + 512], in_=ok.ap), [ok.b], [])
                        else:
                            P.dma("pool", lambda e: e.dma_start(out=KTM[hc, :, tok0:tok0 + gn], in_=ok.ap[:, 0:gn]), [ok.b], [])
                            nt = gn // 128
                            for t in range(nt):
                                P.pe(lambda e, t=t: e.transpose(
                                    PSB.ap[:, t * 128:(t + 1) * 128], ok.ap[:, t * 128:(t + 1) * 128], IDB[:]),
                                    [ok.b, bIDB], [PSB.b])
                            P.act(lambda e: e.activation(
                                km.ap[:, 0:nt, hc * 128:(hc + 1) * 128],
                                PSB.ap[:, 0:nt * 128].rearrange("p (t c) -> p t c", t=nt), AF.Copy), [PSB.b], [km.b])

                    interleave([conv_chain(hc) for hc in range(4)])
                nt = gn // 128
                P.dma("pool", lambda e, km=km, tok0=tok0, nt=nt: e.dma_start(
                    out=KM[tok0:tok0 + nt * 128, :].rearrange("(t p) c -> p t c", p=128), in_=km.ap[:, 0:nt, :]),
                    [km.b], [])
            barrier()
        if stop_after == 2:
            P.emit(nc, es)
            return nc

        LFm = CST[:, CLF:CLF + 128]
        UFm = CST[:, CUF:CUF + 128]
        LRm = CST[:, CLR:CLR + 128]
        URm = CST[:, CUR:CUR + 128]
        NFm = CST[:, CNF:CNF + 128]
        NRm = CST[:, CNR:CNR + 128]

        def mlstm_dir(direction):
            reset_arena()
            isF = direction == "F"
            gofs = 0 if isF else 8
            Lm, Um, Nm = (LFm, UFm, NFm) if isF else (LRm, URm, NRm)
            Cst = T(af(4 * 257).rearrange("p (h e) -> p h e", h=4))
            Cbf = rot(2, ab, 4 * 257 + 4, lambda a: a[:, 0:4 * 257].rearrange("p (h e) -> p h e", h=4))
            gts = rot(2, af, 16)
            lfs = rot(2, af, 4)
            rhsE = rot(2, af, 512, lambda a: a.rearrange("p (h j) -> p h j", h=4))
            DTs = rot(2, af, 512, lambda a: a.rearrange("p (h j) -> p h j", h=4))
            AT = rot(4, ab, 128)
            numB = rot(4, af, 257)
            smalls = rot(2, af, 16)
            qTs = rot(2, ab, 512, lambda a: a.rearrange("p (h t) -> p h t", h=4))
            kTs = rot(2, ab, 512, lambda a: a.rearrange("p (h t) -> p h t", h=4))
            kms_ = rot(2, ab, 512)
            kws = rot(4, ab, 128)
            vxs = rot(2, ab, 4 * 258, lambda a: a[:, 0:4 * 257].rearrange("p (h e) -> p h e", h=4))
            numA = rot(4, af, 257)
            hts = rot(2, af, D)
            hfs = rot(2, af, D)
            smo = rot(2, af, D)
            dens = rot(4, af, 4)
            sq2 = rot(1, af, D)
            st4 = rot(2, af, 4)
            hab = rot(2, ab, D)
            haT = rot(2, ab, D, lambda a: a.rearrange("p (c t) -> p c t", c=8))
            for vx in vxs:
                P.pool(lambda e, vx=vx: e.memset(vx.ap[:, :, 256:257], 1.0), [], [vx.b])
            P.pool(lambda e: e.memset(Cst.ap, 0.0), [], [Cst.b])
            P.pool(lambda e: e.memset(Cbf[0].ap, 0.0), [], [Cbf[0].b])
            if isF:
                order = [(t, False) for t in (0, 1)] + [(t, True) for t in range(2, 34)]
            else:
                order = [(t, False) for t in (1, 0)] + [(t, False) for t in range(65, 33, -1)] + \
                        [(t, True) for t in range(33, 1, -1)]
            cx = {}

            def prologue(step):
                tt, outp = order[step]
                tok0 = tt * 128
                oc = tok0 - 256
                gt = gts[step % 2]
                lf = lfs[step % 2]
                sm = smalls[step % 2]
                kmt = kms_[step % 2]
                vx = vxs[step % 2]
                d = dict(tok0=tok0, oc=oc, gt=gt, lf=lf, sm=sm, kmt=kmt, vx=vx)
                P.dma("sp", lambda e: e.dma_start(out=gt.ap, in_=GT[tok0:tok0 + 128, :]), [], [gt.b])
                P.dma("sp", lambda e: e.dma_start(out=kmt.ap, in_=KM[tok0:tok0 + 128, :]), [], [kmt.b])
                P.dma("sp", lambda e: e.dma_start(
                    out=vx.ap[:, :, 0:256], in_=VM[tok0:tok0 + 128, :].rearrange("p (h e) -> p h e", h=4)), [], [vx.b])
                if outp:
                    qT = qTs[step % 2]
                    kT = kTs[step % 2]
                    d.update(qT=qT, kT=kT)
                    P.dma("sp", lambda e: e.dma_start(
                        out=qT.ap, in_=QTM[:, :, oc:oc + 128].rearrange("h p t -> p h t")), [], [qT.b])
                    P.dma("sp", lambda e: e.dma_start(
                        out=kT.ap, in_=KTM[:, :, tok0:tok0 + 128].rearrange("h p t -> p h t")), [], [kT.b])
                    d["ht"] = hts[step % 2]
                    if not isF:
                        hf = hfs[step % 2]
                        so = smo[step % 2]
                        d.update(hf=hf, so=so)
                        P.dma("sp", lambda e: e.dma_start(out=hf.ap, in_=HF[oc:oc + 128, :]), [], [hf.b])
                        P.dma("sp", lambda e: e.dma_start(out=so.ap, in_=SMO[oc:oc + 128, :]), [], [so.b])
                yield
                P.act(lambda e: e.activation(lf.ap, gt.ap[:, gofs + 4:gofs + 8], AF.Exp, scale=-1.0), [gt.b], [lf.b])
                P.act(lambda e: e.activation(lf.ap, lf.ap, AF.Ln, bias=1.0), [lf.b], [lf.b])
                yield
                P.dve(lambda e: e.tensor_scalar(lf.ap, lf.ap, -1.0, None, ALU.mult), [lf.b], [lf.b])
                yield
                pss = PSF[6]
                P.pe(lambda e: e.matmul(pss.ap[:, 0:4], Lm, lf.ap, start=True, stop=True), [bCST, lf.b], [pss.b])
                P.pe(lambda e: e.matmul(pss.ap[:, 4:8], Um, lf.ap, start=True, stop=True), [bCST, lf.b], [pss.b])
                P.pe(lambda e: e.matmul(pss.ap[:, 8:12], ones, lf.ap, start=True, stop=True), [bCST, lf.b], [pss.b])
                if outp:
                    rE = rhsE[step % 2]
                    DT = DTs[step % 2]
                    d["DT"] = DT
                    for h in range(4):
                        P.act(lambda e, h=h: e.mul(rE.ap[:, h, :], Lm, lf.ap[:, h:h + 1]), [bCST, lf.b], [rE.b])
                yield
                P.dve(lambda e: e.tensor_copy(sm.ap[:, 0:12], pss.ap[:, 0:12]), [pss.b], [sm.b])
                P.dve(lambda e: e.tensor_tensor(sm.ap[:, 4:8], sm.ap[:, 4:8], gt.ap[:, gofs:gofs + 4], ALU.add),
                      [sm.b, gt.b], [sm.b])
                if outp:
                    pe_ = PSF[4]
                    P.pe(lambda e: e.matmul(pe_.ap[:, :], Um, rE.ap.rearrange("p h j -> p (h j)"), start=True, stop=False),
                         [bCST, rE.b], [pe_.b])
                    for h in range(4):
                        P.pe(lambda e, h=h: e.matmul(pe_.ap[:, h * 128:(h + 1) * 128], ident, Nm, start=False, stop=(h == 3)),
                             [bCST], [pe_.b])
                yield
                P.act(lambda e: e.activation(sm.ap[:, 0:12], sm.ap[:, 0:12], AF.Exp), [sm.b], [sm.b])
                if outp:
                    for h in range(4):
                        P.act(lambda e, h=h: e.activation(
                            DT.ap[:, h, :], pe_.ap[:, h * 128:(h + 1) * 128], AF.Exp, bias=gt.ap[:, gofs + h:gofs + h + 1]),
                            [pe_.b, gt.b], [DT.b])
                cx[step] = d

            for _ in prologue(0):
                pass
            for step, (tt, outp) in enumerate(order):
                d = cx[step]
                tok0, oc, gt, lf, sm, kmt, vx = d["tok0"], d["oc"], d["gt"], d["lf"], d["sm"], d["kmt"], d["vx"]
                qT, kT, DT, ht = d.get("qT"), d.get("kT"), d.get("DT"), d.get("ht")
                hf, so = d.get("hf"), d.get("so")
                cb_in = Cbf[step % 2]
                cb_out = Cbf[(step + 1) % 2]
                def head_chain(h):
                    X = PSF[(h % 2) * 2]
                    Y = PSF[(h % 2) * 2 + 1]
                    at = AT[h]
                    na = numA[h]
                    nb = numB[h]
                    dn = dens[h]
                    kw = kws[h]
                    if outp:
                        P.pe(lambda e: e.matmul(Y.ap[:, 0:257], qT.ap[:, h, :], cb_in.ap[:, h, :], start=True, stop=True),
                             [qT.b, cb_in.b], [Y.b])
                        P.pe(lambda e: e.matmul(X.ap[:, 260:388], kT.ap[:, h, :], qT.ap[:, h, :], start=True, stop=True),
                             [kT.b, qT.b], [X.b])
                        yield
                        P.act(lambda e: e.activation(nb.ap, Y.ap[:, 0:257], AF.Copy), [Y.b], [nb.b])
                        P.dve(lambda e: e.tensor_tensor(at.ap, X.ap[:, 260:388], DT.ap[:, h, :], ALU.mult), [X.b, DT.b], [at.b])
                        yield
                        P.pe(lambda e: e.matmul(X.ap[:, 0:257], at.ap, vx.ap[:, h, :], start=True, stop=True),
                             [at.b, vx.b], [X.b])
                        yield
                        P.dve(lambda e: e.scalar_tensor_tensor(
                            na.ap, nb.ap, sm.ap[:, h:h + 1], X.ap[:, 0:257], ALU.mult, ALU.add), [X.b, sm.b, nb.b], [na.b])
                        yield
                        P.dve(lambda e: e.tensor_scalar(dn.ap[:, 0:1], na.ap[:, 256:257], -1.0, None, ALU.mult), [na.b], [dn.b])
                        P.dve(lambda e: e.tensor_tensor(dn.ap[:, 1:2], dn.ap[:, 0:1], na.ap[:, 256:257], ALU.max),
                              [na.b, dn.b], [dn.b])
                        yield
                        P.dve(lambda e: e.tensor_scalar(dn.ap[:, 2:3], dn.ap[:, 1:2], 1.0, None, ALU.max), [dn.b], [dn.b])
                        P.dve(lambda e: e.reciprocal(dn.ap[:, 3:4], dn.ap[:, 2:3]), [dn.b], [dn.b])
                        yield
                        if isF:
                            P.dve(lambda e: e.tensor_scalar(
                                ht.ap[:, h * 256:(h + 1) * 256], na.ap[:, 0:256], dn.ap[:, 3:4], None, ALU.mult),
                                [na.b, dn.b], [ht.b])
                        else:
                            P.dve(lambda e: e.scalar_tensor_tensor(
                                ht.ap[:, h * 256:(h + 1) * 256], na.ap[:, 0:256], dn.ap[:, 3:4],
                                hf.ap[:, h * 256:(h + 1) * 256], ALU.mult, ALU.add), [na.b, dn.b, hf.b], [ht.b])
                    P.act(lambda e: e.mul(kw.ap, kmt.ap[:, h * 128:(h + 1) * 128], sm.ap[:, 4 + h:5 + h]),
                          [kmt.b, sm.b], [kw.b])
                    yield
                    P.pe(lambda e: e.matmul(Y.ap[:, 0:257], kw.ap, vx.ap[:, h, :], start=True, stop=True),
                         [kw.b, vx.b], [Y.b])
                    yield
                    P.dve(lambda e: e.scalar_tensor_tensor(
                        Cst.ap[:, h, :], Cst.ap[:, h, :], sm.ap[:, 8 + h:9 + h], Y.ap[:, 0:257], ALU.mult, ALU.add),
                        [Cst.b, sm.b, Y.b], [Cst.b])

                pro = prologue(step + 1) if step + 1 < len(order) else iter(())
                interleave([head_chain(0), head_chain(1), pro])
                interleave([head_chain(2), head_chain(3)])
                P.act(lambda e, cb_out=cb_out: e.activation(cb_out.ap, Cst.ap, AF.Copy), [Cst.b], [cb_out.b])
                if outp and isF:
                    P.dma("pool", lambda e, ht=ht, oc=oc: e.dma_start(out=HF[oc:oc + 128, :], in_=ht.ap), [ht.b], [])
                if outp and not isF:
                    s4 = st4[step % 2]
                    P.act(lambda e, ht=ht: e.activation(sq2[0].ap, ht.ap, AF.Square), [ht.b], [sq2[0].b])
                    P.dve(lambda e, s4=s4: e.tensor_reduce(s4.ap, sq2[0].ap.rearrange("p (h f) -> p h f", h=4), AX.X, ALU.add),
                          [sq2[0].b], [s4.b])
                    rstd_chain(s4, 4, 256)
                    for h in range(4):
                        P.dve(lambda e, ht=ht, s4=s4, h=h: e.scalar_tensor_tensor(
                            ht.ap[:, h * 256:(h + 1) * 256], ht.ap[:, h * 256:(h + 1) * 256], s4.ap[:, h:h + 1],
                            MN[:, h * 256:(h + 1) * 256], ALU.mult, ALU.mult), [ht.b, s4.b, bK], [ht.b])
                    hb = hab[step % 2]
                    P.pool(lambda e, hb=hb, ht=ht, so=so: e.tensor_tensor(hb.ap, ht.ap, so.ap, ALU.mult), [ht.b, so.b], [hb.b])
                    for c in range(8):
                        P.pe(lambda e, hb=hb, c=c: e.transpose(PSB.ap[:, c * 128:(c + 1) * 128], hb.ap[:, c * 128:(c + 1) * 128],
                                                               IDB[:]), [hb.b, bIDB], [PSB.b])
                    hT = haT[step % 2]
                    P.act(lambda e, hT=hT: e.activation(hT.ap, PSB.ap.rearrange("p (c t) -> p c t", c=8), AF.Copy),
                          [PSB.b], [hT.b])
                    P.dma("pool", lambda e, hT=hT, oc=oc: e.dma_start(
                        out=HAT[:, :, oc:oc + 128].rearrange("c p t -> p c t"), in_=hT.ap), [hT.b], [])
            barrier()

        mlstm_dir("F")
        if stop_after == 3:
            P.emit(nc, es)
            return nc
        mlstm_dir("R")
        if stop_after == 4:
            P.emit(nc, es)
            return nc

        if True:
            reset_arena()
            NKC = 66
            KTs = rot(2, ab, TOK)
            Vs = rot(2, ab, NKC * 130, lambda a: a[:, 0:NKC * 129].rearrange("p (c e) -> p c e", c=NKC))
            QTs = rot(2, ab, OWN)
            PTs = rot(4, ab, 1024)
            obufs = rot(2, af, OWN, lambda a: a.rearrange("p (t e) -> p t e", t=32))
            sst = rot(2, af, 32)
            rr = rot(3, af, 4)
            tA = rot(2, af, 128)
            sqj = rot(1, af, 128)
            hbb = rot(2, ab, 128)
            hbT = rot(2, ab, 512)
            accS = rot(2, af, 8 * 129, lambda a: a.rearrange("p (i e) -> p i e", i=8))
            for v in Vs:
                P.pool(lambda e, v=v: e.memset(v.ap[:, :, 128:129], 1.0), [], [v.b])
            accs = []
            for i in range(8):
                bk = PSF[4 + i // 3]
                accs.append((bk, (i % 3) * 129))
            SB2 = [(PS2[0], (PSF[0].b, PSF[1].b)), (PS2[1], (PSF[2].b, PSF[3].b))]
            heads = []
            for h in range(8):
                Kt = KTs[h % 2]
                Vt = Vs[h % 2]
                Qt = QTs[h % 2]
                heads.append((Kt, Vt, Qt, obufs[h % 2], sst[h % 2]))
            steps = [(h, qb, kc) for h in range(8) for qb in range(8) for kc in range(NKC)]

            def emit_loads(h):
                Kt, Vt, Qt, _, _ = heads[h]
                P.dma("sp", lambda e: e.dma_start(out=Kt.ap, in_=KTD[h, :, :]), [], [Kt.b])
                P.dma("sp", lambda e: e.dma_start(out=Qt.ap, in_=QTD[h, :, :]), [], [Qt.b])
                P.dma("sp", lambda e: e.dma_start(
                    out=Vt.ap[:, :, 0:128], in_=VD[:, h * 128:(h + 1) * 128].rearrange("(c p) e -> p c e", p=128)),
                    [], [Vt.b])

            def emit_qk(i):
                h, qb, kc = steps[i]
                Kt, Vt, Qt, _, _ = heads[h]
                ps2, bb = SB2[i % 2]
                P.pe(lambda e: e.matmul(ps2[:, 0:512], Kt.ap[0:64, kc * 128:(kc + 1) * 128],
                                        Qt.ap[0:64, qb * 512:(qb + 1) * 512], start=True, stop=True),
                     [Kt.b, Qt.b], [bb[0]])
                P.pe(lambda e: e.matmul(ps2[:, 512:1024], Kt.ap[64:128, kc * 128:(kc + 1) * 128],
                                        Qt.ap[64:128, qb * 512:(qb + 1) * 512], start=True, stop=True),
                     [Kt.b, Qt.b], [bb[1]])

            def emit_exp(i):
                h, qb, kc = steps[i]
                ps2, bb = SB2[i % 2]
                pt = PTs[i % 4]
                P.act(lambda e: e.activation(pt.ap, ps2[:, :], AF.Exp), [bb[0], bb[1]], [pt.b])

            def emit_pv(i):
                h, qb, kc = steps[i]
                Kt, Vt, Qt, ob, ssq = heads[h]
                pt = PTs[i % 4]
                for m in range(2):
                    for qs in range(4):
                        bk, o0 = accs[m * 4 + qs]
                        P.pe(lambda e, bk=bk, o0=o0, m=m, qs=qs: e.matmul(
                            bk.ap[:, o0:o0 + 129], pt.ap[:, m * 512 + qs * 128:m * 512 + (qs + 1) * 128],
                            Vt.ap[:, kc, :], start=(kc == 0), stop=(kc == NKC - 1)), [pt.b, Vt.b], [bk.b])
                if kc != NKC - 1:
                    return
                aS = accS[(h * 8 + qb) % 2]
                for bi_ in range(3):
                    n_ = 3 if bi_ < 2 else 2
                    bk = PSF[4 + bi_]
                    P.dve(lambda e, bk=bk, bi_=bi_, n_=n_: e.tensor_copy(
                        aS.ap[:, bi_ * 3:bi_ * 3 + n_, :], bk.ap[:, 0:n_ * 129].rearrange("p (i e) -> p i e", i=n_)),
                        [bk.b], [aS.b])
                for qs in range(4):
                    qt = qb * 4 + qs
                    r = rr[qt % 3]
                    ta = tA[qt % 2]
                    P.dve(lambda e, r=r, qs=qs: e.reciprocal(r.ap[:, 0:1], aS.ap[:, qs, 128:129]), [aS.b], [r.b])
                    P.dve(lambda e, r=r, qs=qs: e.reciprocal(r.ap[:, 1:2], aS.ap[:, 4 + qs, 128:129]), [aS.b, r.b], [r.b])
                    P.dve(lambda e, r=r: e.tensor_tensor(r.ap[:, 2:3], r.ap[:, 1:2], NLAM[:], ALU.mult), [r.b, bK], [r.b])
                    P.pool(lambda e, r=r, ta=ta, qs=qs: e.tensor_scalar(ta.ap, aS.ap[:, qs, 0:128], r.ap[:, 0:1], None,
                                                                        ALU.mult), [aS.b, r.b], [ta.b])
                    P.dve(lambda e, r=r, ta=ta, qs=qs, qt=qt: e.scalar_tensor_tensor(
                        ob.ap[:, qt, :], aS.ap[:, 4 + qs, 0:128], r.ap[:, 2:3], ta.ap, ALU.mult, ALU.add),
                        [aS.b, r.b, ta.b], [ob.b])
                    P.pool(lambda e, qt=qt: e.tensor_tensor(sqj[0].ap, ob.ap[:, qt, :], ob.ap[:, qt, :], ALU.mult),
                           [ob.b], [sqj[0].b])
                    P.dve(lambda e, qt=qt: e.tensor_reduce(ssq.ap[:, qt:qt + 1], sqj[0].ap, AX.X, ALU.add),
                          [sqj[0].b], [ssq.b])
                if qb != 7:
                    return
                rstd_chain(ssq, 32, 128)
                for g4 in range(8):
                    hT = hbT[g4 % 2]
                    for j in range(4):
                        qt = g4 * 4 + j
                        hb = hbb[qt % 2]
                        P.dve(lambda e, hb=hb, qt=qt: e.scalar_tensor_tensor(
                            hb.ap, ob.ap[:, qt, :], ssq.ap[:, qt:qt + 1], G128[:], ALU.mult, ALU.mult),
                            [ob.b, ssq.b, bK], [hb.b])
                        P.pe(lambda e, hb=hb, j=j: e.transpose(PSB.ap[:, j * 128:(j + 1) * 128], hb.ap, IDB[:]),
                             [hb.b, bIDB], [PSB.b])
                    P.dve(lambda e, hT=hT: e.tensor_copy(hT.ap, PSB.ap[:, 0:512]), [PSB.b], [hT.b])
                    P.dma("pool", lambda e, hT=hT, g4=g4: e.dma_start(out=HBT[h, :, g4 * 512:(g4 + 1) * 512], in_=hT.ap),
                          [hT.b], [])

            emit_loads(0)
            emit_qk(0)
            NS = len(steps)
            for i in range(NS):
                emit_exp(i)
                if i + 1 < NS:
                    emit_qk(i + 1)
                if i >= 1:
                    emit_pv(i - 1)
                    h_, qb_, kc_ = steps[i - 1]
                    if qb_ == 0 and kc_ == 0 and h_ + 1 < 8:
                        emit_loads(h_ + 1)
            emit_pv(NS - 1)
            barrier()
        if stop_after == 5:
            P.emit(nc, es)
            return nc

        if True:
            reset_arena()
            v8 = lambda a: a.rearrange("p (k n) -> p k n", k=8)
            WA = T(v8(ab(8 * D)))
            WB = T(v8(ab(8 * D)))
            WO = T(v8(ab(8 * D)))
            for (W, src) in ((WA, w_a), (WB, w_b), (WO, w_o)):
                for hf_ in range(2):
                    load_w_cast(src[:, hf_ * 512:(hf_ + 1) * 512].rearrange("(k p) n -> p k n", p=128),
                                W.ap[:, :, hf_ * 512:(hf_ + 1) * 512], W.b)
            hAs = rot(1, ab, 8 * 512, v8)
            hBs = rot(1, ab, 8 * 512, v8)
            sgs = rot(4, af, 512)
            yTs = rot(1, ab, 8 * 512, v8)
            tms = rot(3, af, 512)
            xts = rot(2, af, D)
            x1s = rot(2, af, D)
            for g in range(8):
                hA = hAs[0]
                hB = hBs[0]
                yT = yTs[0]
                P.dma("sp", lambda e, hA=hA, g=g: e.dma_start(
                    out=hA.ap, in_=HAT[:, :, g * 512:(g + 1) * 512].rearrange("c p t -> p c t")), [], [hA.b])
                P.dma("sp", lambda e, hB=hB, g=g: e.dma_start(
                    out=hB.ap, in_=HBT[:, :, g * 512:(g + 1) * 512].rearrange("c p t -> p c t")), [], [hB.b])
                for fc in range(8):
                    sa = sgs[(fc * 2) % 4]
                    sb_ = sgs[(fc * 2 + 1) % 4]
                    P.dma("sp", lambda e, sa=sa, fc=fc, g=g: e.dma_start(
                        out=sa.ap, in_=SGA[fc * 128:(fc + 1) * 128, g * 512:(g + 1) * 512]), [], [sa.b])
                    P.dma("sp", lambda e, sb_=sb_, fc=fc, g=g: e.dma_start(
                        out=sb_.ap, in_=SGB[fc * 128:(fc + 1) * 128, g * 512:(g + 1) * 512]), [], [sb_.b])
                    pa = PSF[(fc % 2) * 2]
                    pb = PSF[(fc % 2) * 2 + 1]
                    for k in range(8):
                        P.pe(lambda e, pa=pa, hA=hA, k=k, fc=fc: e.matmul(
                            pa.ap[:, :], WA.ap[:, k, fc * 128:(fc + 1) * 128], hA.ap[:, k, :], start=(k == 0), stop=(k == 7)),
                            [WA.b, hA.b], [pa.b])
                    for k in range(8):
                        P.pe(lambda e, pb=pb, hB=hB, k=k, fc=fc: e.matmul(
                            pb.ap[:, :], WB.ap[:, k, fc * 128:(fc + 1) * 128], hB.ap[:, k, :], start=(k == 0), stop=(k == 7)),
                            [WB.b, hB.b], [pb.b])
                    tm = tms[fc % 3]
                    P.dve(lambda e, tm=tm, pa=pa, sa=sa: e.tensor_tensor(tm.ap, pa.ap[:, :], sa.ap, ALU.mult), [pa.b, sa.b], [tm.b])
                    P.dve(lambda e, sb_=sb_, pb=pb: e.tensor_tensor(sb_.ap, pb.ap[:, :], sb_.ap, ALU.mult), [pb.b, sb_.b], [sb_.b])
                    P.pool(lambda e, yT=yT, tm=tm, sb_=sb_, fc=fc: e.tensor_tensor(yT.ap[:, fc, :], tm.ap, sb_.ap, ALU.add),
                           [tm.b, sb_.b], [yT.b])
                for j in range(4):
                    tl = g * 4 + j
                    xt = xts[tl % 2]
                    x1 = x1s[tl % 2]
                    P.dma("sp", lambda e, xt=xt, tl=tl: e.dma_start(out=xt.ap, in_=xa[256 + tl * 128:256 + (tl + 1) * 128, :]),
                          [], [xt.b])
                    for nh in range(2):
                        pz = PSF[4 + nh]
                        for k in range(8):
                            P.pe(lambda e, pz=pz, yT=yT, k=k, j=j, nh=nh: e.matmul(
                                pz.ap[:, :], yT.ap[:, k, j * 128:(j + 1) * 128], WO.ap[:, k, nh * 512:(nh + 1) * 512],
                                start=(k == 0), stop=(k == 7)), [yT.b, WO.b], [pz.b])
                        P.dve(lambda e, pz=pz, x1=x1, nh=nh: e.tensor_tensor(
                            x1.ap[:, nh * 512:(nh + 1) * 512], pz.ap[:, :], G1[:, nh * 512:(nh + 1) * 512], ALU.mult),
                            [pz.b, bMOD], [x1.b])
                    P.pool(lambda e, x1=x1, xt=xt: e.tensor_tensor(x1.ap, x1.ap, xt.ap, ALU.add), [x1.b, xt.b], [x1.b])
                    P.dma("pool", lambda e, x1=x1, tl=tl: e.dma_start(out=X1[tl * 128:(tl + 1) * 128, :], in_=x1.ap), [x1.b], [])
            barrier()
        if stop_after == 6:
            P.emit(nc, es)
            return nc

        if True:
            reset_arena()
            W2 = T(ab(22 * D).rearrange("p (c n) -> p c n", c=22))
            for j in range(11):
                load_w_cast(w_f2[j * 256:(j + 1) * 256, :].rearrange("(c p) n -> p c n", p=128),
                            W2.ap[:, 2 * j:2 * j + 2, :], W2.b)
            xnT = T(ab(8 * 512).rearrange("p (k t) -> p k t", k=8))
            uT = T(ab(22 * 512).rearrange("p (c t) -> p c t", c=22))
            x1r = rot(4, af, D)
            sqs = rot(2, af, D)
            t1s = rot(2, af, D)
            sss = rot(4, af, 1)
            xns = rot(2, ab, D)
            wbf = rot(3, ab, 8 * 256, lambda a: a.rearrange("p (k n) -> p k n", k=8))
            ees = rot(3, af, 512)
            tts = rot(2, af, 512)
            outs = rot(2, af, D)
            for sg in range(8):
                gens = []
                for ti in range(4):
                    tl = sg * 4 + ti
                    xt = x1r[ti]
                    P.dma("sp", lambda e, xt=xt, tl=tl: e.dma_start(out=xt.ap, in_=X1[tl * 128:(tl + 1) * 128, :]), [], [xt.b])
                    gens.append(mod_norm_T(xt, A2[:], B2, [bK, bMOD], sqs[ti % 2], sss[ti], t1s[ti % 2], xns[ti % 2],
                                           xnT.ap[:, :, ti * 128:(ti + 1) * 128], xnT.b))
                    if len(gens) == 2:
                        interleave(gens)
                        gens = []
                if sg == 0:
                    load_w_cast(w_f1[0], wbf[0].ap, wbf[0].b)
                for j in range(22):
                    jj = sg * 22 + j
                    wb_ = wbf[jj % 3]
                    if jj + 1 < 8 * 22:
                        nb = wbf[(jj + 1) % 3]
                        load_w_cast(w_f1[(j + 1) % 22], nb.ap, nb.b)
                    pa = PSF[(j % 2) * 2]
                    pb = PSF[(j % 2) * 2 + 1]
                    for k in range(8):
                        P.pe(lambda e, pa=pa, wb_=wb_, k=k: e.matmul(
                            pa.ap[:, :], wb_.ap[:, k, 0:128], xnT.ap[:, k, :], start=(k == 0), stop=(k == 7)),
                            [wb_.b, xnT.b], [pa.b])
                    for k in range(8):
                        P.pe(lambda e, pb=pb, wb_=wb_, k=k: e.matmul(
                            pb.ap[:, :], wb_.ap[:, k, 128:256], xnT.ap[:, k, :], start=(k == 0), stop=(k == 7)),
                            [wb_.b, xnT.b], [pb.b])
                    ee = ees[j % 3]
                    tq = tts[j % 2]
                    sigmoid_act(ee.ap, pa.ap[:, :], ee.b, pa.b)
                    P.dve(lambda e, ee=ee, pa=pa, tq=tq: e.tensor_tensor(tq.ap, pa.ap[:, :], ee.ap, ALU.mult), [pa.b, ee.b], [tq.b])
                    P.dve(lambda e, tq=tq, pb=pb, j=j: e.tensor_tensor(uT.ap[:, j, :], pb.ap[:, :], tq.ap, ALU.mult),
                          [pb.b, tq.b], [uT.b])
                for ti in range(4):
                    tl = sg * 4 + ti
                    xt = x1r[ti]
                    ot = outs[ti % 2]
                    for nh in range(2):
                        pz = PSF[4 + nh]
                        for c in range(22):
                            P.pe(lambda e, pz=pz, c=c, ti=ti, nh=nh: e.matmul(
                                pz.ap[:, :], uT.ap[:, c, ti * 128:(ti + 1) * 128], W2.ap[:, c, nh * 512:(nh + 1) * 512],
                                start=(c == 0), stop=(c == 21)), [uT.b, W2.b], [pz.b])
                        P.dve(lambda e, pz=pz, ot=ot, nh=nh: e.tensor_tensor(
                            ot.ap[:, nh * 512:(nh + 1) * 512], pz.ap[:, :], G2[:, nh * 512:(nh + 1) * 512], ALU.mult),
                            [pz.b, bMOD], [ot.b])
                    P.pool(lambda e, ot=ot, xt=xt: e.tensor_tensor(ot.ap, ot.ap, xt.ap, ALU.add), [ot.b, xt.b], [ot.b])
                    P.dma("pool", lambda e, ot=ot, tl=tl: e.dma_start(out=out[tl * 128:(tl + 1) * 128, :], in_=ot.ap), [ot.b], [])
        P.emit(nc, es)
    return nc


def _consts():
    t = np.arange(128)
    c = np.zeros((128, CSTW), np.float32)
    c[:, CI:CI + 128] = np.eye(128)
    c[:, CLF:CLF + 128] = (t[:, None] <= t[None, :])
    c[:, CUF:CUF + 128] = (t[:, None] > t[None, :])
    c[:, CLR:CLR + 128] = (t[:, None] >= t[None, :])
    c[:, CUR:CUR + 128] = (t[:, None] < t[None, :])
    c[:, CNF:CNF + 128] = np.where(t[:, None] <= t[None, :], 0.0, NEG)
    c[:, CNR:CNR + 128] = np.where(t[:, None] >= t[None, :], 0.0, NEG)
    c[:, CON:CON + 128] = 1.0
    return c


def _rope_table(S):
    pos = np.arange(S)
    row = (pos // 64).astype(np.float32)
    col = (pos % 64).astype(np.float32)
    inv = (np.float32(10000.0) ** (-np.arange(16, dtype=np.float32) / np.float32(16))).astype(np.float32)
    ar = row[:, None] * inv[None, :]
    ac = col[:, None] * inv[None, :]
    cos = np.concatenate([np.cos(ar), np.cos(ar), np.cos(ac), np.cos(ac)], axis=1)
    sin = np.concatenate([-np.sin(ar), np.sin(ar), -np.sin(ac), np.sin(ac)], axis=1)
    return np.concatenate([cos, sin], axis=1).astype(np.float32)


def make_in_maps(inp):
    f = lambda a: np.ascontiguousarray(np.asarray(a, dtype=np.float32))
    x, c, ctx, c_ctx = f(inp["x"]), f(inp["c"]), f(inp["ctx"]), f(inp["c_ctx"])
    w_in = f(inp["w_in"][0])
    b_gate = f(inp["b_gate"][0])
    conv_w = f(inp["conv_w"][0])
    conv_b = f(inp["conv_b"][0])
    cst = _consts()
    rt = _rope_table(8192)
    ctx_rt = np.zeros((256, 128), np.float32)
    ctx_rt[:, 0:64] = 1.0
    perm = np.arange(8208)
    perm[3072:3080] = np.arange(3080, 3088)
    perm[3080:3088] = np.arange(3072, 3080)
    bperm = np.concatenate([np.arange(8, 16), np.arange(0, 8)])
    wf = f(inp["w_ffn_in"][0]).reshape(8, 128, 2, 22, 128)
    w_f1r = np.ascontiguousarray(wf.transpose(3, 1, 0, 2, 4).reshape(22, 128, 8, 256))
    maps = []
    for core in range(8):
        b, half = core // 2, core % 2
        if half == 0:
            xl, cl, rl = x[b], ctx[b], rt
            wl, bg, cw = w_in, b_gate, conv_w
        else:
            xl, cl, rl = x[b][::-1], ctx[b][::-1], rt[::-1]
            wl, bg, cw = w_in[:, perm], b_gate[bperm], conv_w[::-1]
        xa = np.ascontiguousarray(np.concatenate([cl, xl], axis=0))
        cvec = np.ascontiguousarray(np.concatenate([c[b].reshape(8, 128).T, c_ctx.reshape(8, 128).T], axis=1))
        small = np.concatenate([f(inp["norm1"][0]), f(inp["norm2"][0]), f(inp["mlstm_norm"][0]), f(inp["diff_norm"][0]),
                                f(inp["q_norm"][0]), f(inp["k_norm"][0]), bg, f(inp["lam_vecs"][0]).reshape(-1)])[None, :]
        convp = np.concatenate([cw.T.reshape(8, 128, 5), conv_b.reshape(8, 128, 1)], axis=2).transpose(1, 0, 2)
        maps.append({
            "xa": xa, "cvec": cvec, "w_mod": f(inp["w_mod"][0]), "b_mod": f(inp["b_mod"][0])[None, :],
            "small": np.ascontiguousarray(small), "cst": cst, "convp": np.ascontiguousarray(convp),
            "rope": np.ascontiguousarray(np.concatenate([ctx_rt, rl], axis=0)),
            "w_in": np.ascontiguousarray(wl), "w_a": f(inp["w_branch_a"][0]), "w_b": f(inp["w_branch_b"][0]),
            "w_o": f(inp["w_out"][0]), "w_f1": w_f1r, "w_f2": f(inp["w_ffn_out"][0]),
        })
    return maps


def kernel(**inputs):
    maps = make_in_maps(inputs)
    nc = build()
    res = run_bass_kernel_spmd(nc, maps, core_ids=list(range(8)))
    outp = np.zeros((4, 8192, D), np.float32)
    for core in range(8):
        b, half = core // 2, core % 2
        o = np.asarray(res.results[core]["out"], dtype=np.float32)
        if half == 0:
            outp[b, 0:OWN] = o
        else:
            outp[b, OWN:] = o[::-1]
    return outp
```
